# Optimizing a Trainium2 kernel written in Bass

```python
import math
import jax, jax.numpy as jnp
from jax import lax
import numpy as np

D_MODEL = 1024
BATCH = 2
SEQ = 8192
DEPTH = 2

N_MIXERS = 2
N_SB_LAYERS = (DEPTH + 1) // 2
N_MLA_LAYERS = DEPTH // 2
Q_BLOCK = 128
EPS = 1e-6

SB_HEADS = 16
SB_HEAD_DIM = D_MODEL // SB_HEADS

MLA_HEADS = 16
MLA_Q_LORA = 256
MLA_KV_LORA = 128
MLA_NOPE_DIM = 64
MLA_ROPE_DIM = 32
MLA_V_DIM = 64
ROPE_THETA = 10000.0

D_FF = 2816
FFN_RESIDUAL_WEIGHT = 0.5
N_NORMS_PER_LAYER = 6

kernel_name = "hybrid_stickbreaking_mla_macaron"


def rms_norm(x, g):
    xf = x.astype(jnp.float32)
    y = xf * lax.rsqrt(jnp.mean(xf * xf, axis=-1, keepdims=True) + EPS)
    return (y * g.astype(jnp.float32)).astype(x.dtype)


def swiglu(x, w_in, w_out):
    gate, up = jnp.split(x @ w_in, 2, axis=-1)
    return (jax.nn.silu(gate) * up) @ w_out


def apply_rope(x, positions):
    d = x.shape[-1]
    inv_freq = ROPE_THETA ** (-jnp.arange(0, d, 2, dtype=jnp.float32) / d)
    ang = positions.astype(jnp.float32)[..., None] * inv_freq
    cos = jnp.cos(ang)[:, :, None, :].astype(x.dtype)
    sin = jnp.sin(ang)[:, :, None, :].astype(x.dtype)
    x1, x2 = jnp.split(x, 2, axis=-1)
    return jnp.concatenate([x1 * cos - x2 * sin, x2 * cos + x1 * sin], axis=-1)


def query_blocks(q):
    b, s, h, d = q.shape
    return q.reshape(b, s // Q_BLOCK, Q_BLOCK, h, d).transpose(1, 0, 2, 3, 4)


def merge_blocks(o):
    nb, b, qb, h, dv = o.shape
    return o.transpose(1, 0, 2, 3, 4).reshape(b, nb * qb, h * dv)


def stick_breaking_attention(h, w_in, w_out):
    b, s, _ = h.shape
    qkv = (h @ w_in).reshape(b, s, 3, SB_HEADS, SB_HEAD_DIM)
    q, k, v = qkv[:, :, 0], qkv[:, :, 1], qkv[:, :, 2]
    scale = 1.0 / math.sqrt(SB_HEAD_DIM)
    k_idx = jnp.arange(s)

    def block(args):
        qb, bi = args
        q_idx = bi * Q_BLOCK + jnp.arange(Q_BLOCK)
        z = jnp.einsum('bqhd,bkhd->bhqk', qb, k).astype(jnp.float32) * scale
        mask = k_idx[None, :] < q_idx[:, None]
        sp = jnp.where(mask, jax.nn.softplus(z), 0.0)
        tail = lax.cumsum(sp, axis=sp.ndim - 1, reverse=True) - sp
        log_a = jax.nn.log_sigmoid(z) - tail
        a = jnp.where(mask, jnp.exp(log_a), 0.0).astype(v.dtype)
        return jnp.einsum('bhqk,bkhd->bqhd', a, v)

    nb = s // Q_BLOCK
    o = lax.map(block, (query_blocks(q), jnp.arange(nb)))
    return merge_blocks(o) @ w_out


def multi_head_latent_attention(h, positions, w_in, q_norm, w_uq, kv_norm, w_ukv, w_out):
    b, s, _ = h.shape
    proj = h @ w_in
    c_q = proj[..., :MLA_Q_LORA]
    c_kv = proj[..., MLA_Q_LORA:MLA_Q_LORA + MLA_KV_LORA]
    k_rope = proj[..., MLA_Q_LORA + MLA_KV_LORA:][:, :, None, :]
    k_rope = apply_rope(k_rope, positions)[:, :, 0, :]

    q = (rms_norm(c_q, q_norm) @ w_uq).reshape(b, s, MLA_HEADS, MLA_NOPE_DIM + MLA_ROPE_DIM)
    q_nope = q[..., :MLA_NOPE_DIM]
    q_rope = apply_rope(q[..., MLA_NOPE_DIM:], positions)
    q = jnp.concatenate([q_nope, q_rope], axis=-1)

    kv = (rms_norm(c_kv, kv_norm) @ w_ukv).reshape(b, s, MLA_HEADS, MLA_NOPE_DIM + MLA_V_DIM)
    k_nope = kv[..., :MLA_NOPE_DIM]
    v = kv[..., MLA_NOPE_DIM:]
    scale = 1.0 / math.sqrt(MLA_NOPE_DIM + MLA_ROPE_DIM)
    k_idx = jnp.arange(s)

    def block(args):
        qb, bi = args
        q_idx = bi * Q_BLOCK + jnp.arange(Q_BLOCK)
        qn, qr = qb[..., :MLA_NOPE_DIM], qb[..., MLA_NOPE_DIM:]
        scores = (jnp.einsum('bqhd,bkhd->bhqk', qn, k_nope)
                  + jnp.einsum('bqhr,bkr->bhqk', qr, k_rope)).astype(jnp.float32) * scale
        mask = k_idx[None, :] <= q_idx[:, None]
        scores = jnp.where(mask, scores, -jnp.inf)
        p = jax.nn.softmax(scores, axis=-1).astype(v.dtype)
        return jnp.einsum('bhqk,bkhd->bqhd', p, v)

    nb = s // Q_BLOCK
    o = lax.map(block, (query_blocks(q), jnp.arange(nb)))
    return merge_blocks(o) @ w_out


def setup_inputs(seed: int = 0) -> dict:
    key = jax.random.key(seed)
    ks = jax.random.split(key, 16)

    def w(k, shape, fan_in):
        return jax.random.normal(k, shape, jnp.float32) * fan_in ** -0.5

    def gain(k, shape):
        return 1.0 + 0.05 * jax.random.normal(k, shape, jnp.float32)

    sb_width = SB_HEADS * SB_HEAD_DIM
    mla_in = MLA_Q_LORA + MLA_KV_LORA + MLA_ROPE_DIM
    return {
        "x": jax.random.normal(ks[0], (BATCH, SEQ, D_MODEL), jnp.float32),
        "positions": jnp.broadcast_to(jnp.arange(SEQ, dtype=jnp.int32)[None, :], (BATCH, SEQ)),
        "norm_g": gain(ks[1], (DEPTH, N_NORMS_PER_LAYER, D_MODEL)),
        "ffn_w_in": w(ks[2], (DEPTH, 2, D_MODEL, 2 * D_FF), D_MODEL),
        "ffn_w_out": w(ks[3], (DEPTH, 2, D_FF, D_MODEL), D_FF),
        "sb_w_in": w(ks[4], (N_SB_LAYERS, D_MODEL, 3 * sb_width), D_MODEL),
        "sb_w_out": w(ks[5], (N_SB_LAYERS, sb_width, D_MODEL), sb_width),
        "mla_w_in": w(ks[6], (N_MLA_LAYERS, D_MODEL, mla_in), D_MODEL),
        "mla_q_norm": gain(ks[7], (N_MLA_LAYERS, MLA_Q_LORA)),
        "mla_w_uq": w(ks[8], (N_MLA_LAYERS, MLA_Q_LORA, MLA_HEADS * (MLA_NOPE_DIM + MLA_ROPE_DIM)), MLA_Q_LORA),
        "mla_kv_norm": gain(ks[9], (N_MLA_LAYERS, MLA_KV_LORA)),
        "mla_w_ukv": w(ks[10], (N_MLA_LAYERS, MLA_KV_LORA, MLA_HEADS * (MLA_NOPE_DIM + MLA_V_DIM)), MLA_KV_LORA),
        "mla_w_out": w(ks[11], (N_MLA_LAYERS, MLA_HEADS * MLA_V_DIM, D_MODEL), MLA_HEADS * MLA_V_DIM),
    }


def reference(x, positions, norm_g, ffn_w_in, ffn_w_out, sb_w_in, sb_w_out,
              mla_w_in, mla_q_norm, mla_w_uq, mla_kv_norm, mla_w_ukv, mla_w_out):
    for i in range(DEPTH):
        g = norm_g[i]
        f = swiglu(rms_norm(x, g[0]), ffn_w_in[i, 0], ffn_w_out[i, 0])
        x = x + FFN_RESIDUAL_WEIGHT * rms_norm(f, g[1])
        h = rms_norm(x, g[2])
        j = i // N_MIXERS
        if i % N_MIXERS == 0:
            m = stick_breaking_attention(h, sb_w_in[j], sb_w_out[j])
        else:
            m = multi_head_latent_attention(h, positions, mla_w_in[j], mla_q_norm[j], mla_w_uq[j],
                                            mla_kv_norm[j], mla_w_ukv[j], mla_w_out[j])
        x = x + rms_norm(m, g[3])
        f = swiglu(rms_norm(x, g[4]), ffn_w_in[i, 1], ffn_w_out[i, 1])
        x = x + FFN_RESIDUAL_WEIGHT * rms_norm(f, g[5])
    return x
```

```python
import numpy as np
import ml_dtypes
from contextlib import ExitStack
import concourse.bass as bass
import concourse.mybir as mybir
from concourse.bass_utils import run_bass_kernel_spmd

F32 = mybir.dt.float32
BF16 = mybir.dt.bfloat16
I32 = mybir.dt.int32
AF = mybir.ActivationFunctionType
ALU = mybir.AluOpType
AX = mybir.AxisListType

D = 1024
DFF = 2816
NT = 16
TOK = 2048
EPS = 1e-6
NEG = -30000.0
U8 = mybir.dt.uint8
SB_BYTES = 204 * 1024
DT_SIZE = {F32: 4, BF16: 2, I32: 4, U8: 1}


class Plan:
    ENGS = ("tensor", "vector", "scalar", "gpsimd", "sync")

    def __init__(self, nc, es):
        self.nc = nc
        self.es = es
        self.ops = {e: [] for e in self.ENGS}
        self.sem = {e: es.enter_context(nc.semaphore("s_" + e)) for e in self.ENGS}
        self.cnt = {e: 0 for e in self.ENGS}
        self.seen = {e: {} for e in self.ENGS}
        self.dsems = []
        self.dfree = []
        self.dlive = []
        self.nsb = 0
        self.pending = {e: [] for e in self.ENGS}
        self._init_mem()

    def _init_mem(self):
        self.arena = self.es.enter_context(self.nc.sbuf_tensor("arena", [128, SB_BYTES], U8))
        self.psum = self.es.enter_context(self.nc.psum_tensor("psum", [128, 4096], F32))
        self.sb_off = 0
        self.ps_off = 0

    def sb(self, shape, dtype, name=None):
        esz = DT_SIZE[dtype]
        n = 1
        for d in shape[1:]:
            n *= d
        nb = (n * esz + 63) // 64 * 64
        assert self.sb_off + nb <= SB_BYTES, f"SBUF arena overflow {self.sb_off}+{nb}"
        v = self.arena[0:shape[0], self.sb_off:self.sb_off + n * esz]
        self.sb_off += nb
        if dtype != U8:
            v = v.bitcast(dtype)
        if len(shape) == 3:
            v = v.rearrange("p (a b) -> p a b", a=shape[1])
        elif len(shape) == 4:
            v = v.rearrange("p (a b c) -> p a b c", a=shape[1], b=shape[2])
        return v

    def ps(self, shape, dtype=F32, name=None):
        esz = DT_SIZE[dtype]
        nb = shape[1] * esz
        nbank = (nb + 2047) // 2048
        assert self.ps_off + nbank <= 8, "PSUM overflow"
        c0 = self.ps_off * 512
        self.ps_off += nbank
        v = self.psum[0:shape[0], c0:c0 + nb // 4]
        if dtype != F32:
            v = v.bitcast(dtype)
        return v

    def mark(self):
        return (self.sb_off, self.ps_off, len(self.dlive))

    def release(self, m):
        self.sb_off, self.ps_off, nd = m
        self.dfree.extend(self.dlive[nd:])
        del self.dlive[nd:]

    def barrier(self, extra=(), exclude_cc=False):
        toks = [(self.sem[e], self.cnt[e]) for e in self.ENGS if self.cnt[e] > 0]
        toks += [(d["sem"], d["val"]) for d in self.dsems if d["val"] > 0 and not (exclude_cc and d.get("cc"))]
        toks += list(extra)
        for e in self.ENGS:
            self.pending[e] = self.pending[e] + self._waits(e, toks)

    def dsem(self, name):
        if self.dfree:
            st = self.dfree.pop()
        else:
            s = self.es.enter_context(self.nc.semaphore(f"d{len(self.dsems)}_{name}"))
            st = {"sem": s, "val": 0}
            self.dsems.append(st)
        self.dlive.append(st)
        return st

    def _waits(self, eng, deps):
        w = []
        best = {}
        for d in deps:
            if d is None:
                continue
            if isinstance(d, (list, tuple)) and d and isinstance(d[0], (list, tuple)):
                for dd in d:
                    if dd is not None:
                        k = id(dd[0])
                        if k not in best or best[k][1] < dd[1]:
                            best[k] = dd
                continue
            k = id(d[0])
            if k not in best or best[k][1] < d[1]:
                best[k] = d
        for k, (s, v) in best.items():
            if self.seen[eng].get(k, 0) < v:
                self.seen[eng][k] = v
                w.append((s, v))
        return w

    def op(self, eng, fn, deps=(), sig=True):
        w = self.pending[eng] + self._waits(eng, deps)
        self.pending[eng] = []
        tok = None
        if sig:
            self.cnt[eng] += 1
            tok = (self.sem[eng], self.cnt[eng])
        self.ops[eng].append((w, fn, self.sem[eng] if sig else None, 1))
        return tok

    def dma(self, eng, ds, out, in_, deps=(), **kw):
        w = self.pending[eng] + self._waits(eng, deps)
        self.pending[eng] = []
        ds["val"] += 16
        tok = (ds["sem"], ds["val"])
        self.ops[eng].append((w, lambda e: e.dma_start(out=out, in_=in_, **kw), ds["sem"], 16))
        return tok

    def collective(self, src, dst, groups, deps=()):
        st = {"sem": self.es.enter_context(self.nc.semaphore(f"cc{len(self.dsems)}")), "val": 0, "cc": True}
        self.dsems.append(st)
        w = self.pending["gpsimd"] + self._waits("gpsimd", list(deps))
        self.pending["gpsimd"] = []
        st["val"] += 1
        tok = (st["sem"], st["val"])
        self.ops["gpsimd"].append((w, lambda e: e.collective_compute(
            "AllGather", ALU.bypass, replica_groups=groups, ins=[src.opt()], outs=[dst.opt()]), st["sem"], 1))
        return tok

    def emit(self, final_waits):
        nc = self.nc
        with nc.Block() as block:
            def mk(eng):
                def body(e):
                    for (w, fn, s, inc) in self.ops[eng]:
                        for (ws, wv) in w:
                            e.wait_ge(ws, wv)
                        ins = fn(e)
                        if s is not None:
                            ins.then_inc(s, inc)
                    if eng == "sync":
                        for (ws, wv) in final_waits:
                            e.wait_ge(ws, wv)
                return body
            block.tensor(mk("tensor"))
            block.vector(mk("vector"))
            block.scalar(mk("scalar"))
            block.gpsimd(mk("gpsimd"))
            block.sync(mk("sync"))


class Job:
    pass


class Ring:
    def __init__(self, bufs):
        self.bufs = bufs
        self.users = [[] for _ in bufs]
        self.i = 0

    def get(self):
        k = self.i % len(self.bufs)
        self.i += 1
        deps = self.users[k]
        self.users[k] = []
        self.cur = k
        return self.bufs[k], deps

    def used_by(self, tok, k=None):
        if tok is not None:
            self.users[self.cur if k is None else k].append(tok)


class Consts:
    pass


def load_consts(P, dr):
    C = Consts()
    ds = P.dsem("d_const")
    toks = []
    C.ident = P.sb([128, 128], BF16)
    toks.append(P.dma("sync", ds, C.ident[:], dr["c_ident"][:, :]))
    C.identf = P.sb([128, 128], F32)
    toks.append(P.dma("sync", ds, C.identf[:], dr["c_identf"][:, :]))
    C.negU = P.sb([128, 128], BF16)
    toks.append(P.dma("sync", ds, C.negU[:], dr["c_negU"][:, :]))
    C.mask_sb = P.sb([128, 4, 128], BF16)
    toks.append(P.dma("sync", ds, C.mask_sb[:], dr["c_mask_sb"].rearrange("d p t -> p d t")))
    C.mask_mla = P.sb([128, 4, 128], BF16)
    toks.append(P.dma("sync", ds, C.mask_mla[:], dr["c_mask_mla"].rearrange("d p t -> p d t")))
    C.negOnes = P.sb([128, 128], BF16)
    C.ones64 = P.sb([128, 64], BF16)
    C.zeroW = P.sb([128, 64], BF16)
    C.zeroK = P.sb([128, 512], BF16)
    C.zeroI = P.sb([128, 128], BF16)
    C.mhalf = P.sb([128, 1], F32)
    C.phalf = P.sb([128, 1], F32)
    toks.append(P.op("vector", lambda e: e.memset(C.negOnes[:], -1.0)))
    toks.append(P.op("vector", lambda e: e.memset(C.ones64[:], 1.0)))
    toks.append(P.op("vector", lambda e: e.memset(C.zeroW[:], 0.0)))
    toks.append(P.op("vector", lambda e: e.memset(C.zeroK[:], 0.0)))
    toks.append(P.op("vector", lambda e: e.memset(C.zeroI[:], 0.0)))
    toks.append(P.op("vector", lambda e: e.memset(C.mhalf[:], -0.5)))
    toks.append(P.op("vector", lambda e: e.memset(C.phalf[:], 0.5)))
    C.ready = toks
    return C


def bcast_vec(P, ds, vec_ap, n, dtype=F32):
    t = P.sb([128, n], dtype)
    tok = P.dma("sync", ds, t[:], vec_ap.rearrange("(o n) -> o n", o=1).to_broadcast([128, n]))
    return t, tok


def rstd_from_ss(P, C, ss, rstd, n, deps, post_scale=1.0):
    t = P.op("vector", lambda e: e.tensor_scalar(out=rstd, in0=ss, scalar1=1.0 / n, scalar2=EPS,
                                                 op0=ALU.mult, op1=ALU.add), deps=deps)
    t = P.op("gpsimd", lambda e: e.tensor_tensor(out=rstd, in0=rstd, in1=C.mhalf[:], op=ALU.pow), deps=[t])
    if post_scale != 1.0:
        t = P.op("gpsimd", lambda e: e.tensor_scalar(out=rstd, in0=rstd, scalar1=post_scale, scalar2=None,
                                                     op0=ALU.mult), deps=[t])
    return t


class NormT:
    def __init__(self, P, C, gvec_ap, tag, ntiles=NT, nps=2):
        self.P, self.C = P, C
        self.ds = P.dsem("nt" + tag)
        self.xds = [P.dsem("ntx" + tag) for _ in range(3)]
        self.g, self.tg = bcast_vec(P, self.ds, gvec_ap, D)
        self.xs_ring = Ring([P.sb([128, D], F32) for _ in range(3)])
        self.xn_ring = Ring([P.sb([128, D], BF16) for _ in range(2)])
        self.junk = P.sb([128, D], BF16)
        self.stat = P.sb([128, 2 * ntiles], F32)
        self.ps_tr = Ring([P.ps([128, 1024], BF16) for _ in range(nps)])
        self.jt = None
        self.n = 0

    def run_many(self, tiles, hT, extra_deps=()):
        P, C = self.P, self.C
        jobs = []
        for (x_ap, col0) in tiles:
            jb = Job()
            jb.x_ap, jb.col0 = x_ap, col0
            jb.i = self.n
            self.n += 1
            jobs.append(jb)
        junk, g = self.junk, self.g

        def s1(jb):
            jb.xs, d0 = self.xs_ring.get()
            jb.xs_k = self.xs_ring.cur
            xs = jb.xs
            t_ld = P.dma("sync", self.xds[jb.xs_k], xs[:], jb.x_ap, deps=list(d0))
            ss = self.stat[:, 2 * jb.i:2 * jb.i + 1]
            jb.rstd = self.stat[:, 2 * jb.i + 1:2 * jb.i + 2]
            t_ss = P.op("scalar", lambda e: e.activation(out=junk[:], in_=xs[:], func=AF.Square, accum_out=ss),
                        deps=[t_ld, self.jt] + C.ready)
            self.jt = t_ss
            jb.t_r = rstd_from_ss(P, C, ss, jb.rstd, D, [t_ss])

        def s2(jb):
            xs, rstd = jb.xs, jb.rstd
            xn, d1 = self.xn_ring.get()
            t_xn = P.op("vector", lambda e: e.scalar_tensor_tensor(
                out=xn[:], in0=xs[:], scalar=rstd, in1=g[:], op0=ALU.mult, op1=ALU.mult),
                deps=[jb.t_r, self.tg] + list(d1))
            self.xs_ring.used_by(t_xn, k=jb.xs_k)
            jb.pt, d2 = self.ps_tr.get()
            jb.pt_k = self.ps_tr.cur
            pt = jb.pt
            t_tr = None
            for k in range(8):
                t_tr = P.op("tensor", lambda e, k=k: e.transpose(
                    out=pt[:, k * 128:(k + 1) * 128], in_=xn[:, k * 128:(k + 1) * 128], identity=C.ident[:]),
                    deps=[t_xn] + list(d2), sig=(k == 7))
            self.xn_ring.used_by(t_tr)
            jb.t_tr = t_tr

        def s3(jb):
            pt, col0 = jb.pt, jb.col0
            t_cp = P.op("scalar", lambda e: e.copy(
                out=hT[:, :, col0:col0 + 128], in_=pt[:].rearrange("p (k t) -> p k t", k=8)),
                deps=[jb.t_tr] + list(extra_deps))
            self.ps_tr.used_by(t_cp, k=jb.pt_k)
            return t_cp

        toks = []
        n = len(jobs)
        lag = 1 if len(self.ps_tr.bufs) < 2 else 2
        for i in range(n + lag):
            if i < n:
                s1(jobs[i])
            if 0 <= i - 1 < n:
                s2(jobs[i - 1])
            if lag == 1:
                if 0 <= i - 1 < n:
                    toks.append(s3(jobs[i - 1]))
            elif 0 <= i - 2 < n:
                toks.append(s3(jobs[i - 2]))
        return toks


class Epilogue:
    def __init__(self, P, C, gvec_ap, x_in, x_out, post_scale, tag):
        self.P, self.C = P, C
        self.ds = P.dsem("ep" + tag)
        self.xds = [P.dsem("epx" + tag) for _ in range(2)]
        self.ods = [P.dsem("epo" + tag) for _ in range(2)]
        self.g, self.tg = bcast_vec(P, self.ds, gvec_ap, D)
        self.x_in, self.x_out, self.post_scale = x_in, x_out, post_scale
        self.junk = P.sb([128, D], BF16)
        self.stat = P.sb([128, 2 * NT], F32)
        self.xs_ring = Ring([P.sb([128, D], F32) for _ in range(2)])
        self.tmp_ring = Ring([P.sb([128, D], F32) for _ in range(2)])
        self.xo_ring = Ring([P.sb([128, D], F32) for _ in range(2)])
        self.jt = None

    def run(self, pf, tt, t_f):
        P, C = self.P, self.C
        ss = self.stat[:, 2 * tt:2 * tt + 1]
        rstd = self.stat[:, 2 * tt + 1:2 * tt + 2]
        junk, g = self.junk, self.g
        t_ss = P.op("scalar", lambda e: e.activation(out=junk[:], in_=pf, func=AF.Square, accum_out=ss),
                    deps=[t_f, self.jt] + C.ready)
        self.jt = t_ss
        t_r = rstd_from_ss(P, C, ss, rstd, D, [t_ss], post_scale=self.post_scale)
        xs, d2 = self.xs_ring.get()
        t_ld = P.dma("sync", self.xds[self.xs_ring.cur], xs[:], self.x_in[tt * 128:(tt + 1) * 128, :], deps=list(d2))
        tmp, d3 = self.tmp_ring.get()
        t_t = P.op("vector", lambda e: e.scalar_tensor_tensor(
            out=tmp[:], in0=pf, scalar=rstd, in1=g[:], op0=ALU.mult, op1=ALU.mult),
            deps=[t_r, self.tg] + list(d3))
        xo, d4 = self.xo_ring.get()
        t_a = P.op("vector", lambda e: e.tensor_tensor(out=xo[:], in0=tmp[:], in1=xs[:], op=ALU.add),
                   deps=[t_t, t_ld] + list(d4))
        self.tmp_ring.used_by(t_a)
        self.xs_ring.used_by(t_a)
        t_st = P.dma("sync", self.ods[self.xo_ring.cur], self.x_out[tt * 128:(tt + 1) * 128, :], xo[:], deps=[t_a])
        self.xo_ring.used_by(t_st)
        return t_t, t_st


def ffn_phase(P, C, x_in, x_out, w_in, w_out, g_pre, g_post, tag):
    m0 = P.mark()
    HT, KC, FC = 8, D // 128, DFF // 128
    ds_w = [P.dsem(f"dw{tag}{i}") for i in range(2)]
    ds_wo = P.dsem("dwo" + tag)
    nt = NormT(P, C, g_pre, "f" + tag)
    ep = Epilogue(P, C, g_post, x_in, x_out, 0.5, "f" + tag)
    xnT = P.sb([128, KC, HT * 128], BF16)
    g = P.sb([128, FC, HT * 128], BF16)
    wo = P.sb([128, FC, D], BF16)
    win_ring = Ring([P.sb([128, KC, 256], BF16) for _ in range(2)])
    sg_ring = Ring([P.sb([128, 512], F32) for _ in range(2)])
    pfs = [P.ps([128, 1024], F32) for _ in range(2)]
    ps_f = Ring(pfs)
    ps_g = Ring([pfs[0][:, 0:512], pfs[1][:, 0:512]])
    ps_u = Ring([pfs[0][:, 512:1024], pfs[1][:, 512:1024]])
    pf_free = []
    w_in_v = w_in.rearrange("(kc p) n -> p kc n", p=128)
    w_out_v = w_out.rearrange("(kc p) n -> p kc n", p=128)
    t_wo = []
    for k0 in range(0, FC, 6):
        k1 = min(FC, k0 + 6)
        t_wo.append(P.dma("gpsimd", ds_wo, wo[:, k0:k1, :], w_out_v[:, k0:k1, :]))
    last = None
    xnT_free, g_free = [], []
    wcount = 0
    t_s = None
    for hf in range(2):
        tA = nt.run_many([(x_in[(hf * HT + j) * 128:(hf * HT + j + 1) * 128, :], j * 128) for j in range(HT)], xnT, xnT_free)
        xnT_free = []
        tB = []
        for c in range(FC):
            wb, d0 = win_ring.get()
            dsw = ds_w[wcount % 2]
            wcount += 1
            t_w1 = P.dma("gpsimd", dsw, wb[:, :, 0:128], w_in_v[:, :, c * 128:(c + 1) * 128], deps=list(d0))
            t_w2 = P.dma("gpsimd", dsw, wb[:, :, 128:256], w_in_v[:, :, DFF + c * 128:DFF + (c + 1) * 128], deps=list(d0))
            for tg in range(2):
                pg, dg = ps_g.get()
                pu, du = ps_u.get()
                sl = slice(tg * 512, (tg + 1) * 512)
                t_g = t_u = None
                for k in range(KC):
                    t_g = P.op("tensor", lambda e, pg=pg, wb=wb, k=k, sl=sl: e.matmul(
                        pg, lhsT=wb[:, k, 0:128], rhs=xnT[:, k, sl], start=(k == 0), stop=(k == KC - 1)),
                        deps=[t_w1, t_w2] + tA + list(dg) + pf_free, sig=(k == KC - 1))
                for k in range(KC):
                    t_u = P.op("tensor", lambda e, pu=pu, wb=wb, k=k, sl=sl: e.matmul(
                        pu, lhsT=wb[:, k, 128:256], rhs=xnT[:, k, sl], start=(k == 0), stop=(k == KC - 1)),
                        deps=list(du), sig=(k == KC - 1))
                win_ring.used_by(t_u)
                xnT_free.append(t_u)
                sg, ds_ = sg_ring.get()
                t_s = P.op("scalar", lambda e, sg=sg, pg=pg: e.activation(out=sg[:], in_=pg, func=AF.Silu),
                           deps=[t_g] + list(ds_))
                ps_g.used_by(t_s)
                t_m = P.op("vector", lambda e, sg=sg, pu=pu, c=c, sl=sl: e.tensor_tensor(
                    out=g[:, c, sl], in0=sg[:], in1=pu, op=ALU.mult), deps=[t_s, t_u] + g_free)
                sg_ring.used_by(t_m)
                ps_u.used_by(t_m)
                tB.append(t_m)
        g_free, pf_free = [], []
        xnT_free = xnT_free[-2:]
        for j in range(HT):
            tt = hf * HT + j
            pf, d0 = ps_f.get()
            t_f = None
            for nh in range(2):
                for k in range(FC):
                    t_f = P.op("tensor", lambda e, pf=pf, nh=nh, k=k, j=j: e.matmul(
                        pf[:, nh * 512:(nh + 1) * 512], lhsT=g[:, k, j * 128:(j + 1) * 128],
                        rhs=wo[:, k, nh * 512:(nh + 1) * 512], start=(k == 0), stop=(k == FC - 1)),
                        deps=tB[-2:] + t_wo + list(d0) + [t_s], sig=(nh == 1 and k == FC - 1))
            g_free.append(t_f)
            t_t, last = ep.run(pf, tt, t_f)
            ps_f.used_by(t_t)
            pf_free = [t_t]
    P.release(m0)
    return [last]


def oproj_phase(P, C, oT_d, w_out, g_post, x_in, x_out, tag):
    m0 = P.mark()
    ds = P.dsem("op" + tag)
    ep = Epilogue(P, C, g_post, x_in, x_out, 1.0, "o" + tag)
    oT = P.sb([128, 8, TOK], BF16)
    wo = P.sb([128, 8, D], BF16)
    t_in = [P.dma("sync", ds, oT[:, 0:4, :], oT_d.rearrange("(kc p) t -> p kc t", p=128)[:, 0:4, :]),
            P.dma("sync", ds, oT[:, 4:8, :], oT_d.rearrange("(kc p) t -> p kc t", p=128)[:, 4:8, :]),
            P.dma("gpsimd", ds, wo[:], w_out.rearrange("(kc p) n -> p kc n", p=128))]
    ps_f = Ring([P.ps([128, 1024], F32) for _ in range(2)])
    last = None
    for tt in range(NT):
        pf, d0 = ps_f.get()
        t_f = None
        for nh in range(2):
            for k in range(8):
                t_f = P.op("tensor", lambda e, pf=pf, nh=nh, k=k, tt=tt: e.matmul(
                    pf[:, nh * 512:(nh + 1) * 512], lhsT=oT[:, k, tt * 128:(tt + 1) * 128],
                    rhs=wo[:, k, nh * 512:(nh + 1) * 512], start=(k == 0), stop=(k == 7)),
                    deps=t_in + list(d0), sig=(nh == 1 and k == 7))
        t_t, last = ep.run(pf, tt, t_f)
        ps_f.used_by(t_t)
    P.release(m0)
    return [last]


def sbproj_phase(P, C, x_in, w_in, g_pre, qT_d, kT_d, v_d, tag, coll=None):
    m0 = P.mark()
    ds_w = [P.dsem(f"sw{tag}{i}") for i in range(2)]
    ds_v = P.dsem("sv" + tag)
    ds_o = [P.dsem(f"so{tag}{i}") for i in range(2)]
    ds_o2 = [P.dsem(f"sp{tag}{i}") for i in range(2)]
    nt = NormT(P, C, g_pre, "s" + tag)
    hT = P.sb([128, 8, TOK], BF16)
    w_v = w_in.rearrange("(kc p) n -> p kc n", p=128)
    wv = P.sb([128, 8, D], BF16)
    t_wv = [P.dma("gpsimd", ds_v, wv[:, 0:4, :], w_v[:, 0:4, 2048:3072]),
            P.dma("gpsimd", ds_v, wv[:, 4:8, :], w_v[:, 4:8, 2048:3072])]
    stg_ring = Ring([P.sb([128, TOK], BF16) for _ in range(2)])
    ps_ring = Ring([P.ps([128, 512], F32) for _ in range(2)])
    order = list(range(8, 16)) + list(range(0, 8))
    wq_all = P.sb([128, 16, 8, 128], BF16)
    ds_wg = [P.dsem(f"swg{tag}{i}") for i in range(2)]
    tA = nt.run_many([(x_in[j * 128:(j + 1) * 128, :], j * 128) for j in range(NT)], hT)
    t_wq = {}
    hist = []
    for i, oc in enumerate(order):
        dep = [hist[i - 2]] if i >= 2 else []
        t_wq[oc] = P.dma("gpsimd", ds_wg[i % 2], wq_all[:, oc, :, :], w_v[:, :, oc * 128:(oc + 1) * 128], deps=dep)
        hist.append(t_wq[oc])
    tc_v, tc_k = [], []
    pv_ring = Ring([P.ps([128, 1024], F32) for _ in range(2)])
    vs_ring = Ring([P.sb([128, D], BF16) for _ in range(2)])
    vst = []
    for tt in range(NT):
        pv, d0 = pv_ring.get()
        t_m = None
        for nh in range(2):
            for k in range(8):
                t_m = P.op("tensor", lambda e, pv=pv, nh=nh, k=k, tt=tt: e.matmul(
                    pv[:, nh * 512:(nh + 1) * 512], lhsT=hT[:, k, tt * 128:(tt + 1) * 128],
                    rhs=wv[:, k, nh * 512:(nh + 1) * 512], start=(k == 0), stop=(k == 7)),
                    deps=t_wv + tA + list(d0), sig=(nh == 1 and k == 7))
        vs, d1 = vs_ring.get()
        t_c = P.op("vector", lambda e, pv=pv, vs=vs: e.tensor_copy(out=vs[:], in_=pv), deps=[t_m] + list(d1))
        pv_ring.used_by(t_c)
        t_st = P.dma("sync", ds_o2[vs_ring.cur], v_d[tt * 128:(tt + 1) * 128, :], vs[:], deps=[t_c])
        vs_ring.used_by(t_st)
        vst.append(t_st)
        if coll is not None and tt % 4 == 3:
            k4 = tt // 4
            tc_v.append(P.collective(v_d[k4 * 512:(k4 + 1) * 512, :], coll[1][k4 * 2048:(k4 + 1) * 2048, :], coll[2],
                                     deps=vst[-4:]))
    kst = []
    pend_coll = None
    for i, oc in enumerate(order):
        wb = wq_all[:, oc, :, :]
        t_w = t_wq[oc]
        if pend_coll is not None:
            k4, deps_ = pend_coll
            tc_k.append(P.collective(kT_d[k4 * 256:(k4 + 1) * 256, :], coll[0][k4 * 1024:(k4 + 1) * 1024, :], coll[2], deps=deps_))
            pend_coll = None
        stg, d1 = stg_ring.get()
        t_e = None
        t_m = None
        for tg in range(4):
            pp, d2 = ps_ring.get()
            for k in range(8):
                t_m = P.op("tensor", lambda e, pp=pp, wb=wb, k=k, tg=tg: e.matmul(
                    pp, lhsT=wb[:, k, :], rhs=hT[:, k, tg * 512:(tg + 1) * 512], start=(k == 0), stop=(k == 7)),
                    deps=[t_w] + tA + list(d2), sig=(k == 7))
            sc = 0.125 if oc < 8 else 1.0
            t_e = P.op("scalar", lambda e, pp=pp, stg=stg, tg=tg, sc=sc: e.mul(
                out=stg[:, tg * 512:(tg + 1) * 512], in_=pp, mul=sc), deps=[t_m] + list(d1))
            ps_ring.used_by(t_e)
        dst = (qT_d if oc < 8 else kT_d)[(oc % 8) * 128:(oc % 8 + 1) * 128, :]
        t_st = P.dma("sync", ds_o[stg_ring.cur], dst, stg[:], deps=[t_e])
        stg_ring.used_by(t_st)
        if oc >= 8:
            kst.append(t_st)
            if coll is not None and (oc - 8) % 2 == 1:
                pend_coll = ((oc - 8) // 2, kst[-2:])
    P.release(m0)
    return tc_k, tc_v


def key_schedule(L):
    out = []
    for kb in range(16 * L + 15, -1, -1):
        if kb >= 16 * L:
            j = (kb - 16 * L) // 4
            out.append((kb, 128 * j, (kb - 16 * L) % 4))
        else:
            out.append((kb, 0, None))
    return out


class Job:
    pass


def sbattn_phase(P, C, qT_d, KT_g, V_g, oT_d, tag, heads=range(16), chunked=False, kv_toks=None):
    m0 = P.mark()
    ds_k = [P.dsem(f"ak{tag}{i}") for i in range(2)]
    ds_o = [P.dsem(f"ao{tag}{i}") for i in range(2)]
    kT_bufs = [(P.sb([128, 4, TOK], BF16), P.sb([128, 4, TOK], BF16)) for _ in range(2)]
    t_kz = []
    for (ka_, kb_) in kT_bufs:
        t_kz.append(P.op("vector", lambda e, ka_=ka_: e.memset(ka_[64:128, :, :], 0.0)))
        t_kz.append(P.op("vector", lambda e, kb_=kb_: e.memset(kb_[0:64, :, :], 0.0)))
    kT_ring = Ring(kT_bufs)
    v_ring = Ring([P.sb([128, 4, 16, 128], BF16) for _ in range(2)])
    q_ring = Ring([P.sb([128, TOK], BF16) for _ in range(2)])
    E_ring = Ring([P.sb([128, 2, 512], F32) for _ in range(2)])
    SP_ring = Ring([P.sb([128, 2, 512], BF16) for _ in range(3)])
    A_ring = Ring([P.sb([128, 2, 512], BF16) for _ in range(3)])
    Smid_ring = Ring([P.sb([128, 512], BF16) for _ in range(3)])
    Snx_ring = Ring([P.sb([128, 512], BF16) for _ in range(3)])
    S32 = P.sb([128, 512], F32)
    S32m = P.sb([128, 512], F32)
    ostg_ring = Ring([P.sb([128, TOK], BF16) for _ in range(2)])
    Z_ring = Ring([P.ps([128, 1024], F32).rearrange("p (b n) -> p b n", b=2) for _ in range(3)])
    O_ring = Ring([P.ps([128, 512], F32) for _ in range(2)])

    jobs = []
    pairs = sorted(set(h // 2 for h in heads))
    holders = []

    def load_pair(pi):
        if pi >= len(pairs):
            return
        hd = holders[pi]
        hp = pairs[pi]
        hd.kT, d0 = kT_ring.get()
        hd.v, d1 = v_ring.get()
        hd.q, d2 = q_ring.get()
        dsk = ds_k[pi % 2]
        if kv_toks is not None:
            d0 = list(d0) + [kv_toks[0][hp // 2]]
            d1 = list(d1) + list(kv_toks[1])
        if chunked:
            ksrc = KT_g[hp // 2, :, (hp % 2) * 128:(hp % 2) * 128 + 128, :].rearrange("r p t -> p r t")
        else:
            ksrc = KT_g[:, hp * 128:(hp + 1) * 128, :].rearrange("r p t -> p r t")
        lt = [P.dma("sync", dsk, hd.kT[0][0:64, :, :], ksrc[0:64], deps=list(d0) + t_kz),
              P.dma("sync", dsk, hd.kT[1][64:128, :, :], ksrc[64:128], deps=list(d0) + t_kz)]
        for r in range(4):
            if chunked:
                for k in range(4):
                    lt.append(P.dma("sync", dsk, hd.v[:, r, 4 * k:4 * k + 4, :],
                                    V_g[k, r].rearrange("(m p) c -> p m c", p=128)[:, :, hp * 128:(hp + 1) * 128],
                                    deps=list(d1)))
            else:
                lt.append(P.dma("sync", dsk, hd.v[:, r, :, :],
                                V_g[r].rearrange("(m p) c -> p m c", p=128)[:, :, hp * 128:(hp + 1) * 128], deps=list(d1)))
        lt.append(P.dma("sync", dsk, hd.q[:], qT_d[hp * 128:(hp + 1) * 128, :], deps=list(d2)))
        hd.loads = lt
        hd.pair_slots = (kT_ring.cur, v_ring.cur, q_ring.cur)

    for pi, hp in enumerate(pairs):
        hd = Job()
        holders.append(hd)
        hs = [h for h in heads if h // 2 == hp]
        cnt = 0
        for h in hs:
            for L in range(4):
                sched = key_schedule(L)
                st = Job()
                npair = len(sched) // 2
                for i in range(npair):
                    (kbA, c0, dlA), (kbB, c0b, dlB) = sched[2 * i], sched[2 * i + 1]
                    assert c0 == c0b
                    jb = Job()
                    jb.stream, jb.hd = st, hd
                    jb.h, jb.hh, jb.L, jb.c0 = h, h % 2, L, c0
                    jb.kb, jb.dl = (kbA, kbB), (dlA, dlB)
                    jb.first, jb.last = (i == 0), (i == npair - 1)
                    jb.next_c0 = sched[2 * i + 2][1] if not jb.last else None
                    jb.pair_last = jb.last and L == 3 and h == hs[-1]
                    jb.pair_first_head = (h == hs[0])
                    jb.prefetch = pi + 1 if cnt == 3 else None
                    cnt += 1
                    jobs.append(jb)
    load_pair(0)
    fin = []
    state = {"s32": None, "s32m": None, "ostg": None}

    def kq(jb, b):
        kb = jb.kb[b]
        ks = jb.hd.kT[jb.hh][:, kb % 4, (kb // 4) * 128:(kb // 4) * 128 + 128]
        qs = jb.hd.q[:, jb.L * 512 + jb.c0:(jb.L + 1) * 512]
        return ks, qs

    def chain(ops, sig_last, fresh=True, close=True):
        n_ = len(ops)
        tok = None
        for i_, (o, l, r, d) in enumerate(ops):
            tok = P.op("tensor", lambda e, o=o, l=l, r=r, i_=i_: e.matmul(
                o, lhsT=l, rhs=r, start=(fresh and i_ == 0), stop=(close and i_ == n_ - 1), skip_group_check=True),
                deps=d, sig=(sig_last and i_ == n_ - 1))
        return tok

    def zops(jb, dst, b, deps):
        c0 = jb.c0
        ks, qs = kq(jb, b)
        ops = [(dst[:, b, c0:512], ks, qs, deps)]
        if jb.dl[b] is not None:
            ops.append((dst[:, b, c0:c0 + 128], C.ident[:], C.mask_sb[:, jb.dl[b], :], []))
        return ops

    def st1_pe(jb):
        Zb, dz = Z_ring.get()
        jb.Z = Zb
        chain(zops(jb, Zb, 0, jb.hd.loads + list(dz) + C.ready), False, close=False)
        jb.tz = chain(zops(jb, Zb, 1, []), True, close=False)

    def st1_act_e(jb):
        c0 = jb.c0
        Eb, de = E_ring.get()
        jb.E = Eb
        Zb = jb.Z
        jb.te = P.op("scalar", lambda e: e.activation(out=Eb[:, :, c0:512], in_=Zb[:, :, c0:512], func=AF.Exp),
                     deps=[jb.tz] + list(de))
        jb.Z_k = Z_ring.cur

    def st1_act_sp(jb):
        c0 = jb.c0
        Eb = jb.E
        SPb, dsp = SP_ring.get()
        jb.SP, jb.SP_k = SPb, SP_ring.cur
        jb.tsp = P.op("scalar", lambda e: e.activation(out=SPb[:, :, c0:512], in_=Eb[:, :, c0:512], func=AF.Ln, bias=1.0),
                      deps=[jb.te] + list(dsp))
        E_ring.used_by(jb.tsp)
        st = jb.stream
        Sm, dsm = Smid_ring.get()
        jb.Smid, jb.Smid_k = Sm, Smid_ring.cur
        if jb.first:
            t_a = P.op("vector", lambda e: e.tensor_copy(out=S32m[:, c0:512], in_=SPb[:, 0, c0:512]),
                       deps=[jb.tsp, state["s32m"]])
            jb.tSmid = P.op("vector", lambda e: e.tensor_copy(out=Sm[:, c0:512], in_=SPb[:, 0, c0:512]),
                            deps=[jb.tsp] + list(dsm))
        else:
            t_a = P.op("vector", lambda e: e.tensor_tensor(out=S32m[:, c0:512], in0=S32[:, c0:512], in1=SPb[:, 0, c0:512],
                                                           op=ALU.add), deps=[jb.tsp, state["s32m"], st.t_b])
            jb.tSmid = P.op("vector", lambda e: e.tensor_tensor(out=Sm[:, c0:512], in0=S32[:, c0:512], in1=SPb[:, 0, c0:512],
                                                                op=ALU.add), deps=[jb.tsp, st.t_b] + list(dsm))
        SP_ring.used_by(jb.tSmid, k=jb.SP_k)
        if not jb.last:
            n0 = jb.next_c0
            t_z = None
            if jb.first and c0 > 0:
                t_z = P.op("vector", lambda e: e.memset(S32m[:, 0:c0], 0.0), deps=[state["s32m"]])
            t_b = P.op("vector", lambda e: e.tensor_tensor(out=S32[:, c0:512], in0=S32m[:, c0:512], in1=SPb[:, 1, c0:512],
                                                           op=ALU.add), deps=[t_a, state["s32"]])
            Sn, dsn = Snx_ring.get()
            t_cv = P.op("vector", lambda e: e.tensor_tensor(out=Sn[:, c0:512], in0=S32m[:, c0:512], in1=SPb[:, 1, c0:512],
                                                            op=ALU.add), deps=[t_a] + list(dsn))
            t_x = None
            if n0 < c0:
                P.op("vector", lambda e: e.memset(S32[:, n0:c0], 0.0), deps=[state["s32"]], sig=False)
                t_cv = P.op("vector", lambda e: e.memset(Sn[:, n0:c0], 0.0), deps=[t_cv])
            st.t_b = t_cv
            state["s32"] = t_cv
            state["s32m"] = t_cv
            SP_ring.used_by(t_cv, k=jb.SP_k)
            st.next_S = (Sn, Snx_ring.cur, t_cv)
        else:
            state["s32m"] = jb.tSmid

    def st2_pe(jb):
        c0 = jb.c0
        Tb = jb.Z
        jb.T = Tb
        SPb = jb.SP
        ops0 = [(Tb[:, 0, c0:512], C.negU[:], SPb[:, 0, c0:512], [jb.tsp, jb.te])]
        if not jb.first:
            Sb, Sk, tS = jb.Sprev
            ops0.append((Tb[:, 0, c0:512], C.negOnes[:], Sb[:, c0:512], [tS]))
        chain(ops0, False, fresh=False)
        ops1 = [(Tb[:, 1, c0:512], C.negU[:], SPb[:, 1, c0:512], []),
                (Tb[:, 1, c0:512], C.negOnes[:], jb.Smid[:, c0:512], [jb.tSmid])]
        jb.tT = chain(ops1, True, fresh=False)
        if not jb.first:
            Snx_ring.used_by(jb.tT, k=Sk)
        Smid_ring.used_by(jb.tT, k=jb.Smid_k)
        SP_ring.used_by(jb.tT, k=jb.SP_k)

    def st2_act(jb):
        c0 = jb.c0
        Tb = jb.T
        Ab, da = A_ring.get()
        jb.A, jb.A_k = Ab, A_ring.cur
        jb.tA = P.op("scalar", lambda e: e.activation(out=Ab[:, :, c0:512], in_=Tb[:, :, c0:512], func=AF.Exp),
                     deps=[jb.tT] + list(da))
        Z_ring.used_by(jb.tA, k=jb.Z_k)

    def st3(jb):
        c0 = jb.c0
        st = jb.stream
        if jb.first:
            st.O, dO = O_ring.get()
            st.O_k = O_ring.cur
            P.op("tensor", lambda e: e.matmul(st.O[:, 0:512], lhsT=C.zeroI[:], rhs=C.zeroK[:], start=True, stop=False),
                 deps=list(dO), sig=False)
        Ab = jb.A
        tav = None
        for b in range(2):
            kb = jb.kb[b]
            vb = jb.hd.v[:, kb % 4, kb // 4, :]
            tav = P.op("tensor", lambda e, b=b, vb=vb: e.matmul(st.O[:, c0:512], lhsT=vb, rhs=Ab[:, b, c0:512], start=False,
                                                                stop=(jb.last and b == 1), skip_group_check=True),
                       deps=[jb.tA], sig=(b == 1))
        A_ring.used_by(tav, k=jb.A_k)
        if jb.last:
            if jb.L == 0 and jb.pair_first_head:
                state["ostg"], state["ostg_d"] = ostg_ring.get()
                state["ostg_k"] = ostg_ring.cur
            og = state["ostg"]
            L = jb.L
            pr = slice(jb.hh * 64, jb.hh * 64 + 64)
            tcp = P.op("vector", lambda e: e.tensor_copy(out=og[pr, L * 512:(L + 1) * 512], in_=st.O[pr, 0:512]),
                       deps=[tav] + list(state["ostg_d"]))
            O_ring.used_by(tcp, k=st.O_k)
            if jb.L == 3:
                hp_ = jb.h // 2
                t_st = P.dma("sync", ds_o[state["ostg_k"]], oT_d[jb.h * 64:(jb.h + 1) * 64, :], og[pr, :], deps=[tcp])
                ostg_ring.used_by(t_st, k=state["ostg_k"])
                fin[:] = [t_st]
            if jb.pair_last:
                kk, vk, qk = jb.hd.pair_slots
                kT_ring.used_by(tav, k=kk)
                v_ring.used_by(tav, k=vk)
                q_ring.used_by(tav, k=qk)

    n = len(jobs)
    for i in range(-1, n + 1):
        nx = jobs[i + 1] if 0 <= i + 1 < n else None
        cur = jobs[i] if 0 <= i < n else None
        pv = jobs[i - 1] if 0 <= i - 1 < n else None
        if nx is not None:
            if not nx.first:
                nx.Sprev = nx.stream.next_S
            if nx.prefetch is not None:
                load_pair(nx.prefetch)
            st1_pe(nx)
        if cur is not None:
            st2_pe(cur)
        if pv is not None:
            st3(pv)
        if nx is not None:
            st1_act_e(nx)
            st1_act_sp(nx)
        if cur is not None:
            st2_act(cur)
    P.release(m0)
    return fin


MLA_SCALE = 1.0 / np.sqrt(96.0)
TWO_PI = 2.0 * np.pi
CW1 = 6.28125
CW2 = float(TWO_PI - 6.28125)


def mlaproj_phase(P, C, x_in, pos_d, w_in, qnorm_g, w_uq, kvnorm_g, w_ukv, g_pre, invf_d,
                  qaT_d, qnT_d, latT_d, knmax_d, tag):
    m0 = P.mark()
    ds = P.dsem("mp" + tag)
    ds_o = P.dsem("mo" + tag)
    nt = NormT(P, C, g_pre, "m" + tag, nps=1)
    hT = P.sb([128, 8, TOK], BF16)
    tA = nt.run_many([(x_in[j * 128:(j + 1) * 128, :], j * 128) for j in range(NT)], hT)
    win = P.sb([128, 8, 416], BF16)
    wuq = P.sb([128, 2, 1536], BF16)
    wuk = P.sb([128, 16, 64], BF16)
    t_w = [P.dma("gpsimd", ds, win[:], w_in.rearrange("(kc p) n -> p kc n", p=128)),
           P.dma("gpsimd", ds, wuq[:], w_uq.rearrange("(kc p) n -> p kc n", p=128)),
           P.dma("gpsimd", ds, wuk[:], w_ukv.rearrange("p (h c) -> p h c", c=128)[:, :, 0:64])]
    gq, t1 = bcast_vec(P, ds, qnorm_g, 256)
    gkv, t2 = bcast_vec(P, ds, kvnorm_g, 128)
    invf, t3 = bcast_vec(P, ds, invf_d, 16)
    posi = P.sb([128, NT], I32)
    t4 = P.dma("sync", ds, posi[:], pos_d.rearrange("(t p) -> p t", p=128), allow_slow_non_contiguous=True)
    t_c = [t1, t2, t3, t4] + t_w
    posf = P.sb([128, NT], F32)
    ang = P.sb([128, NT, 16], F32)
    ang2 = P.sb([128, NT, 16], F32)
    ni = P.sb([128, NT, 16], I32)
    nf = P.sb([128, NT, 16], F32)
    rr = P.sb([128, NT, 16], F32)
    gt = P.sb([128, NT, 16], F32)
    sint = P.sb([128, NT, 16], F32)
    cost = P.sb([128, NT, 16], F32)
    t = P.op("vector", lambda e: e.tensor_copy(out=posf[:], in_=posi[:]), deps=t_c + C.ready)
    t = P.op("vector", lambda e: e.tensor_tensor(out=ang[:], in0=posf[:].unsqueeze(2).to_broadcast([128, NT, 16]),
                                                 in1=invf[:].unsqueeze(1).to_broadcast([128, NT, 16]), op=ALU.mult), deps=[t])
    t = P.op("vector", lambda e: e.tensor_scalar(out=ang2[:], in0=ang[:], scalar1=float(np.pi / 2), scalar2=None,
                                                 op0=ALU.add), deps=[t])

    def sincos(a, out, t):
        t = P.op("vector", lambda e: e.tensor_scalar(out=ni[:], in0=a[:], scalar1=float(1.0 / TWO_PI), scalar2=None,
                                                     op0=ALU.mult), deps=[t])
        t = P.op("vector", lambda e: e.tensor_copy(out=nf[:], in_=ni[:]), deps=[t])
        t = P.op("vector", lambda e: e.scalar_tensor_tensor(out=rr[:], in0=nf[:], scalar=-CW1, in1=a[:],
                                                            op0=ALU.mult, op1=ALU.add), deps=[t])
        t = P.op("vector", lambda e: e.scalar_tensor_tensor(out=rr[:], in0=nf[:], scalar=-CW2, in1=rr[:],
                                                            op0=ALU.mult, op1=ALU.add), deps=[t])
        t = P.op("vector", lambda e: e.tensor_scalar(out=gt[:], in0=rr[:], scalar1=float(np.pi), scalar2=float(-TWO_PI),
                                                     op0=ALU.is_gt, op1=ALU.mult), deps=[t])
        t = P.op("vector", lambda e: e.tensor_tensor(out=rr[:], in0=rr[:], in1=gt[:], op=ALU.add), deps=[t])
        t = P.op("vector", lambda e: e.tensor_scalar(out=gt[:], in0=rr[:], scalar1=float(-np.pi), scalar2=float(TWO_PI),
                                                     op0=ALU.is_lt, op1=ALU.mult), deps=[t])
        t = P.op("vector", lambda e: e.tensor_tensor(out=rr[:], in0=rr[:], in1=gt[:], op=ALU.add), deps=[t])
        t = P.op("vector", lambda e: e.tensor_scalar(out=rr[:], in0=rr[:], scalar1=3.14159, scalar2=-3.14159,
                                                     op0=ALU.min, op1=ALU.max), deps=[t])
        t = P.op("scalar", lambda e: e.activation(out=out[:], in_=rr[:], func=AF.Sin), deps=[t])
        return t
    t = sincos(ang, sint, t)
    t_tab = sincos(ang2, cost, t)

    junk = P.sb([128, 1024], BF16)
    stat = P.sb([128, 8 * NT], F32)
    cqn = P.sb([128, 256], BF16)
    lat = P.sb([128, 160], BF16)
    kr = P.sb([128, 32], F32)
    kro = P.sb([128, 32], F32)
    tr = P.sb([128, 4, 16], F32)
    cqnT = [P.sb([128, 2, 128], BF16) for _ in range(2)]
    ckvT = [P.sb([128, 128], BF16) for _ in range(2)]
    latT_s = P.sb([128, TOK], BF16)
    krT_s = P.sb([32, TOK], BF16)
    qs = P.sb([128, 16, 96], F32)
    qt = P.sb([128, 4, 16, 16], F32)
    qsq = P.sb([128, 16, 96], F32)
    qn = P.sb([128, 16], F32)
    qa = P.sb([128, 16, 96], BF16)
    ksq = P.sb([128, 16, 64], F32)
    kn2 = P.sb([128, 16], F32)
    knmax = P.sb([128, 16], F32)
    qaT_s = P.sb([96, 16, 128], BF16)
    qnT_s = P.sb([16, TOK], F32)
    pj = P.ps([128, 512], F32)
    pT = P.ps([128, 1024], BF16)
    pq3 = P.ps([128, 1536], F32)
    pkn = P.ps([128, 1024], F32)
    pq = pq3[:, 0:1024].bitcast(BF16)
    pqn = pq3[:, 1024:1536]
    last = []
    def tileF(tt, prevF, prevB):
        tsl = slice(tt * 128, (tt + 1) * 128)
        sc = stat[:, 8 * tt:8 * tt + 8]
        t_p = None
        for k in range(8):
            t_p = P.op("tensor", lambda e, k=k: e.matmul(pj[:, 0:416], lhsT=hT[:, k, tsl], rhs=win[:, k, :],
                                                          start=(k == 0), stop=(k == 7)),
                       deps=tA + t_c + [prevF], sig=(k == 7))
        t_s1 = P.op("scalar", lambda e: e.activation(out=junk[:, 0:256], in_=pj[:, 0:256], func=AF.Square,
                                                     accum_out=sc[:, 0:1]), deps=[t_p, prevF])
        t_s2 = P.op("scalar", lambda e: e.activation(out=junk[:, 256:384], in_=pj[:, 256:384], func=AF.Square,
                                                     accum_out=sc[:, 2:3]), deps=[t_p, prevF])
        t_r1 = rstd_from_ss(P, C, sc[:, 0:1], sc[:, 1:2], 256, [t_s1])
        t_r2 = rstd_from_ss(P, C, sc[:, 2:3], sc[:, 3:4], 128, [t_s2])
        t_cq = P.op("vector", lambda e: e.scalar_tensor_tensor(out=cqn[:], in0=pj[:, 0:256], scalar=sc[:, 1:2],
                                                               in1=gq[:], op0=ALU.mult, op1=ALU.mult), deps=[t_r1, prevF])
        t_ck = P.op("vector", lambda e: e.scalar_tensor_tensor(out=lat[:, 0:128], in0=pj[:, 256:384], scalar=sc[:, 3:4],
                                                               in1=gkv[:], op0=ALU.mult, op1=ALU.mult), deps=[t_r2, prevF])
        t_kr = P.op("vector", lambda e: e.tensor_copy(out=kr[:], in_=pj[:, 384:416]), deps=[t_p, prevF])
        cs, sn = cost[:, tt, :], sint[:, tt, :]
        t_a = P.op("vector", lambda e: e.tensor_tensor(out=tr[:, 0, :], in0=kr[:, 0:16], in1=cs, op=ALU.mult), deps=[t_kr, t_tab])
        t_b = P.op("vector", lambda e: e.tensor_tensor(out=tr[:, 1, :], in0=kr[:, 16:32], in1=sn, op=ALU.mult), deps=[t_kr])
        t_c2 = P.op("vector", lambda e: e.tensor_tensor(out=tr[:, 2, :], in0=kr[:, 16:32], in1=cs, op=ALU.mult), deps=[t_kr])
        t_d = P.op("vector", lambda e: e.tensor_tensor(out=tr[:, 3, :], in0=kr[:, 0:16], in1=sn, op=ALU.mult), deps=[t_kr])
        t_o1 = P.op("vector", lambda e: e.tensor_tensor(out=kro[:, 0:16], in0=tr[:, 0, :], in1=tr[:, 1, :], op=ALU.subtract),
                    deps=[t_a, t_b])
        t_o2 = P.op("vector", lambda e: e.tensor_tensor(out=kro[:, 16:32], in0=tr[:, 2, :], in1=tr[:, 3, :], op=ALU.add),
                    deps=[t_c2, t_d])
        t_kl = P.op("vector", lambda e: e.tensor_copy(out=lat[:, 128:160], in_=kro[:]), deps=[t_o1, t_o2])
        t_r2k = P.op("scalar", lambda e: e.activation(out=junk[:, 384:416], in_=kro[:], func=AF.Square,
                                                      accum_out=sc[:, 4:5]), deps=[t_o1, t_o2])
        t_t = None
        for k in range(2):
            P.op("tensor", lambda e, k=k: e.transpose(out=pT[:, k * 128:(k + 1) * 128], in_=cqn[:, k * 128:(k + 1) * 128],
                                                      identity=C.ident[:]), deps=[t_cq, prevF], sig=False)
        P.op("tensor", lambda e: e.transpose(out=pT[:, 256:384], in_=lat[:, 0:128], identity=C.ident[:]),
             deps=[t_ck], sig=False)
        t_t = P.op("tensor", lambda e: e.transpose(out=pT[0:32, 384:512], in_=lat[:, 128:160], identity=C.ident[:]),
                   deps=[t_kl])
        t_x1 = P.op("scalar", lambda e: e.copy(out=cqnT[tt % 2][:], in_=pT[:, 0:256].rearrange("p (k t) -> p k t", k=2)), deps=[t_t, prevF, prevB])
        t_x2 = P.op("scalar", lambda e: e.copy(out=ckvT[tt % 2][:], in_=pT[:, 256:384]), deps=[t_t, prevF, prevB])
        t_x3 = P.op("scalar", lambda e: e.copy(out=latT_s[:, tsl], in_=pT[:, 256:384]), deps=[t_t])
        t_x4 = P.op("scalar", lambda e: e.copy(out=krT_s[:, tsl], in_=pT[0:32, 384:512]), deps=[t_t])
        return [t_x1, t_x2, t_x3, t_x4, t_r2k, t_kl, t_ck, t_cq]

    def tileB(tt, fF, prevB):
        tsl = slice(tt * 128, (tt + 1) * 128)
        sc = stat[:, 8 * tt:8 * tt + 8]
        cs, sn = cost[:, tt, :], sint[:, tt, :]
        prev = prevB
        t_q = None
        for n3 in range(3):
            for k in range(2):
                t_q = P.op("tensor", lambda e, n3=n3, k=k: e.matmul(pq3[:, n3 * 512:(n3 + 1) * 512], lhsT=cqnT[tt % 2][:, k, :],
                                                                    rhs=wuq[:, k, n3 * 512:(n3 + 1) * 512],
                                                                    start=(k == 0), stop=(k == 1)),
                           deps=[fF, prev], sig=(n3 == 2 and k == 1))
        t_kn = None
        for n2 in range(2):
            t_kn = P.op("tensor", lambda e, n2=n2: e.matmul(
                pkn[:, n2 * 512:(n2 + 1) * 512], lhsT=ckvT[tt % 2][:],
                rhs=wuk[:, n2 * 8:(n2 + 1) * 8, :], start=True, stop=True),
                deps=[fF, prev], sig=(n2 == 1))
        t_k1 = P.op("scalar", lambda e: e.activation(out=ksq[:], in_=pkn[:].rearrange("p (h c) -> p h c", c=64),
                                                     func=AF.Square), deps=[t_kn, prev])
        t_k2 = P.op("vector", lambda e: e.tensor_reduce(out=kn2[:], in_=ksq[:], axis=AX.X, op=ALU.add), deps=[t_k1, prev])
        if tt == 0:
            t_k3 = P.op("vector", lambda e: e.tensor_scalar(out=knmax[:], in0=kn2[:], scalar1=sc[:, 4:5], scalar2=None,
                                                            op0=ALU.add), deps=[t_k2, fF])
        else:
            t_k3 = P.op("vector", lambda e: e.scalar_tensor_tensor(out=knmax[:], in0=kn2[:], scalar=sc[:, 4:5], in1=knmax[:],
                                                                   op0=ALU.add, op1=ALU.max), deps=[t_k2, fF, prev])
        t_qs = P.op("scalar", lambda e: e.mul(out=qs[:], in_=pq3[:].rearrange("p (h c) -> p h c", c=96), mul=float(MLA_SCALE)),
                    deps=[t_q, prev])
        csb = cs.unsqueeze(1).to_broadcast([128, 16, 16])
        snb = sn.unsqueeze(1).to_broadcast([128, 16, 16])
        t_a = P.op("vector", lambda e: e.tensor_tensor(out=qt[:, 0, :, :], in0=qs[:, :, 64:80], in1=csb, op=ALU.mult), deps=[t_qs, t_tab, prev])
        t_b = P.op("vector", lambda e: e.tensor_tensor(out=qt[:, 1, :, :], in0=qs[:, :, 80:96], in1=snb, op=ALU.mult), deps=[t_qs])
        t_c2 = P.op("vector", lambda e: e.tensor_tensor(out=qt[:, 2, :, :], in0=qs[:, :, 80:96], in1=csb, op=ALU.mult), deps=[t_qs])
        t_d = P.op("vector", lambda e: e.tensor_tensor(out=qt[:, 3, :, :], in0=qs[:, :, 64:80], in1=snb, op=ALU.mult), deps=[t_qs])
        t_o1 = P.op("vector", lambda e: e.tensor_tensor(out=qs[:, :, 64:80], in0=qt[:, 0, :, :], in1=qt[:, 1, :, :], op=ALU.subtract),
                    deps=[t_a, t_b, t_c2, t_d])
        t_o2 = P.op("vector", lambda e: e.tensor_tensor(out=qs[:, :, 80:96], in0=qt[:, 2, :, :], in1=qt[:, 3, :, :], op=ALU.add),
                    deps=[t_o1])
        t_sq = P.op("vector", lambda e: e.tensor_tensor(out=qsq[:], in0=qs[:], in1=qs[:], op=ALU.mult), deps=[t_o1, t_o2, prev])
        t_n2 = P.op("vector", lambda e: e.tensor_reduce(out=qn[:], in_=qsq[:], axis=AX.X, op=ALU.add), deps=[t_sq])
        t_n = P.op("gpsimd", lambda e: e.tensor_tensor(out=qn[:], in0=qn[:], in1=C.phalf[:].to_broadcast([128, 16]), op=ALU.pow),
                   deps=[t_n2])
        t_qa = P.op("scalar", lambda e: e.copy(out=qa[:], in_=qs[:]), deps=[t_o1, t_o2, prev])
        t_tq = None
        for h in range(16):
            t_tq = P.op("tensor", lambda e, h=h: e.transpose(out=pq[0:96, h * 128:(h + 1) * 128], in_=qa[:, h, :],
                                                             identity=C.ident[:]), deps=[t_qa, t_qs], sig=(h == 15))
        t_tn = P.op("tensor", lambda e: e.transpose(out=pqn[0:16, 0:128], in_=qn[:], identity=C.identf[:]), deps=[t_n, t_qs])
        t_y1 = P.op("vector", lambda e: e.tensor_copy(out=qaT_s[:], in_=pq[0:96, :].rearrange("p (h t) -> p h t", h=16)),
                    deps=[t_tq, prev])
        t_y2 = P.op("vector", lambda e: e.tensor_copy(out=qnT_s[:, tsl], in_=pqn[0:16, 0:128]), deps=[t_tn])
        t_st = P.dma("sync", ds_o, qaT_d[:, :, tsl].rearrange("h d t -> d h t"), qaT_s[:], deps=[t_y1])
        return [t_st, t_y2, t_y1, t_k3, t_n, t_k1, t_tq, t_tn, t_sq]


    fF = {}
    prevF, prevB = [t_tab], [t_tab]
    for i in range(NT + 1):
        if i < NT:
            fF[i] = tileF(i, prevF, prevB)
            prevF = fF[i]
        if i >= 1:
            prevB = tileB(i - 1, fF[i - 1], prevB)
    prev = prevB
    last = [P.dma("sync", ds_o, latT_d[0:128, :], latT_s[:], deps=[prev]),
            P.dma("sync", ds_o, latT_d[128:160, :], krT_s[:], deps=[prev]),
            P.dma("sync", ds_o, qnT_d[:, :], qnT_s[:], deps=[prev]),
            P.dma("sync", ds_o, knmax_d[:, :], knmax[:], deps=[prev])]
    P.release(m0)
    return last[-1:]


def mlaattn_phase(P, C, qaT_d, qnT_d, LatT_g, KN_g, w_ukv, ones_d, oT_d, tag, heads=range(16)):
    m0 = P.mark()
    ds = P.dsem("la" + tag)
    ds_q = [P.dsem(f"lq{tag}{i}") for i in range(2)]
    ds_o = [P.dsem(f"lo{tag}{i}") for i in range(2)]
    ckvnT = P.sb([128, 4, TOK], BF16)
    wkv = P.sb([128, 2048], BF16)
    kn = P.sb([128, 4, 16], F32)
    qn = P.sb([16, TOK], F32)
    mrow = P.sb([16, TOK], BF16)
    kmax = P.sb([16, 2], F32)
    ka_bufs = [P.sb([97, 4, TOK], BF16) for _ in range(2)]
    t0 = [P.dma("sync", ds, ckvnT[:], LatT_g[:, 0:128, :].rearrange("r p t -> p r t")),
          P.dma("gpsimd", ds, wkv[:], w_ukv[:, :]),
          P.dma("sync", ds, kn[:], KN_g.rearrange("r p h -> p r h")),
          P.dma("sync", ds, qn[:], qnT_d[:, :])]
    for kb_ in ka_bufs:
        t0.append(P.dma("sync", ds, kb_[64:96, :, :], LatT_g[:, 128:160, :].rearrange("r p t -> p r t")))
        t0.append(P.dma("sync", ds, kb_[96:97, :, :], ones_d.rearrange("(o r t) -> o r t", o=1, r=4)))
    ka_ring = Ring(ka_bufs)
    v_bufs = [P.sb([128, 4, 16, 128], BF16) for _ in range(2)]
    for vb_ in v_bufs:
        t0.append(P.op("vector", lambda e, vb_=vb_: e.memset(vb_[:, :, :, 64:128], 1.0)))
    v_ring = Ring(v_bufs)
    q_ring = Ring([P.sb([97, TOK], BF16) for _ in range(2)])
    Pm_ring = Ring([P.sb([128, 2, 512], BF16) for _ in range(3)])
    rl_ring = Ring([P.sb([128, 512], F32) for _ in range(2)])
    rls_ring = Ring([P.sb([64, 512], F32) for _ in range(2)])
    ds_r = [P.dsem(f"lr{tag}{i}") for i in range(2)]
    ostg_ring = Ring([P.sb([64, TOK], BF16) for _ in range(2)])
    Sc_ring = Ring([P.ps([128, 1024], F32).rearrange("p (b n) -> p b n", b=2) for _ in range(2)])
    O_ring = Ring([P.ps([128, 512], F32) for _ in range(2)])
    G_ring = Ring([P.ps([128, 512], F32) for _ in range(2)])

    pk, _ = G_ring.get()
    t_k = None
    for r in range(4):
        t_k = P.op("tensor", lambda e, r=r: e.transpose(out=pk[0:16, r * 128:(r + 1) * 128], in_=kn[:, r, :],
                                                        identity=C.identf[:]), deps=t0 + C.ready, sig=(r == 3))
    t_k = P.op("vector", lambda e: e.tensor_reduce(out=kmax[:, 0:1], in_=pk[0:16, :], axis=AX.X, op=ALU.max), deps=[t_k])
    G_ring.used_by(t_k)
    t_k = P.op("gpsimd", lambda e: e.tensor_tensor(out=kmax[:, 1:2], in0=kmax[:, 0:1], in1=C.phalf[0:16, :], op=ALU.pow),
               deps=[t_k] + C.ready)
    t_m = P.op("vector", lambda e: e.tensor_scalar(out=mrow[:], in0=qn[:], scalar1=kmax[:, 1:2], scalar2=-1.0,
                                                   op0=ALU.mult, op1=ALU.mult), deps=[t_k] + t0)

    heads = list(heads)

    def gen(h):
        hd = Job()
        hd.h = h
        qa, dq = q_ring.get()
        hd.q_k = q_ring.cur
        dsq = ds_q[heads.index(h) % 2]
        hd.loads = [P.dma("sync", dsq, qa[0:96, :], qaT_d[h], deps=list(dq)),
                    P.dma("sync", dsq, qa[96:97, :], mrow[h:h + 1, :], deps=list(dq) + [t_m])]
        hd.qa = qa
        ka, dk = ka_ring.get()
        hd.ka_k = ka_ring.cur
        hd.ka = ka
        tg_ = []
        for r in range(4):
            for tg in range(4):
                pg, dg = G_ring.get()
                tm = P.op("tensor", lambda e, pg=pg, r=r, tg=tg: e.matmul(
                    pg[:, :], lhsT=wkv[:, h * 128:h * 128 + 128], rhs=ckvnT[:, r, tg * 512:(tg + 1) * 512],
                    start=True, stop=True), deps=t0 + list(dg))
                tcp = P.op("vector", lambda e, pg=pg, r=r, tg=tg: e.tensor_copy(
                    out=ka[0:64, r, tg * 512:(tg + 1) * 512], in_=pg[0:64, :]), deps=[tm] + list(dk))
                G_ring.used_by(tcp)
                tg_.append(tcp)
        v, dv = v_ring.get()
        hd.v_k = v_ring.cur
        hd.v = v
        for r in range(4):
            for mg in range(2):
                pg, dg = G_ring.get()
                tm = None
                for i in range(8):
                    blk = mg * 8 + i
                    tm = P.op("tensor", lambda e, pg=pg, r=r, i=i, blk=blk: e.matmul(
                        pg[:, i * 64:(i + 1) * 64], lhsT=ckvnT[:, r, blk * 128:(blk + 1) * 128],
                        rhs=wkv[:, h * 128 + 64:h * 128 + 128], start=True, stop=True),
                        deps=t0 + list(dg), sig=(i == 7))
                tcp = P.op("vector", lambda e, pg=pg, r=r, mg=mg: e.tensor_copy(
                    out=v[:, r, mg * 8:(mg + 1) * 8, 0:64], in_=pg[:].rearrange("p (i c) -> p i c", c=64)),
                    deps=[tm] + list(dv))
                G_ring.used_by(tcp)
                tg_.append(tcp)
        hd.ready = hd.loads + tg_[-1:] + t0
        return hd

    fin = []
    state = {}

    def st1(jb):
        c0, hd = jb.c0, jb.hd
        Sb, dsb = Sc_ring.get()
        qs = hd.qa[:, jb.L * 512 + c0:(jb.L + 1) * 512]
        tz = None
        for b_ in range(2):
            kb = jb.kb[b_]
            ks = hd.ka[:, kb % 4, (kb // 4) * 128:(kb // 4) * 128 + 128]
            dl = jb.dl[b_]
            tz = P.op("tensor", lambda e, b_=b_, ks=ks: e.matmul(Sb[:, b_, c0:512], lhsT=ks, rhs=qs, start=True, stop=(dl is None)),
                      deps=(hd.ready + list(dsb)) if b_ == 0 else [], sig=(b_ == 1 and dl is None))
            if dl is not None:
                tz = P.op("tensor", lambda e, b_=b_, dl=dl: e.matmul(Sb[:, b_, c0:c0 + 128], lhsT=C.ident[:], rhs=C.mask_mla[:, dl, :],
                                                                  start=False, stop=True, skip_group_check=True), sig=(b_ == 1))
        Pb, dp = Pm_ring.get()
        jb.Pm, jb.Pm_k = Pb, Pm_ring.cur
        jb.tP = P.op("scalar", lambda e: e.activation(out=Pb[:, :, c0:512], in_=Sb[:, :, c0:512], func=AF.Exp),
                     deps=[tz] + list(dp))
        Sc_ring.used_by(jb.tP)

    def st2(jb):
        c0, hd, st = jb.c0, jb.hd, jb.stream
        if jb.first:
            st.O, dO = O_ring.get()
            st.O_k = O_ring.cur
            P.op("tensor", lambda e: e.matmul(st.O[:, 0:512], lhsT=C.zeroI[:], rhs=C.zeroK[:], start=True, stop=False),
                 deps=list(dO), sig=False)
        Pb = jb.Pm
        tav = None
        for b_ in range(2):
            kb = jb.kb[b_]
            vb = hd.v[:, kb % 4, kb // 4, :]
            tav = P.op("tensor", lambda e, b_=b_, vb=vb: e.matmul(st.O[:, c0:512], lhsT=vb, rhs=Pb[:, b_, c0:512], start=False,
                                                                  stop=(jb.last and b_ == 1), skip_group_check=True),
                       deps=[jb.tP], sig=(b_ == 1))
        Pm_ring.used_by(tav, k=jb.Pm_k)
        if jb.last:
            L = jb.L
            if L == 0:
                state["ostg"], state["ostg_d"] = ostg_ring.get()
                state["ostg_k"] = ostg_ring.cur
            og = state["ostg"]
            rl, drl = rl_ring.get()
            t_r = P.op("vector", lambda e: e.reciprocal(out=rl[64:128, :], in_=st.O[64:128, 0:512]), deps=[tav] + list(drl))
            rls, drs = rls_ring.get()
            t_sh = P.dma("sync", ds_r[rls_ring.cur], rls[:, :], rl[64:128, :], deps=[t_r] + list(drs))
            rl_ring.used_by(t_sh)
            t_o = P.op("vector", lambda e: e.tensor_tensor(out=og[:, L * 512:(L + 1) * 512], in0=st.O[0:64, 0:512], in1=rls[:, :],
                                                           op=ALU.mult), deps=[t_sh] + list(state["ostg_d"]))
            rls_ring.used_by(t_o)
            O_ring.used_by(t_o, k=st.O_k)
            if L == 3:
                t_st = P.dma("sync", ds_o[state["ostg_k"]], oT_d[hd.h * 64:(hd.h + 1) * 64, :], og[:], deps=[t_o])
                ostg_ring.used_by(t_st, k=state["ostg_k"])
                fin[:] = [t_st]
                ka_ring.used_by(tav, k=hd.ka_k)
                v_ring.used_by(tav, k=hd.v_k)
                q_ring.used_by(tav, k=hd.q_k)

    def head_jobs(hd):
        out = []
        for L in range(4):
            sched = key_schedule(L)
            st = Job()
            npair = len(sched) // 2
            for i in range(npair):
                (kbA, c0, dlA), (kbB, c0b, dlB) = sched[2 * i], sched[2 * i + 1]
                jb = Job()
                jb.stream, jb.hd, jb.L, jb.c0 = st, hd, L, c0
                jb.kb, jb.dl = (kbA, kbB), (dlA, dlB)
                jb.first, jb.last = (i == 0), (i == npair - 1)
                out.append(jb)
        return out

    hds = {}
    hds[heads[0]] = gen(heads[0])
    prev_job = None
    for hi, h in enumerate(heads):
        for ji, jb in enumerate(head_jobs(hds[h])):
            if ji == 2 and hi + 1 < len(heads):
                hds[heads[hi + 1]] = gen(heads[hi + 1])
            st1(jb)
            if prev_job is not None:
                st2(prev_job)
            prev_job = jb
    st2(prev_job)
    P.release(m0)
    return fin


CONST_SPECS = [("c_ident", [128, 128], BF16), ("c_identf", [128, 128], F32), ("c_negU", [128, 128], BF16),
               ("c_mask_sb", [4, 128, 128], BF16), ("c_mask_mla", [4, 128, 128], BF16),
               ("c_ones", [8192], BF16), ("c_invf", [16], F32)]
WEIGHT_SPECS = [("norm_g", [2, 6, D], F32), ("ffn_w_in", [2, 2, D, 2 * DFF], F32), ("ffn_w_out", [2, 2, DFF, D], F32),
                ("sb_w_in", [1, D, 3 * D], F32), ("sb_w_out", [1, D, D], F32), ("mla_w_in", [1, D, 416], F32),
                ("mla_q_norm", [1, 256], F32), ("mla_w_uq", [1, 256, 1536], F32), ("mla_kv_norm", [1, 128], F32),
                ("mla_w_ukv", [1, 128, 2048], F32), ("mla_w_out", [1, D, D], F32)]
ACT_SPECS = {"qT": ([D, TOK], BF16), "kT": ([D, TOK], BF16), "v": ([TOK, D], BF16),
             "KT_g": ([4, D, TOK], BF16), "V_g": ([4, TOK, D], BF16),
             "qaT": ([16, 96, TOK], BF16), "qnT": ([16, TOK], F32), "latT": ([160, TOK], BF16), "knmax": ([128, 16], F32),
             "LatT_g": ([4, 160, TOK], BF16), "KN_g": ([4, 128, 16], F32),
             "x": ([TOK, D], F32), "pos": ([TOK], I32), "x1": ([TOK, D], F32), "x4": ([TOK, D], F32), "y": ([TOK, D], F32)}


def build_program(stage, debug=False):
    nc = bass.Bass("TRN2", target_bir_lowering=False)
    dr = {}

    def ext(name, kind):
        shp, dt = ACT_SPECS[name]
        dr[name] = nc.dram_tensor(name, shp, dt, kind=kind).ap()

    CC_NAMES = ("kT", "v", "KT_g", "V_g", "latT", "knmax", "LatT_g", "KN_g")

    def internal(name, shp, dt):
        if debug and name not in CC_NAMES:
            dr[name] = nc.dram_tensor(name, shp, dt, kind="ExternalOutput").ap()
        else:
            dr[name] = nc.dram_tensor(name, shp, dt).ap()

    for n, shp, dt in CONST_SPECS + WEIGHT_SPECS:
        dr[n] = nc.dram_tensor(n, shp, dt, kind="ExternalInput").ap()
    ins = {0: ["x", "pos"], 1: ["x"], 2: ["x1", "qT", "KT_g", "V_g", "pos"], 3: ["x4", "qaT", "qnT", "LatT_g", "KN_g"]}[stage]
    outs = {0: ["y"], 1: ["x1", "qT", "kT", "v"], 2: ["x4", "qaT", "qnT", "latT", "knmax"], 3: ["y"]}[stage]
    for n in ins:
        ext(n, "ExternalInput")
    for n in outs:
        ext(n, "ExternalOutput")
    g = dr["norm_g"]
    with ExitStack() as es:
        P = Plan(nc, es)
        C = load_consts(P, dr)
        if stage == 0:
            GR = [[0, 1, 2, 3], [4, 5, 6, 7]]
            for n in ("xa", "xb", "xc", "xd", "xe"):
                internal(n, [TOK, D], F32)
            for n in ("qT", "kT", "v", "qaT", "qnT", "latT", "knmax"):
                internal(n, *ACT_SPECS[n])
            internal("oT", [D, TOK], BF16)
            internal("oT2", [D, TOK], BF16)
            internal("KT_g", [4 * D, TOK], BF16)
            internal("V_g", [4 * TOK, D], BF16)
            internal("LatT_g", [4 * 160, TOK], BF16)
            internal("KN_g", [4 * 128, 16], F32)
            KT_g = dr["KT_g"].rearrange("(k r p) t -> k r p t", k=4, r=4)
            V_g = dr["V_g"].rearrange("(k r p) t -> k r p t", k=4, r=4)
            LatT_g = dr["LatT_g"].rearrange("(r p) t -> r p t", r=4)
            KN_g = dr["KN_g"].rearrange("(r p) t -> r p t", r=4)
            ffn_phase(P, C, dr["x"], dr["xa"], dr["ffn_w_in"][0, 0], dr["ffn_w_out"][0, 0], g[0, 0], g[0, 1], "a")
            P.barrier()
            tc_k, tc_v = sbproj_phase(P, C, dr["xa"], dr["sb_w_in"][0], g[0, 2], dr["qT"], dr["kT"], dr["v"], "a",
                                      coll=(dr["KT_g"], dr["V_g"], GR))
            P.barrier(exclude_cc=True)
            sbattn_phase(P, C, dr["qT"], KT_g, V_g, dr["oT"], "a", chunked=True, kv_toks=(tc_k, tc_v))
            P.barrier()
            oproj_phase(P, C, dr["oT"], dr["sb_w_out"][0], g[0, 3], dr["xa"], dr["xb"], "a")
            P.barrier()
            ffn_phase(P, C, dr["xb"], dr["xc"], dr["ffn_w_in"][0, 1], dr["ffn_w_out"][0, 1], g[0, 4], g[0, 5], "b")
            P.barrier()
            ffn_phase(P, C, dr["xc"], dr["xd"], dr["ffn_w_in"][1, 0], dr["ffn_w_out"][1, 0], g[1, 0], g[1, 1], "c")
            P.barrier()
            mlaproj_phase(P, C, dr["xd"], dr["pos"], dr["mla_w_in"][0], dr["mla_q_norm"][0], dr["mla_w_uq"][0],
                          dr["mla_kv_norm"][0], dr["mla_w_ukv"][0], g[1, 2], dr["c_invf"],
                          dr["qaT"], dr["qnT"], dr["latT"], dr["knmax"], "a")
            P.barrier()
            t1 = P.collective(dr["latT"], dr["LatT_g"], GR)
            t2 = P.collective(dr["knmax"], dr["KN_g"], GR)
            P.barrier([t1, t2])
            mlaattn_phase(P, C, dr["qaT"], dr["qnT"], LatT_g, KN_g, dr["mla_w_ukv"][0], dr["c_ones"], dr["oT2"], "a")
            P.barrier()
            oproj_phase(P, C, dr["oT2"], dr["mla_w_out"][0], g[1, 3], dr["xd"], dr["xe"], "b")
            P.barrier()
            ffn_phase(P, C, dr["xe"], dr["y"], dr["ffn_w_in"][1, 1], dr["ffn_w_out"][1, 1], g[1, 4], g[1, 5], "d")
        elif stage == 1:
            ffn_phase(P, C, dr["x"], dr["x1"], dr["ffn_w_in"][0, 0], dr["ffn_w_out"][0, 0], g[0, 0], g[0, 1], "a")
            P.barrier()
            sbproj_phase(P, C, dr["x1"], dr["sb_w_in"][0], g[0, 2], dr["qT"], dr["kT"], dr["v"], "a")
        elif stage == 2:
            internal("oT", [D, TOK], BF16)
            internal("x2", [TOK, D], F32)
            internal("x3", [TOK, D], F32)
            sbattn_phase(P, C, dr["qT"], dr["KT_g"], dr["V_g"], dr["oT"], "a")
            P.barrier()
            oproj_phase(P, C, dr["oT"], dr["sb_w_out"][0], g[0, 3], dr["x1"], dr["x2"], "a")
            P.barrier()
            ffn_phase(P, C, dr["x2"], dr["x3"], dr["ffn_w_in"][0, 1], dr["ffn_w_out"][0, 1], g[0, 4], g[0, 5], "b")
            P.barrier()
            ffn_phase(P, C, dr["x3"], dr["x4"], dr["ffn_w_in"][1, 0], dr["ffn_w_out"][1, 0], g[1, 0], g[1, 1], "c")
            P.barrier()
            mlaproj_phase(P, C, dr["x4"], dr["pos"], dr["mla_w_in"][0], dr["mla_q_norm"][0], dr["mla_w_uq"][0],
                          dr["mla_kv_norm"][0], dr["mla_w_ukv"][0], g[1, 2], dr["c_invf"],
                          dr["qaT"], dr["qnT"], dr["latT"], dr["knmax"], "a")
        elif stage == 3:
            internal("oT", [D, TOK], BF16)
            internal("x5", [TOK, D], F32)
            mlaattn_phase(P, C, dr["qaT"], dr["qnT"], dr["LatT_g"], dr["KN_g"], dr["mla_w_ukv"][0], dr["c_ones"],
                          dr["oT"], "a")
            P.barrier()
            oproj_phase(P, C, dr["oT"], dr["mla_w_out"][0], g[1, 3], dr["x4"], dr["x5"], "b")
            P.barrier()
            ffn_phase(P, C, dr["x5"], dr["y"], dr["ffn_w_in"][1, 1], dr["ffn_w_out"][1, 1], g[1, 4], g[1, 5], "d")
        if debug and stage == 0:
            P.barrier()
            dsd = P.dsem("dbg")
            for n in CC_NAMES:
                src = dr[n]
                o = nc.dram_tensor("dbg_" + n, list(src.shape), src.dtype, kind="ExternalOutput").ap()
                R = src.shape[0]
                step = max(1, R // 8)
                for r0 in range(0, R, step):
                    P.dma("sync", dsd, o[r0:r0 + step, :], src[r0:r0 + step, :])
        P.emit([(d["sem"], d["val"]) for d in P.dsems if d["val"] > 0])
    return nc


def host_consts(cp):
    s = np.arange(128)[:, None]
    t = np.arange(128)[None, :]
    negU = np.where(s >= t, -1.0, 0.0)
    msb = np.zeros((4, 128, 128), np.float32)
    mml = np.zeros((4, 128, 128), np.float32)
    for d in range(4):
        if d == cp:
            msb[d] = np.where(s >= t, NEG, 0.0)
            mml[d] = np.where(s > t, NEG, 0.0)
        elif d > cp:
            msb[d] = NEG
            mml[d] = NEG
    bf = ml_dtypes.bfloat16
    invf = (10000.0 ** (-np.arange(0, 32, 2, dtype=np.float32) / np.float32(32))).astype(np.float32)
    return {"c_ident": np.eye(128, dtype=bf), "c_identf": np.eye(128, dtype=np.float32), "c_negU": negU.astype(bf),
            "c_mask_sb": msb.astype(bf), "c_mask_mla": mml.astype(bf), "c_ones": np.ones(8192, dtype=bf),
            "c_invf": invf}


_PROGS = {}


def _prog(stage):
    if stage not in _PROGS:
        _PROGS[stage] = build_program(stage)
    return _PROGS[stage]


def kernel(x, positions, norm_g, ffn_w_in, ffn_w_out, sb_w_in, sb_w_out, mla_w_in, mla_q_norm, mla_w_uq,
           mla_kv_norm, mla_w_ukv, mla_w_out):
    ncore = 8
    W = {"norm_g": norm_g, "ffn_w_in": ffn_w_in, "ffn_w_out": ffn_w_out, "sb_w_in": sb_w_in, "sb_w_out": sb_w_out,
         "mla_w_in": mla_w_in, "mla_q_norm": mla_q_norm, "mla_w_uq": mla_w_uq, "mla_kv_norm": mla_kv_norm,
         "mla_w_ukv": mla_w_ukv, "mla_w_out": mla_w_out}
    W = {k: np.ascontiguousarray(np.asarray(v, dtype=np.float32)) for k, v in W.items()}
    x = np.asarray(x, dtype=np.float32)
    positions = np.asarray(positions, dtype=np.int32)
    base = []
    for c in range(ncore):
        b, cp = c // 4, c % 4
        m = dict(W)
        m.update(host_consts(cp))
        base.append(m)
    xs = x.reshape(2, 16, 4, 128, D)
    ps = positions.reshape(2, 16, 4, 128)
    ids = list(range(ncore))

    maps = [dict(base[c], x=np.ascontiguousarray(xs[c // 4, :, c % 4]).reshape(TOK, D),
                 pos=np.ascontiguousarray(ps[c // 4, :, c % 4]).reshape(TOK)) for c in range(ncore)]
    r3 = run_bass_kernel_spmd(_prog(0), maps, core_ids=ids).results
    out = np.empty((2, 16, 4, 128, D), np.float32)
    for c in range(ncore):
        out[c // 4, :, c % 4] = r3[c]["y"].reshape(16, 128, D)
    return out.reshape(2, 8192, D)
```

```python
import numpy as np
import ml_dtypes
from contextlib import ExitStack
import concourse.bass as bass
import concourse.mybir as mybir
from concourse.bass_utils import run_bass_kernel_spmd

F32 = mybir.dt.float32
BF16 = mybir.dt.bfloat16
I32 = mybir.dt.int32
AF = mybir.ActivationFunctionType
ALU = mybir.AluOpType
AX = mybir.AxisListType

D = 1024
DFF = 2816
NT = 16
TOK = 2048
EPS = 1e-6
NEG = -30000.0
U8 = mybir.dt.uint8
SB_BYTES = 204 * 1024
DT_SIZE = {F32: 4, BF16: 2, I32: 4, U8: 1}


class Plan:
    ENGS = ("tensor", "vector", "scalar", "gpsimd", "sync")

    def __init__(self, nc, es):
        self.nc = nc
        self.es = es
        self.ops = {e: [] for e in self.ENGS}
        self.sem = {e: es.enter_context(nc.semaphore("s_" + e)) for e in self.ENGS}
        self.cnt = {e: 0 for e in self.ENGS}
        self.seen = {e: {} for e in self.ENGS}
        self.dsems = []
        self.dfree = []
        self.dlive = []
        self.nsb = 0
        self.pending = {e: [] for e in self.ENGS}
        self._init_mem()

    def _init_mem(self):
        self.arena = self.es.enter_context(self.nc.sbuf_tensor("arena", [128, SB_BYTES], U8))
        self.psum = self.es.enter_context(self.nc.psum_tensor("psum", [128, 4096], F32))
        self.sb_off = 0
        self.ps_off = 0

    def sb(self, shape, dtype, name=None):
        esz = DT_SIZE[dtype]
        n = 1
        for d in shape[1:]:
            n *= d
        nb = (n * esz + 63) // 64 * 64
        assert self.sb_off + nb <= SB_BYTES, f"SBUF arena overflow {self.sb_off}+{nb}"
        v = self.arena[0:shape[0], self.sb_off:self.sb_off + n * esz]
        self.sb_off += nb
        if dtype != U8:
            v = v.bitcast(dtype)
        if len(shape) == 3:
            v = v.rearrange("p (a b) -> p a b", a=shape[1])
        elif len(shape) == 4:
            v = v.rearrange("p (a b c) -> p a b c", a=shape[1], b=shape[2])
        return v

    def ps(self, shape, dtype=F32, name=None):
        esz = DT_SIZE[dtype]
        nb = shape[1] * esz
        nbank = (nb + 2047) // 2048
        assert self.ps_off + nbank <= 8, "PSUM overflow"
        c0 = self.ps_off * 512
        self.ps_off += nbank
        v = self.psum[0:shape[0], c0:c0 + nb // 4]
        if dtype != F32:
            v = v.bitcast(dtype)
        return v

    def mark(self):
        return (self.sb_off, self.ps_off, len(self.dlive))

    def release(self, m):
        self.sb_off, self.ps_off, nd = m
        self.dfree.extend(self.dlive[nd:])
        del self.dlive[nd:]

    def barrier(self, extra=(), exclude_cc=False):
        toks = [(self.sem[e], self.cnt[e]) for e in self.ENGS if self.cnt[e] > 0]
        toks += [(d["sem"], d["val"]) for d in self.dsems if d["val"] > 0 and not (exclude_cc and d.get("cc"))]
        toks += list(extra)
        for e in self.ENGS:
            self.pending[e] = self.pending[e] + self._waits(e, toks)

    def dsem(self, name):
        if self.dfree:
            st = self.dfree.pop()
        else:
            s = self.es.enter_context(self.nc.semaphore(f"d{len(self.dsems)}_{name}"))
            st = {"sem": s, "val": 0}
            self.dsems.append(st)
        self.dlive.append(st)
        return st

    def _waits(self, eng, deps):
        w = []
        best = {}
        for d in deps:
            if d is None:
                continue
            if isinstance(d, (list, tuple)) and d and isinstance(d[0], (list, tuple)):
                for dd in d:
                    if dd is not None:
                        k = id(dd[0])
                        if k not in best or best[k][1] < dd[1]:
                            best[k] = dd
                continue
            k = id(d[0])
            if k not in best or best[k][1] < d[1]:
                best[k] = d
        for k, (s, v) in best.items():
            if self.seen[eng].get(k, 0) < v:
                self.seen[eng][k] = v
                w.append((s, v))
        return w

    def op(self, eng, fn, deps=(), sig=True):
        w = self.pending[eng] + self._waits(eng, deps)
        self.pending[eng] = []
        tok = None
        if sig:
            self.cnt[eng] += 1
            tok = (self.sem[eng], self.cnt[eng])
        self.ops[eng].append((w, fn, self.sem[eng] if sig else None, 1))
        return tok

    def dma(self, eng, ds, out, in_, deps=(), **kw):
        w = self.pending[eng] + self._waits(eng, deps)
        self.pending[eng] = []
        ds["val"] += 16
        tok = (ds["sem"], ds["val"])
        self.ops[eng].append((w, lambda e: e.dma_start(out=out, in_=in_, **kw), ds["sem"], 16))
        return tok

    def collective(self, src, dst, groups, deps=()):
        st = {"sem": self.es.enter_context(self.nc.semaphore(f"cc{len(self.dsems)}")), "val": 0, "cc": True}
        self.dsems.append(st)
        w = self.pending["gpsimd"] + self._waits("gpsimd", list(deps))
        self.pending["gpsimd"] = []
        st["val"] += 1
        tok = (st["sem"], st["val"])
        self.ops["gpsimd"].append((w, lambda e: e.collective_compute(
            "AllGather", ALU.bypass, replica_groups=groups, ins=[src.opt()], outs=[dst.opt()]), st["sem"], 1))
        return tok

    def emit(self, final_waits):
        nc = self.nc
        with nc.Block() as block:
            def mk(eng):
                def body(e):
                    for (w, fn, s, inc) in self.ops[eng]:
                        for (ws, wv) in w:
                            e.wait_ge(ws, wv)
                        ins = fn(e)
                        if s is not None:
                            ins.then_inc(s, inc)
                    if eng == "sync":
                        for (ws, wv) in final_waits:
                            e.wait_ge(ws, wv)
                return body
            block.tensor(mk("tensor"))
            block.vector(mk("vector"))
            block.scalar(mk("scalar"))
            block.gpsimd(mk("gpsimd"))
            block.sync(mk("sync"))


class Job:
    pass


class Ring:
    def __init__(self, bufs):
        self.bufs = bufs
        self.users = [[] for _ in bufs]
        self.i = 0

    def get(self):
        k = self.i % len(self.bufs)
        self.i += 1
        deps = self.users[k]
        self.users[k] = []
        self.cur = k
        return self.bufs[k], deps

    def used_by(self, tok, k=None):
        if tok is not None:
            self.users[self.cur if k is None else k].append(tok)


class Consts:
    pass


def load_consts(P, dr):
    C = Consts()
    ds = P.dsem("d_const")
    toks = []
    C.ident = P.sb([128, 128], BF16)
    toks.append(P.dma("sync", ds, C.ident[:], dr["c_ident"][:, :]))
    C.identf = P.sb([128, 128], F32)
    toks.append(P.dma("sync", ds, C.identf[:], dr["c_identf"][:, :]))
    C.negU = P.sb([128, 128], BF16)
    toks.append(P.dma("sync", ds, C.negU[:], dr["c_negU"][:, :]))
    C.mask_sb = P.sb([128, 4, 128], BF16)
    toks.append(P.dma("sync", ds, C.mask_sb[:], dr["c_mask_sb"].rearrange("d p t -> p d t")))
    C.mask_mla = P.sb([128, 4, 128], BF16)
    toks.append(P.dma("sync", ds, C.mask_mla[:], dr["c_mask_mla"].rearrange("d p t -> p d t")))
    C.negOnes = P.sb([128, 128], BF16)
    C.ones64 = P.sb([128, 64], BF16)
    C.zeroW = P.sb([128, 64], BF16)
    C.zeroK = P.sb([128, 512], BF16)
    C.zeroI = P.sb([128, 128], BF16)
    C.mhalf = P.sb([128, 1], F32)
    C.phalf = P.sb([128, 1], F32)
    toks.append(P.op("vector", lambda e: e.memset(C.negOnes[:], -1.0)))
    toks.append(P.op("vector", lambda e: e.memset(C.ones64[:], 1.0)))
    toks.append(P.op("vector", lambda e: e.memset(C.zeroW[:], 0.0)))
    toks.append(P.op("vector", lambda e: e.memset(C.zeroK[:], 0.0)))
    toks.append(P.op("vector", lambda e: e.memset(C.zeroI[:], 0.0)))
    toks.append(P.op("vector", lambda e: e.memset(C.mhalf[:], -0.5)))
    toks.append(P.op("vector", lambda e: e.memset(C.phalf[:], 0.5)))
    C.ready = toks
    return C


def bcast_vec(P, ds, vec_ap, n, dtype=F32):
    t = P.sb([128, n], dtype)
    tok = P.dma("sync", ds, t[:], vec_ap.rearrange("(o n) -> o n", o=1).to_broadcast([128, n]))
    return t, tok


def rstd_from_ss(P, C, ss, rstd, n, deps, post_scale=1.0):
    t = P.op("vector", lambda e: e.tensor_scalar(out=rstd, in0=ss, scalar1=1.0 / n, scalar2=EPS,
                                                 op0=ALU.mult, op1=ALU.add), deps=deps)
    t = P.op("gpsimd", lambda e: e.tensor_tensor(out=rstd, in0=rstd, in1=C.mhalf[:], op=ALU.pow), deps=[t])
    if post_scale != 1.0:
        t = P.op("gpsimd", lambda e: e.tensor_scalar(out=rstd, in0=rstd, scalar1=post_scale, scalar2=None,
                                                     op0=ALU.mult), deps=[t])
    return t


class NormT:
    def __init__(self, P, C, gvec_ap, tag, ntiles=NT, nps=2):
        self.P, self.C = P, C
        self.ds = P.dsem("nt" + tag)
        self.xds = [P.dsem("ntx" + tag) for _ in range(3)]
        self.g, self.tg = bcast_vec(P, self.ds, gvec_ap, D)
        self.xs_ring = Ring([P.sb([128, D], F32) for _ in range(3)])
        self.xn_ring = Ring([P.sb([128, D], BF16) for _ in range(2)])
        self.junk = P.sb([128, D], BF16)
        self.stat = P.sb([128, 2 * ntiles], F32)
        self.ps_tr = Ring([P.ps([128, 1024], BF16) for _ in range(nps)])
        self.jt = None
        self.n = 0

    def run_many(self, tiles, hT, extra_deps=()):
        P, C = self.P, self.C
        jobs = []
        for (x_ap, col0) in tiles:
            jb = Job()
            jb.x_ap, jb.col0 = x_ap, col0
            jb.i = self.n
            self.n += 1
            jobs.append(jb)
        junk, g = self.junk, self.g

        def s1(jb):
            jb.xs, d0 = self.xs_ring.get()
            jb.xs_k = self.xs_ring.cur
            xs = jb.xs
            t_ld = P.dma("sync", self.xds[jb.xs_k], xs[:], jb.x_ap, deps=list(d0))
            ss = self.stat[:, 2 * jb.i:2 * jb.i + 1]
            jb.rstd = self.stat[:, 2 * jb.i + 1:2 * jb.i + 2]
            t_ss = P.op("scalar", lambda e: e.activation(out=junk[:], in_=xs[:], func=AF.Square, accum_out=ss),
                        deps=[t_ld, self.jt] + C.ready)
            self.jt = t_ss
            jb.t_r = rstd_from_ss(P, C, ss, jb.rstd, D, [t_ss])

        def s2(jb):
            xs, rstd = jb.xs, jb.rstd
            xn, d1 = self.xn_ring.get()
            t_xn = P.op("vector", lambda e: e.scalar_tensor_tensor(
                out=xn[:], in0=xs[:], scalar=rstd, in1=g[:], op0=ALU.mult, op1=ALU.mult),
                deps=[jb.t_r, self.tg] + list(d1))
            self.xs_ring.used_by(t_xn, k=jb.xs_k)
            jb.pt, d2 = self.ps_tr.get()
            jb.pt_k = self.ps_tr.cur
            pt = jb.pt
            t_tr = None
            for k in range(8):
                t_tr = P.op("tensor", lambda e, k=k: e.transpose(
                    out=pt[:, k * 128:(k + 1) * 128], in_=xn[:, k * 128:(k + 1) * 128], identity=C.ident[:]),
                    deps=[t_xn] + list(d2), sig=(k == 7))
            self.xn_ring.used_by(t_tr)
            jb.t_tr = t_tr

        def s3(jb):
            pt, col0 = jb.pt, jb.col0
            t_cp = P.op("scalar", lambda e: e.copy(
                out=hT[:, :, col0:col0 + 128], in_=pt[:].rearrange("p (k t) -> p k t", k=8)),
                deps=[jb.t_tr] + list(extra_deps))
            self.ps_tr.used_by(t_cp, k=jb.pt_k)
            return t_cp

        toks = []
        n = len(jobs)
        lag = 1 if len(self.ps_tr.bufs) < 2 else 2
        for i in range(n + lag):
            if i < n:
                s1(jobs[i])
            if 0 <= i - 1 < n:
                s2(jobs[i - 1])
            if lag == 1:
                if 0 <= i - 1 < n:
                    toks.append(s3(jobs[i - 1]))
            elif 0 <= i - 2 < n:
                toks.append(s3(jobs[i - 2]))
        return toks


class Epilogue:
    def __init__(self, P, C, gvec_ap, x_in, x_out, post_scale, tag):
        self.P, self.C = P, C
        self.ds = P.dsem("ep" + tag)
        self.xds = [P.dsem("epx" + tag) for _ in range(2)]
        self.ods = [P.dsem("epo" + tag) for _ in range(2)]
        self.g, self.tg = bcast_vec(P, self.ds, gvec_ap, D)
        self.x_in, self.x_out, self.post_scale = x_in, x_out, post_scale
        self.junk = P.sb([128, D], BF16)
        self.stat = P.sb([128, 2 * NT], F32)
        self.xs_ring = Ring([P.sb([128, D], F32) for _ in range(2)])
        self.tmp_ring = Ring([P.sb([128, D], F32) for _ in range(2)])
        self.xo_ring = Ring([P.sb([128, D], F32) for _ in range(2)])
        self.jt = None

    def run(self, pf, tt, t_f):
        P, C = self.P, self.C
        ss = self.stat[:, 2 * tt:2 * tt + 1]
        rstd = self.stat[:, 2 * tt + 1:2 * tt + 2]
        junk, g = self.junk, self.g
        t_ss = P.op("scalar", lambda e: e.activation(out=junk[:], in_=pf, func=AF.Square, accum_out=ss),
                    deps=[t_f, self.jt] + C.ready)
        self.jt = t_ss
        t_r = rstd_from_ss(P, C, ss, rstd, D, [t_ss], post_scale=self.post_scale)
        xs, d2 = self.xs_ring.get()
        t_ld = P.dma("sync", self.xds[self.xs_ring.cur], xs[:], self.x_in[tt * 128:(tt + 1) * 128, :], deps=list(d2))
        tmp, d3 = self.tmp_ring.get()
        t_t = P.op("vector", lambda e: e.scalar_tensor_tensor(
            out=tmp[:], in0=pf, scalar=rstd, in1=g[:], op0=ALU.mult, op1=ALU.mult),
            deps=[t_r, self.tg] + list(d3))
        xo, d4 = self.xo_ring.get()
        t_a = P.op("vector", lambda e: e.tensor_tensor(out=xo[:], in0=tmp[:], in1=xs[:], op=ALU.add),
                   deps=[t_t, t_ld] + list(d4))
        self.tmp_ring.used_by(t_a)
        self.xs_ring.used_by(t_a)
        t_st = P.dma("sync", self.ods[self.xo_ring.cur], self.x_out[tt * 128:(tt + 1) * 128, :], xo[:], deps=[t_a])
        self.xo_ring.used_by(t_st)
        return t_t, t_st


def ffn_phase(P, C, x_in, x_out, w_in, w_out, g_pre, g_post, tag):
    m0 = P.mark()
    HT, KC, FC = 8, D // 128, DFF // 128
    ds_w = [P.dsem(f"dw{tag}{i}") for i in range(2)]
    ds_wo = P.dsem("dwo" + tag)
    nt = NormT(P, C, g_pre, "f" + tag)
    ep = Epilogue(P, C, g_post, x_in, x_out, 0.5, "f" + tag)
    xnT = P.sb([128, KC, HT * 128], BF16)
    g = P.sb([128, FC, HT * 128], BF16)
    wo = P.sb([128, FC, D], BF16)
    win_ring = Ring([P.sb([128, KC, 256], BF16) for _ in range(2)])
    sg_ring = Ring([P.sb([128, 512], F32) for _ in range(2)])
    pfs = [P.ps([128, 1024], F32) for _ in range(2)]
    ps_f = Ring(pfs)
    ps_g = Ring([pfs[0][:, 0:512], pfs[1][:, 0:512]])
    ps_u = Ring([pfs[0][:, 512:1024], pfs[1][:, 512:1024]])
    pf_free = []
    w_in_v = w_in.rearrange("(kc p) n -> p kc n", p=128)
    w_out_v = w_out.rearrange("(kc p) n -> p kc n", p=128)
    t_wo = []
    for k0 in range(0, FC, 6):
        k1 = min(FC, k0 + 6)
        t_wo.append(P.dma("gpsimd", ds_wo, wo[:, k0:k1, :], w_out_v[:, k0:k1, :]))
    last = None
    xnT_free, g_free = [], []
    wcount = 0
    t_s = None
    for hf in range(2):
        tA = nt.run_many([(x_in[(hf * HT + j) * 128:(hf * HT + j + 1) * 128, :], j * 128) for j in range(HT)], xnT, xnT_free)
        xnT_free = []
        tB = []
        for c in range(FC):
            wb, d0 = win_ring.get()
            dsw = ds_w[wcount % 2]
            wcount += 1
            t_w1 = P.dma("gpsimd", dsw, wb[:, :, 0:128], w_in_v[:, :, c * 128:(c + 1) * 128], deps=list(d0))
            t_w2 = P.dma("gpsimd", dsw, wb[:, :, 128:256], w_in_v[:, :, DFF + c * 128:DFF + (c + 1) * 128], deps=list(d0))
            for tg in range(2):
                pg, dg = ps_g.get()
                pu, du = ps_u.get()
                sl = slice(tg * 512, (tg + 1) * 512)
                t_g = t_u = None
                for k in range(KC):
                    t_g = P.op("tensor", lambda e, pg=pg, wb=wb, k=k, sl=sl: e.matmul(
                        pg, lhsT=wb[:, k, 0:128], rhs=xnT[:, k, sl], start=(k == 0), stop=(k == KC - 1)),
                        deps=[t_w1, t_w2] + tA + list(dg) + pf_free, sig=(k == KC - 1))
                for k in range(KC):
                    t_u = P.op("tensor", lambda e, pu=pu, wb=wb, k=k, sl=sl: e.matmul(
                        pu, lhsT=wb[:, k, 128:256], rhs=xnT[:, k, sl], start=(k == 0), stop=(k == KC - 1)),
                        deps=list(du), sig=(k == KC - 1))
                win_ring.used_by(t_u)
                xnT_free.append(t_u)
                sg, ds_ = sg_ring.get()
                t_s = P.op("scalar", lambda e, sg=sg, pg=pg: e.activation(out=sg[:], in_=pg, func=AF.Silu),
                           deps=[t_g] + list(ds_))
                ps_g.used_by(t_s)
                t_m = P.op("vector", lambda e, sg=sg, pu=pu, c=c, sl=sl: e.tensor_tensor(
                    out=g[:, c, sl], in0=sg[:], in1=pu, op=ALU.mult), deps=[t_s, t_u] + g_free)
                sg_ring.used_by(t_m)
                ps_u.used_by(t_m)
                tB.append(t_m)
        g_free, pf_free = [], []
        xnT_free = xnT_free[-2:]
        for j in range(HT):
            tt = hf * HT + j
            pf, d0 = ps_f.get()
            t_f = None
            for nh in range(2):
                for k in range(FC):
                    t_f = P.op("tensor", lambda e, pf=pf, nh=nh, k=k, j=j: e.matmul(
                        pf[:, nh * 512:(nh + 1) * 512], lhsT=g[:, k, j * 128:(j + 1) * 128],
                        rhs=wo[:, k, nh * 512:(nh + 1) * 512], start=(k == 0), stop=(k == FC - 1)),
                        deps=tB[-2:] + t_wo + list(d0) + [t_s], sig=(nh == 1 and k == FC - 1))
            g_free.append(t_f)
            t_t, last = ep.run(pf, tt, t_f)
            ps_f.used_by(t_t)
            pf_free = [t_t]
    P.release(m0)
    return [last]


def oproj_phase(P, C, oT_d, w_out, g_post, x_in, x_out, tag):
    m0 = P.mark()
    ds = P.dsem("op" + tag)
    ep = Epilogue(P, C, g_post, x_in, x_out, 1.0, "o" + tag)
    oT = P.sb([128, 8, TOK], BF16)
    wo = P.sb([128, 8, D], BF16)
    t_in = [P.dma("sync", ds, oT[:, 0:4, :], oT_d.rearrange("(kc p) t -> p kc t", p=128)[:, 0:4, :]),
            P.dma("sync", ds, oT[:, 4:8, :], oT_d.rearrange("(kc p) t -> p kc t", p=128)[:, 4:8, :]),
            P.dma("gpsimd", ds, wo[:], w_out.rearrange("(kc p) n -> p kc n", p=128))]
    ps_f = Ring([P.ps([128, 1024], F32) for _ in range(2)])
    last = None
    for tt in range(NT):
        pf, d0 = ps_f.get()
        t_f = None
        for nh in range(2):
            for k in range(8):
                t_f = P.op("tensor", lambda e, pf=pf, nh=nh, k=k, tt=tt: e.matmul(
                    pf[:, nh * 512:(nh + 1) * 512], lhsT=oT[:, k, tt * 128:(tt + 1) * 128],
                    rhs=wo[:, k, nh * 512:(nh + 1) * 512], start=(k == 0), stop=(k == 7)),
                    deps=t_in + list(d0), sig=(nh == 1 and k == 7))
        t_t, last = ep.run(pf, tt, t_f)
        ps_f.used_by(t_t)
    P.release(m0)
    return [last]


def sbproj_phase(P, C, x_in, w_in, g_pre, qT_d, kT_d, v_d, tag, coll=None):
    m0 = P.mark()
    ds_w = [P.dsem(f"sw{tag}{i}") for i in range(2)]
    ds_v = P.dsem("sv" + tag)
    ds_o = [P.dsem(f"so{tag}{i}") for i in range(4)]
    ds_o2 = [P.dsem(f"sp{tag}{i}") for i in range(4)]
    nt = NormT(P, C, g_pre, "s" + tag)
    hT = P.sb([128, 8, TOK], BF16)
    w_v = w_in.rearrange("(kc p) n -> p kc n", p=128)
    wv = P.sb([128, 8, D], BF16)
    t_wv = [P.dma("gpsimd", ds_v, wv[:, 0:4, :], w_v[:, 0:4, 2048:3072]),
            P.dma("gpsimd", ds_v, wv[:, 4:8, :], w_v[:, 4:8, 2048:3072])]
    stg_ring = Ring([P.sb([128, TOK], BF16) for _ in range(4)])
    ps_ring = Ring([P.ps([128, 512], F32) for _ in range(2)])
    order = list(range(8, 16)) + list(range(0, 8))
    wq_all = P.sb([128, 16, 8, 128], BF16)
    ds_wg = [P.dsem(f"swg{tag}{i}") for i in range(2)]
    tA = nt.run_many([(x_in[j * 128:(j + 1) * 128, :], j * 128) for j in range(NT)], hT)
    t_wq = {}
    hist = []
    for i, oc in enumerate(order):
        dep = [hist[i - 2]] if i >= 2 else []
        t_wq[oc] = P.dma("gpsimd", ds_wg[i % 2], wq_all[:, oc, :, :], w_v[:, :, oc * 128:(oc + 1) * 128], deps=dep)
        hist.append(t_wq[oc])
    tc_v, tc_k = [], []
    pv_ring = Ring([P.ps([128, 1024], F32) for _ in range(2)])
    vs_ring = Ring([P.sb([128, D], BF16) for _ in range(4)])
    vst = []
    for tt in range(NT):
        pv, d0 = pv_ring.get()
        t_m = None
        for nh in range(2):
            for k in range(8):
                t_m = P.op("tensor", lambda e, pv=pv, nh=nh, k=k, tt=tt: e.matmul(
                    pv[:, nh * 512:(nh + 1) * 512], lhsT=hT[:, k, tt * 128:(tt + 1) * 128],
                    rhs=wv[:, k, nh * 512:(nh + 1) * 512], start=(k == 0), stop=(k == 7)),
                    deps=t_wv + [tA[tt]] + list(d0), sig=(nh == 1 and k == 7))
        vs, d1 = vs_ring.get()
        t_c = P.op("vector", lambda e, pv=pv, vs=vs: e.tensor_copy(out=vs[:], in_=pv), deps=[t_m] + list(d1))
        pv_ring.used_by(t_c)
        t_st = P.dma("sync", ds_o2[vs_ring.cur], v_d[tt * 128:(tt + 1) * 128, :], vs[:], deps=[t_c])
        vs_ring.used_by(t_st)
        vst.append(t_st)
        if coll is not None and tt % 4 == 3:
            k4 = tt // 4
            tc_v.append(P.collective(v_d[k4 * 512:(k4 + 1) * 512, :], coll[1][k4 * 2048:(k4 + 1) * 2048, :], coll[2],
                                     deps=vst[-4:]))
    kst = []
    pend_coll = None
    for i, oc in enumerate(order):
        wb = wq_all[:, oc, :, :]
        t_w = t_wq[oc]
        if pend_coll is not None:
            k4, deps_ = pend_coll
            tc_k.append(P.collective(kT_d[k4 * 256:(k4 + 1) * 256, :], coll[0][k4 * 1024:(k4 + 1) * 1024, :], coll[2], deps=deps_))
            pend_coll = None
        stg, d1 = stg_ring.get()
        t_e = None
        t_m = None
        for tg in range(4):
            pp, d2 = ps_ring.get()
            for k in range(8):
                t_m = P.op("tensor", lambda e, pp=pp, wb=wb, k=k, tg=tg: e.matmul(
                    pp, lhsT=wb[:, k, :], rhs=hT[:, k, tg * 512:(tg + 1) * 512], start=(k == 0), stop=(k == 7)),
                    deps=[t_w] + tA + list(d2), sig=(k == 7))
            sc = 0.125 if oc < 8 else 1.0
            t_e = P.op("scalar", lambda e, pp=pp, stg=stg, tg=tg, sc=sc: e.mul(
                out=stg[:, tg * 512:(tg + 1) * 512], in_=pp, mul=sc), deps=[t_m] + list(d1))
            ps_ring.used_by(t_e)
        dst = (qT_d if oc < 8 else kT_d)[(oc % 8) * 128:(oc % 8 + 1) * 128, :]
        t_st = P.dma("sync", ds_o[stg_ring.cur], dst, stg[:], deps=[t_e])
        stg_ring.used_by(t_st)
        if oc >= 8:
            kst.append(t_st)
            if coll is not None and (oc - 8) % 2 == 1:
                pend_coll = ((oc - 8) // 2, kst[-2:])
    P.release(m0)
    return tc_k, tc_v


def key_schedule(L):
    out = []
    for kb in range(16 * L + 15, -1, -1):
        if kb >= 16 * L:
            j = (kb - 16 * L) // 4
            out.append((kb, 128 * j, (kb - 16 * L) % 4))
        else:
            out.append((kb, 0, None))
    return out


class Job:
    pass


def sbattn_phase(P, C, qT_d, KT_g, V_g, oT_d, tag, heads=range(16), chunked=False, kv_toks=None):
    m0 = P.mark()
    ds_k = [P.dsem(f"ak{tag}{i}") for i in range(2)]
    ds_o = [P.dsem(f"ao{tag}{i}") for i in range(2)]
    kT_bufs = [(P.sb([128, 4, TOK], BF16), P.sb([128, 4, TOK], BF16)) for _ in range(2)]
    t_kz = []
    for (ka_, kb_) in kT_bufs:
        t_kz.append(P.op("vector", lambda e, ka_=ka_: e.memset(ka_[64:128, :, :], 0.0)))
        t_kz.append(P.op("vector", lambda e, kb_=kb_: e.memset(kb_[0:64, :, :], 0.0)))
    kT_ring = Ring(kT_bufs)
    v_ring = Ring([P.sb([128, 4, 16, 128], BF16) for _ in range(2)])
    q_ring = Ring([P.sb([128, TOK], BF16) for _ in range(2)])
    E_ring = Ring([P.sb([128, 2, 512], BF16) for _ in range(2)])
    SP_ring = Ring([P.sb([128, 2, 512], BF16) for _ in range(3)])
    A_ring = Ring([P.sb([128, 2, 512], BF16) for _ in range(3)])
    Smid_ring = Ring([P.sb([128, 512], BF16) for _ in range(3)])
    Snx_ring = Ring([P.sb([128, 512], BF16) for _ in range(3)])
    S32 = P.sb([128, 512], F32)
    S32m = P.sb([128, 512], F32)
    ostg_ring = Ring([P.sb([128, TOK], BF16) for _ in range(2)])
    Z_ring = Ring([P.ps([128, 1024], F32).rearrange("p (b n) -> p b n", b=2) for _ in range(3)])
    O_ring = Ring([P.ps([128, 512], F32) for _ in range(2)])

    jobs = []
    pairs = sorted(set(h // 2 for h in heads))
    holders = []

    def load_pair(pi):
        if pi >= len(pairs):
            return
        hd = holders[pi]
        hp = pairs[pi]
        hd.kT, d0 = kT_ring.get()
        hd.v, d1 = v_ring.get()
        hd.q, d2 = q_ring.get()
        dsk = ds_k[pi % 2]
        if kv_toks is not None:
            d0 = list(d0) + [kv_toks[0][hp // 2]]
            d1 = list(d1) + list(kv_toks[1])
        if chunked:
            ksrc = KT_g[hp // 2, :, (hp % 2) * 128:(hp % 2) * 128 + 128, :].rearrange("r p t -> p r t")
        else:
            ksrc = KT_g[:, hp * 128:(hp + 1) * 128, :].rearrange("r p t -> p r t")
        lt = [P.dma("sync", dsk, hd.kT[0][0:64, :, :], ksrc[0:64], deps=list(d0) + t_kz),
              P.dma("sync", dsk, hd.kT[1][64:128, :, :], ksrc[64:128], deps=list(d0) + t_kz)]
        for r in range(4):
            if chunked:
                for k in range(4):
                    lt.append(P.dma("sync", dsk, hd.v[:, r, 4 * k:4 * k + 4, :],
                                    V_g[k, r].rearrange("(m p) c -> p m c", p=128)[:, :, hp * 128:(hp + 1) * 128],
                                    deps=list(d1)))
            else:
                lt.append(P.dma("sync", dsk, hd.v[:, r, :, :],
                                V_g[r].rearrange("(m p) c -> p m c", p=128)[:, :, hp * 128:(hp + 1) * 128], deps=list(d1)))
        lt.append(P.dma("sync", dsk, hd.q[:], qT_d[hp * 128:(hp + 1) * 128, :], deps=list(d2)))
        hd.loads = lt
        hd.pair_slots = (kT_ring.cur, v_ring.cur, q_ring.cur)

    for pi, hp in enumerate(pairs):
        hd = Job()
        holders.append(hd)
        hs = [h for h in heads if h // 2 == hp]
        cnt = 0
        for h in hs:
            for L in range(4):
                sched = key_schedule(L)
                st = Job()
                npair = len(sched) // 2
                for i in range(npair):
                    (kbA, c0, dlA), (kbB, c0b, dlB) = sched[2 * i], sched[2 * i + 1]
                    assert c0 == c0b
                    jb = Job()
                    jb.stream, jb.hd = st, hd
                    jb.h, jb.hh, jb.L, jb.c0 = h, h % 2, L, c0
                    jb.kb, jb.dl = (kbA, kbB), (dlA, dlB)
                    jb.first, jb.last = (i == 0), (i == npair - 1)
                    jb.next_c0 = sched[2 * i + 2][1] if not jb.last else None
                    jb.pair_last = jb.last and L == 3 and h == hs[-1]
                    jb.pair_first_head = (h == hs[0])
                    jb.prefetch = pi + 1 if cnt == 3 else None
                    cnt += 1
                    jobs.append(jb)
    load_pair(0)
    fin = []
    state = {"s32": None, "s32m": None, "ostg": None}

    def kq(jb, b):
        kb = jb.kb[b]
        ks = jb.hd.kT[jb.hh][:, kb % 4, (kb // 4) * 128:(kb // 4) * 128 + 128]
        qs = jb.hd.q[:, jb.L * 512 + jb.c0:(jb.L + 1) * 512]
        return ks, qs

    def chain(ops, sig_last, fresh=True, close=True):
        n_ = len(ops)
        tok = None
        for i_, (o, l, r, d) in enumerate(ops):
            tok = P.op("tensor", lambda e, o=o, l=l, r=r, i_=i_: e.matmul(
                o, lhsT=l, rhs=r, start=(fresh and i_ == 0), stop=(close and i_ == n_ - 1), skip_group_check=True),
                deps=d, sig=(sig_last and i_ == n_ - 1))
        return tok

    def zops(jb, dst, b, deps):
        c0 = jb.c0
        ks, qs = kq(jb, b)
        ops = [(dst[:, b, c0:512], ks, qs, deps)]
        if jb.dl[b] is not None:
            ops.append((dst[:, b, c0:c0 + 128], C.ident[:], C.mask_sb[:, jb.dl[b], :], []))
        return ops

    def st1_pe(jb):
        Zb, dz = Z_ring.get()
        jb.Z = Zb
        chain(zops(jb, Zb, 0, jb.hd.loads + list(dz) + C.ready), False, close=False)
        jb.tz = chain(zops(jb, Zb, 1, []), True, close=False)

    def st1_act_e(jb):
        c0 = jb.c0
        Eb, de = E_ring.get()
        jb.E = Eb
        Zb = jb.Z
        jb.te = P.op("scalar", lambda e: e.activation(out=Eb[:, :, c0:512], in_=Zb[:, :, c0:512], func=AF.Exp),
                     deps=[jb.tz] + list(de))
        jb.Z_k = Z_ring.cur

    def st1_act_sp(jb):
        c0 = jb.c0
        Eb = jb.E
        SPb, dsp = SP_ring.get()
        jb.SP, jb.SP_k = SPb, SP_ring.cur
        jb.tsp = P.op("scalar", lambda e: e.activation(out=SPb[:, :, c0:512], in_=Eb[:, :, c0:512], func=AF.Ln, bias=1.0),
                      deps=[jb.te] + list(dsp))
        E_ring.used_by(jb.tsp)
        st = jb.stream
        Sm, dsm = Smid_ring.get()
        jb.Smid, jb.Smid_k = Sm, Smid_ring.cur
        if jb.first:
            t_a = P.op("vector", lambda e: e.tensor_copy(out=S32m[:, c0:512], in_=SPb[:, 0, c0:512]),
                       deps=[jb.tsp, state["s32m"]])
            jb.tSmid = P.op("vector", lambda e: e.tensor_copy(out=Sm[:, c0:512], in_=SPb[:, 0, c0:512]),
                            deps=[jb.tsp] + list(dsm))
        else:
            t_a = P.op("vector", lambda e: e.tensor_tensor(out=S32m[:, c0:512], in0=S32[:, c0:512], in1=SPb[:, 0, c0:512],
                                                           op=ALU.add), deps=[jb.tsp, state["s32m"], st.t_b])
            jb.tSmid = P.op("vector", lambda e: e.tensor_tensor(out=Sm[:, c0:512], in0=S32[:, c0:512], in1=SPb[:, 0, c0:512],
                                                                op=ALU.add), deps=[jb.tsp, st.t_b] + list(dsm))
        SP_ring.used_by(jb.tSmid, k=jb.SP_k)
        if not jb.last:
            n0 = jb.next_c0
            t_z = None
            if jb.first and c0 > 0:
                t_z = P.op("vector", lambda e: e.memset(S32m[:, 0:c0], 0.0), deps=[state["s32m"]])
            t_b = P.op("vector", lambda e: e.tensor_tensor(out=S32[:, c0:512], in0=S32m[:, c0:512], in1=SPb[:, 1, c0:512],
                                                           op=ALU.add), deps=[t_a, state["s32"]])
            Sn, dsn = Snx_ring.get()
            t_cv = P.op("vector", lambda e: e.tensor_tensor(out=Sn[:, c0:512], in0=S32m[:, c0:512], in1=SPb[:, 1, c0:512],
                                                            op=ALU.add), deps=[t_a] + list(dsn))
            t_x = None
            if n0 < c0:
                P.op("vector", lambda e: e.memset(S32[:, n0:c0], 0.0), deps=[state["s32"]], sig=False)
                t_cv = P.op("vector", lambda e: e.memset(Sn[:, n0:c0], 0.0), deps=[t_cv])
            st.t_b = t_cv
            state["s32"] = t_cv
            state["s32m"] = t_cv
            SP_ring.used_by(t_cv, k=jb.SP_k)
            st.next_S = (Sn, Snx_ring.cur, t_cv)
        else:
            state["s32m"] = jb.tSmid

    def st2_pe(jb):
        c0 = jb.c0
        Tb = jb.Z
        jb.T = Tb
        SPb = jb.SP
        ops0 = [(Tb[:, 0, c0:512], C.negU[:], SPb[:, 0, c0:512], [jb.tsp, jb.te])]
        if not jb.first:
            Sb, Sk, tS = jb.Sprev
            ops0.append((Tb[:, 0, c0:512], C.negOnes[:], Sb[:, c0:512], [tS]))
        chain(ops0, False, fresh=False)
        ops1 = [(Tb[:, 1, c0:512], C.negU[:], SPb[:, 1, c0:512], []),
                (Tb[:, 1, c0:512], C.negOnes[:], jb.Smid[:, c0:512], [jb.tSmid])]
        jb.tT = chain(ops1, True, fresh=False)
        if not jb.first:
            Snx_ring.used_by(jb.tT, k=Sk)
        Smid_ring.used_by(jb.tT, k=jb.Smid_k)
        SP_ring.used_by(jb.tT, k=jb.SP_k)

    def st2_act(jb):
        c0 = jb.c0
        Tb = jb.T
        Ab, da = A_ring.get()
        jb.A, jb.A_k = Ab, A_ring.cur
        jb.tA = P.op("scalar", lambda e: e.activation(out=Ab[:, :, c0:512], in_=Tb[:, :, c0:512], func=AF.Exp),
                     deps=[jb.tT] + list(da))
        Z_ring.used_by(jb.tA, k=jb.Z_k)

    def st3(jb):
        c0 = jb.c0
        st = jb.stream
        if jb.first:
            st.O, dO = O_ring.get()
            st.O_k = O_ring.cur
            P.op("tensor", lambda e: e.matmul(st.O[:, 0:512], lhsT=C.zeroI[:], rhs=C.zeroK[:], start=True, stop=False),
                 deps=list(dO), sig=False)
        Ab = jb.A
        tav = None
        for b in range(2):
            kb = jb.kb[b]
            vb = jb.hd.v[:, kb % 4, kb // 4, :]
            tav = P.op("tensor", lambda e, b=b, vb=vb: e.matmul(st.O[:, c0:512], lhsT=vb, rhs=Ab[:, b, c0:512], start=False,
                                                                stop=(jb.last and b == 1), skip_group_check=True),
                       deps=[jb.tA], sig=(b == 1))
        A_ring.used_by(tav, k=jb.A_k)
        if jb.last:
            if jb.L == 0 and jb.pair_first_head:
                state["ostg"], state["ostg_d"] = ostg_ring.get()
                state["ostg_k"] = ostg_ring.cur
            og = state["ostg"]
            L = jb.L
            pr = slice(jb.hh * 64, jb.hh * 64 + 64)
            tcp = P.op("vector", lambda e: e.tensor_copy(out=og[pr, L * 512:(L + 1) * 512], in_=st.O[pr, 0:512]),
                       deps=[tav] + list(state["ostg_d"]))
            O_ring.used_by(tcp, k=st.O_k)
            if jb.L == 3:
                hp_ = jb.h // 2
                t_st = P.dma("sync", ds_o[state["ostg_k"]], oT_d[jb.h * 64:(jb.h + 1) * 64, :], og[pr, :], deps=[tcp])
                ostg_ring.used_by(t_st, k=state["ostg_k"])
                fin[:] = [t_st]
            if jb.pair_last:
                kk, vk, qk = jb.hd.pair_slots
                kT_ring.used_by(tav, k=kk)
                v_ring.used_by(tav, k=vk)
                q_ring.used_by(tav, k=qk)

    n = len(jobs)
    for i in range(-1, n + 1):
        nx = jobs[i + 1] if 0 <= i + 1 < n else None
        cur = jobs[i] if 0 <= i < n else None
        pv = jobs[i - 1] if 0 <= i - 1 < n else None
        if nx is not None:
            if not nx.first:
                nx.Sprev = nx.stream.next_S
            if nx.prefetch is not None:
                load_pair(nx.prefetch)
            st1_pe(nx)
        if cur is not None:
            st2_pe(cur)
        if pv is not None:
            st3(pv)
        if nx is not None:
            st1_act_e(nx)
            st1_act_sp(nx)
        if cur is not None:
            st2_act(cur)
    P.release(m0)
    return fin


MLA_SCALE = 1.0 / np.sqrt(96.0)
TWO_PI = 2.0 * np.pi
CW1 = 6.28125
CW2 = float(TWO_PI - 6.28125)


def mlaproj_phase(P, C, x_in, pos_d, w_in, qnorm_g, w_uq, kvnorm_g, w_ukv, g_pre, invf_d,
                  qaT_d, qnT_d, latT_d, knmax_d, tag):
    m0 = P.mark()
    ds = P.dsem("mp" + tag)
    ds_o = P.dsem("mo" + tag)
    nt = NormT(P, C, g_pre, "m" + tag, nps=1)
    hT = P.sb([128, 8, TOK], BF16)
    tA = nt.run_many([(x_in[j * 128:(j + 1) * 128, :], j * 128) for j in range(NT)], hT)
    win = P.sb([128, 8, 416], BF16)
    wuq = P.sb([128, 2, 1536], BF16)
    wuk = P.sb([128, 16, 64], BF16)
    t_w = [P.dma("gpsimd", ds, win[:], w_in.rearrange("(kc p) n -> p kc n", p=128)),
           P.dma("gpsimd", ds, wuq[:], w_uq.rearrange("(kc p) n -> p kc n", p=128)),
           P.dma("gpsimd", ds, wuk[:], w_ukv.rearrange("p (h c) -> p h c", c=128)[:, :, 0:64])]
    gq, t1 = bcast_vec(P, ds, qnorm_g, 256)
    gkv, t2 = bcast_vec(P, ds, kvnorm_g, 128)
    invf, t3 = bcast_vec(P, ds, invf_d, 16)
    posi = P.sb([128, NT], I32)
    t4 = P.dma("sync", ds, posi[:], pos_d.rearrange("(t p) -> p t", p=128), allow_slow_non_contiguous=True)
    t_c = [t1, t2, t3, t4] + t_w
    posf = P.sb([128, NT], F32)
    ang = P.sb([128, NT, 16], F32)
    ang2 = P.sb([128, NT, 16], F32)
    ni = P.sb([128, NT, 16], I32)
    nf = P.sb([128, NT, 16], F32)
    rr = P.sb([128, NT, 16], F32)
    gt = P.sb([128, NT, 16], F32)
    sint = P.sb([128, NT, 16], F32)
    cost = P.sb([128, NT, 16], F32)
    t = P.op("vector", lambda e: e.tensor_copy(out=posf[:], in_=posi[:]), deps=t_c + C.ready)
    t = P.op("vector", lambda e: e.tensor_tensor(out=ang[:], in0=posf[:].unsqueeze(2).to_broadcast([128, NT, 16]),
                                                 in1=invf[:].unsqueeze(1).to_broadcast([128, NT, 16]), op=ALU.mult), deps=[t])
    t = P.op("vector", lambda e: e.tensor_scalar(out=ang2[:], in0=ang[:], scalar1=float(np.pi / 2), scalar2=None,
                                                 op0=ALU.add), deps=[t])

    def sincos(a, out, t):
        t = P.op("vector", lambda e: e.tensor_scalar(out=ni[:], in0=a[:], scalar1=float(1.0 / TWO_PI), scalar2=None,
                                                     op0=ALU.mult), deps=[t])
        t = P.op("vector", lambda e: e.tensor_copy(out=nf[:], in_=ni[:]), deps=[t])
        t = P.op("vector", lambda e: e.scalar_tensor_tensor(out=rr[:], in0=nf[:], scalar=-CW1, in1=a[:],
                                                            op0=ALU.mult, op1=ALU.add), deps=[t])
        t = P.op("vector", lambda e: e.scalar_tensor_tensor(out=rr[:], in0=nf[:], scalar=-CW2, in1=rr[:],
                                                            op0=ALU.mult, op1=ALU.add), deps=[t])
        t = P.op("vector", lambda e: e.tensor_scalar(out=gt[:], in0=rr[:], scalar1=float(np.pi), scalar2=float(-TWO_PI),
                                                     op0=ALU.is_gt, op1=ALU.mult), deps=[t])
        t = P.op("vector", lambda e: e.tensor_tensor(out=rr[:], in0=rr[:], in1=gt[:], op=ALU.add), deps=[t])
        t = P.op("vector", lambda e: e.tensor_scalar(out=gt[:], in0=rr[:], scalar1=float(-np.pi), scalar2=float(TWO_PI),
                                                     op0=ALU.is_lt, op1=ALU.mult), deps=[t])
        t = P.op("vector", lambda e: e.tensor_tensor(out=rr[:], in0=rr[:], in1=gt[:], op=ALU.add), deps=[t])
        t = P.op("vector", lambda e: e.tensor_scalar(out=rr[:], in0=rr[:], scalar1=3.14159, scalar2=-3.14159,
                                                     op0=ALU.min, op1=ALU.max), deps=[t])
        t = P.op("scalar", lambda e: e.activation(out=out[:], in_=rr[:], func=AF.Sin), deps=[t])
        return t
    t = sincos(ang, sint, t)
    t_tab = sincos(ang2, cost, t)

    junk = P.sb([128, 1024], BF16)
    stat = P.sb([128, 8 * NT], F32)
    cqn = P.sb([128, 256], BF16)
    lat = P.sb([128, 160], BF16)
    kr = P.sb([128, 32], F32)
    kro = P.sb([128, 32], F32)
    tr = P.sb([128, 4, 16], F32)
    cqnT = [P.sb([128, 2, 128], BF16) for _ in range(2)]
    ckvT = [P.sb([128, 128], BF16) for _ in range(2)]
    latT_s = P.sb([128, TOK], BF16)
    krT_s = P.sb([32, TOK], BF16)
    qs = P.sb([128, 16, 96], F32)
    qt = P.sb([128, 4, 16, 16], F32)
    qsq = P.sb([128, 16, 96], F32)
    qn = P.sb([128, 16], F32)
    qa = P.sb([128, 16, 96], BF16)
    ksq = P.sb([128, 16, 64], F32)
    kn2 = P.sb([128, 16], F32)
    knmax = P.sb([128, 16], F32)
    qaT_s = P.sb([96, 16, 128], BF16)
    qnT_s = P.sb([16, TOK], F32)
    pj = P.ps([128, 512], F32)
    pT = P.ps([128, 1024], BF16)
    pq3 = P.ps([128, 1536], F32)
    pkn = P.ps([128, 1024], F32)
    pq = pq3[:, 0:1024].bitcast(BF16)
    pqn = pq3[:, 1024:1536]
    last = []
    def tileF(tt, prevF, prevB):
        tsl = slice(tt * 128, (tt + 1) * 128)
        sc = stat[:, 8 * tt:8 * tt + 8]
        t_p = None
        for k in range(8):
            t_p = P.op("tensor", lambda e, k=k: e.matmul(pj[:, 0:416], lhsT=hT[:, k, tsl], rhs=win[:, k, :],
                                                          start=(k == 0), stop=(k == 7)),
                       deps=tA + t_c + [prevF], sig=(k == 7))
        t_s1 = P.op("scalar", lambda e: e.activation(out=junk[:, 0:256], in_=pj[:, 0:256], func=AF.Square,
                                                     accum_out=sc[:, 0:1]), deps=[t_p, prevF])
        t_s2 = P.op("scalar", lambda e: e.activation(out=junk[:, 256:384], in_=pj[:, 256:384], func=AF.Square,
                                                     accum_out=sc[:, 2:3]), deps=[t_p, prevF])
        t_r1 = rstd_from_ss(P, C, sc[:, 0:1], sc[:, 1:2], 256, [t_s1])
        t_r2 = rstd_from_ss(P, C, sc[:, 2:3], sc[:, 3:4], 128, [t_s2])
        t_cq = P.op("vector", lambda e: e.scalar_tensor_tensor(out=cqn[:], in0=pj[:, 0:256], scalar=sc[:, 1:2],
                                                               in1=gq[:], op0=ALU.mult, op1=ALU.mult), deps=[t_r1, prevF])
        t_ck = P.op("vector", lambda e: e.scalar_tensor_tensor(out=lat[:, 0:128], in0=pj[:, 256:384], scalar=sc[:, 3:4],
                                                               in1=gkv[:], op0=ALU.mult, op1=ALU.mult), deps=[t_r2, prevF])
        t_kr = P.op("vector", lambda e: e.tensor_copy(out=kr[:], in_=pj[:, 384:416]), deps=[t_p, prevF])
        cs, sn = cost[:, tt, :], sint[:, tt, :]
        t_a = P.op("vector", lambda e: e.tensor_tensor(out=tr[:, 0, :], in0=kr[:, 0:16], in1=cs, op=ALU.mult), deps=[t_kr, t_tab])
        t_b = P.op("vector", lambda e: e.tensor_tensor(out=tr[:, 1, :], in0=kr[:, 16:32], in1=sn, op=ALU.mult), deps=[t_kr])
        t_c2 = P.op("vector", lambda e: e.tensor_tensor(out=tr[:, 2, :], in0=kr[:, 16:32], in1=cs, op=ALU.mult), deps=[t_kr])
        t_d = P.op("vector", lambda e: e.tensor_tensor(out=tr[:, 3, :], in0=kr[:, 0:16], in1=sn, op=ALU.mult), deps=[t_kr])
        t_o1 = P.op("vector", lambda e: e.tensor_tensor(out=kro[:, 0:16], in0=tr[:, 0, :], in1=tr[:, 1, :], op=ALU.subtract),
                    deps=[t_a, t_b])
        t_o2 = P.op("vector", lambda e: e.tensor_tensor(out=kro[:, 16:32], in0=tr[:, 2, :], in1=tr[:, 3, :], op=ALU.add),
                    deps=[t_c2, t_d])
        t_kl = P.op("vector", lambda e: e.tensor_copy(out=lat[:, 128:160], in_=kro[:]), deps=[t_o1, t_o2])
        t_r2k = P.op("scalar", lambda e: e.activation(out=junk[:, 384:416], in_=kro[:], func=AF.Square,
                                                      accum_out=sc[:, 4:5]), deps=[t_o1, t_o2])
        t_t = None
        for k in range(2):
            P.op("tensor", lambda e, k=k: e.transpose(out=pT[:, k * 128:(k + 1) * 128], in_=cqn[:, k * 128:(k + 1) * 128],
                                                      identity=C.ident[:]), deps=[t_cq, prevF], sig=False)
        P.op("tensor", lambda e: e.transpose(out=pT[:, 256:384], in_=lat[:, 0:128], identity=C.ident[:]),
             deps=[t_ck], sig=False)
        t_t = P.op("tensor", lambda e: e.transpose(out=pT[0:32, 384:512], in_=lat[:, 128:160], identity=C.ident[:]),
                   deps=[t_kl])
        t_x1 = P.op("scalar", lambda e: e.copy(out=cqnT[tt % 2][:], in_=pT[:, 0:256].rearrange("p (k t) -> p k t", k=2)), deps=[t_t, prevF, prevB])
        t_x2 = P.op("scalar", lambda e: e.copy(out=ckvT[tt % 2][:], in_=pT[:, 256:384]), deps=[t_t, prevF, prevB])
        t_x3 = P.op("scalar", lambda e: e.copy(out=latT_s[:, tsl], in_=pT[:, 256:384]), deps=[t_t])
        t_x4 = P.op("scalar", lambda e: e.copy(out=krT_s[:, tsl], in_=pT[0:32, 384:512]), deps=[t_t])
        return [t_x1, t_x2, t_x3, t_x4, t_r2k, t_kl, t_ck, t_cq]

    def tileB(tt, fF, prevB):
        tsl = slice(tt * 128, (tt + 1) * 128)
        sc = stat[:, 8 * tt:8 * tt + 8]
        cs, sn = cost[:, tt, :], sint[:, tt, :]
        prev = prevB
        t_q = None
        for n3 in range(3):
            for k in range(2):
                t_q = P.op("tensor", lambda e, n3=n3, k=k: e.matmul(pq3[:, n3 * 512:(n3 + 1) * 512], lhsT=cqnT[tt % 2][:, k, :],
                                                                    rhs=wuq[:, k, n3 * 512:(n3 + 1) * 512],
                                                                    start=(k == 0), stop=(k == 1)),
                           deps=[fF, prev], sig=(n3 == 2 and k == 1))
        t_kn = None
        for n2 in range(2):
            t_kn = P.op("tensor", lambda e, n2=n2: e.matmul(
                pkn[:, n2 * 512:(n2 + 1) * 512], lhsT=ckvT[tt % 2][:],
                rhs=wuk[:, n2 * 8:(n2 + 1) * 8, :], start=True, stop=True),
                deps=[fF, prev], sig=(n2 == 1))
        t_k1 = P.op("scalar", lambda e: e.activation(out=ksq[:], in_=pkn[:].rearrange("p (h c) -> p h c", c=64),
                                                     func=AF.Square), deps=[t_kn, prev])
        t_k2 = P.op("vector", lambda e: e.tensor_reduce(out=kn2[:], in_=ksq[:], axis=AX.X, op=ALU.add), deps=[t_k1, prev])
        if tt == 0:
            t_k3 = P.op("vector", lambda e: e.tensor_scalar(out=knmax[:], in0=kn2[:], scalar1=sc[:, 4:5], scalar2=None,
                                                            op0=ALU.add), deps=[t_k2, fF])
        else:
            t_k3 = P.op("vector", lambda e: e.scalar_tensor_tensor(out=knmax[:], in0=kn2[:], scalar=sc[:, 4:5], in1=knmax[:],
                                                                   op0=ALU.add, op1=ALU.max), deps=[t_k2, fF, prev])
        t_qs = P.op("scalar", lambda e: e.mul(out=qs[:], in_=pq3[:].rearrange("p (h c) -> p h c", c=96), mul=float(MLA_SCALE)),
                    deps=[t_q, prev])
        csb = cs.unsqueeze(1).to_broadcast([128, 16, 16])
        snb = sn.unsqueeze(1).to_broadcast([128, 16, 16])
        t_a = P.op("vector", lambda e: e.tensor_tensor(out=qt[:, 0, :, :], in0=qs[:, :, 64:80], in1=csb, op=ALU.mult), deps=[t_qs, t_tab, prev])
        t_b = P.op("vector", lambda e: e.tensor_tensor(out=qt[:, 1, :, :], in0=qs[:, :, 80:96], in1=snb, op=ALU.mult), deps=[t_qs])
        t_c2 = P.op("vector", lambda e: e.tensor_tensor(out=qt[:, 2, :, :], in0=qs[:, :, 80:96], in1=csb, op=ALU.mult), deps=[t_qs])
        t_d = P.op("vector", lambda e: e.tensor_tensor(out=qt[:, 3, :, :], in0=qs[:, :, 64:80], in1=snb, op=ALU.mult), deps=[t_qs])
        t_o1 = P.op("vector", lambda e: e.tensor_tensor(out=qs[:, :, 64:80], in0=qt[:, 0, :, :], in1=qt[:, 1, :, :], op=ALU.subtract),
                    deps=[t_a, t_b, t_c2, t_d])
        t_o2 = P.op("vector", lambda e: e.tensor_tensor(out=qs[:, :, 80:96], in0=qt[:, 2, :, :], in1=qt[:, 3, :, :], op=ALU.add),
                    deps=[t_o1])
        t_sq = P.op("vector", lambda e: e.tensor_tensor(out=qsq[:], in0=qs[:], in1=qs[:], op=ALU.mult), deps=[t_o1, t_o2, prev])
        t_n2 = P.op("vector", lambda e: e.tensor_reduce(out=qn[:], in_=qsq[:], axis=AX.X, op=ALU.add), deps=[t_sq])
        t_n = P.op("gpsimd", lambda e: e.tensor_tensor(out=qn[:], in0=qn[:], in1=C.phalf[:].to_broadcast([128, 16]), op=ALU.pow),
                   deps=[t_n2])
        t_qa = P.op("scalar", lambda e: e.copy(out=qa[:], in_=qs[:]), deps=[t_o1, t_o2, prev])
        t_tq = None
        for h in range(16):
            t_tq = P.op("tensor", lambda e, h=h: e.transpose(out=pq[0:96, h * 128:(h + 1) * 128], in_=qa[:, h, :],
                                                             identity=C.ident[:]), deps=[t_qa, t_qs], sig=(h == 15))
        t_tn = P.op("tensor", lambda e: e.transpose(out=pqn[0:16, 0:128], in_=qn[:], identity=C.identf[:]), deps=[t_n, t_qs])
        t_y1 = P.op("vector", lambda e: e.tensor_copy(out=qaT_s[:], in_=pq[0:96, :].rearrange("p (h t) -> p h t", h=16)),
                    deps=[t_tq, prev])
        t_y2 = P.op("vector", lambda e: e.tensor_copy(out=qnT_s[:, tsl], in_=pqn[0:16, 0:128]), deps=[t_tn])
        t_st = P.dma("sync", ds_o, qaT_d[:, :, tsl].rearrange("h d t -> d h t"), qaT_s[:], deps=[t_y1])
        return [t_st, t_y2, t_y1, t_k3, t_n, t_k1, t_tq, t_tn, t_sq]


    fF = {}
    prevF, prevB = [t_tab], [t_tab]
    for i in range(NT + 1):
        if i < NT:
            fF[i] = tileF(i, prevF, prevB)
            prevF = fF[i]
        if i >= 1:
            prevB = tileB(i - 1, fF[i - 1], prevB)
    prev = prevB
    last = [P.dma("sync", ds_o, latT_d[0:128, :], latT_s[:], deps=[prev]),
            P.dma("sync", ds_o, latT_d[128:160, :], krT_s[:], deps=[prev]),
            P.dma("sync", ds_o, qnT_d[:, :], qnT_s[:], deps=[prev]),
            P.dma("sync", ds_o, knmax_d[:, :], knmax[:], deps=[prev])]
    P.release(m0)
    return last[-1:]


def mlaattn_phase(P, C, qaT_d, qnT_d, LatT_g, KN_g, w_ukv, ones_d, oT_d, tag, heads=range(16)):
    m0 = P.mark()
    ds = P.dsem("la" + tag)
    ds_q = [P.dsem(f"lq{tag}{i}") for i in range(2)]
    ds_o = [P.dsem(f"lo{tag}{i}") for i in range(2)]
    ckvnT = P.sb([128, 4, TOK], BF16)
    wkv = P.sb([128, 2048], BF16)
    kn = P.sb([128, 4, 16], F32)
    qn = P.sb([16, TOK], F32)
    mrow = P.sb([16, TOK], BF16)
    kmax = P.sb([16, 2], F32)
    ka_bufs = [P.sb([97, 4, TOK], BF16) for _ in range(2)]
    t0 = [P.dma("sync", ds, ckvnT[:], LatT_g[:, 0:128, :].rearrange("r p t -> p r t")),
          P.dma("gpsimd", ds, wkv[:], w_ukv[:, :]),
          P.dma("sync", ds, kn[:], KN_g.rearrange("r p h -> p r h")),
          P.dma("sync", ds, qn[:], qnT_d[:, :])]
    for kb_ in ka_bufs:
        t0.append(P.dma("sync", ds, kb_[64:96, :, :], LatT_g[:, 128:160, :].rearrange("r p t -> p r t")))
        t0.append(P.dma("sync", ds, kb_[96:97, :, :], ones_d.rearrange("(o r t) -> o r t", o=1, r=4)))
    ka_ring = Ring(ka_bufs)
    v_bufs = [P.sb([128, 4, 16, 128], BF16) for _ in range(2)]
    for vb_ in v_bufs:
        t0.append(P.op("vector", lambda e, vb_=vb_: e.memset(vb_[:, :, :, 64:128], 1.0)))
    v_ring = Ring(v_bufs)
    q_ring = Ring([P.sb([97, TOK], BF16) for _ in range(2)])
    Pm_ring = Ring([P.sb([128, 2, 512], BF16) for _ in range(3)])
    rl_ring = Ring([P.sb([128, 512], F32) for _ in range(2)])
    rls_ring = Ring([P.sb([64, 512], F32) for _ in range(2)])
    ds_r = [P.dsem(f"lr{tag}{i}") for i in range(2)]
    ostg_ring = Ring([P.sb([64, TOK], BF16) for _ in range(2)])
    Sc_ring = Ring([P.ps([128, 1024], F32).rearrange("p (b n) -> p b n", b=2) for _ in range(2)])
    O_ring = Ring([P.ps([128, 512], F32) for _ in range(2)])
    G_ring = Ring([P.ps([128, 512], F32) for _ in range(2)])

    pk, _ = G_ring.get()
    t_k = None
    for r in range(4):
        t_k = P.op("tensor", lambda e, r=r: e.transpose(out=pk[0:16, r * 128:(r + 1) * 128], in_=kn[:, r, :],
                                                        identity=C.identf[:]), deps=t0 + C.ready, sig=(r == 3))
    t_k = P.op("vector", lambda e: e.tensor_reduce(out=kmax[:, 0:1], in_=pk[0:16, :], axis=AX.X, op=ALU.max), deps=[t_k])
    G_ring.used_by(t_k)
    t_k = P.op("gpsimd", lambda e: e.tensor_tensor(out=kmax[:, 1:2], in0=kmax[:, 0:1], in1=C.phalf[0:16, :], op=ALU.pow),
               deps=[t_k] + C.ready)
    t_m = P.op("vector", lambda e: e.tensor_scalar(out=mrow[:], in0=qn[:], scalar1=kmax[:, 1:2], scalar2=-1.0,
                                                   op0=ALU.mult, op1=ALU.mult), deps=[t_k] + t0)

    heads = list(heads)

    def gen(h):
        hd = Job()
        hd.h = h
        qa, dq = q_ring.get()
        hd.q_k = q_ring.cur
        dsq = ds_q[heads.index(h) % 2]
        hd.loads = [P.dma("sync", dsq, qa[0:96, :], qaT_d[h], deps=list(dq)),
                    P.dma("sync", dsq, qa[96:97, :], mrow[h:h + 1, :], deps=list(dq) + [t_m])]
        hd.qa = qa
        ka, dk = ka_ring.get()
        hd.ka_k = ka_ring.cur
        hd.ka = ka
        tg_ = []
        for r in range(4):
            for tg in range(4):
                pg, dg = G_ring.get()
                tm = P.op("tensor", lambda e, pg=pg, r=r, tg=tg: e.matmul(
                    pg[:, :], lhsT=wkv[:, h * 128:h * 128 + 128], rhs=ckvnT[:, r, tg * 512:(tg + 1) * 512],
                    start=True, stop=True), deps=t0 + list(dg))
                tcp = P.op("vector", lambda e, pg=pg, r=r, tg=tg: e.tensor_copy(
                    out=ka[0:64, r, tg * 512:(tg + 1) * 512], in_=pg[0:64, :]), deps=[tm] + list(dk))
                G_ring.used_by(tcp)
                tg_.append(tcp)
        v, dv = v_ring.get()
        hd.v_k = v_ring.cur
        hd.v = v
        for r in range(4):
            for mg in range(2):
                pg, dg = G_ring.get()
                tm = None
                for i in range(8):
                    blk = mg * 8 + i
                    tm = P.op("tensor", lambda e, pg=pg, r=r, i=i, blk=blk: e.matmul(
                        pg[:, i * 64:(i + 1) * 64], lhsT=ckvnT[:, r, blk * 128:(blk + 1) * 128],
                        rhs=wkv[:, h * 128 + 64:h * 128 + 128], start=True, stop=True),
                        deps=t0 + list(dg), sig=(i == 7))
                tcp = P.op("vector", lambda e, pg=pg, r=r, mg=mg: e.tensor_copy(
                    out=v[:, r, mg * 8:(mg + 1) * 8, 0:64], in_=pg[:].rearrange("p (i c) -> p i c", c=64)),
                    deps=[tm] + list(dv))
                G_ring.used_by(tcp)
                tg_.append(tcp)
        hd.ready = hd.loads + tg_[-1:] + t0
        return hd

    fin = []
    state = {}

    def st1(jb):
        c0, hd = jb.c0, jb.hd
        Sb, dsb = Sc_ring.get()
        qs = hd.qa[:, jb.L * 512 + c0:(jb.L + 1) * 512]
        tz = None
        for b_ in range(2):
            kb = jb.kb[b_]
            ks = hd.ka[:, kb % 4, (kb // 4) * 128:(kb // 4) * 128 + 128]
            dl = jb.dl[b_]
            tz = P.op("tensor", lambda e, b_=b_, ks=ks: e.matmul(Sb[:, b_, c0:512], lhsT=ks, rhs=qs, start=True, stop=(dl is None)),
                      deps=(hd.ready + list(dsb)) if b_ == 0 else [], sig=(b_ == 1 and dl is None))
            if dl is not None:
                tz = P.op("tensor", lambda e, b_=b_, dl=dl: e.matmul(Sb[:, b_, c0:c0 + 128], lhsT=C.ident[:], rhs=C.mask_mla[:, dl, :],
                                                                  start=False, stop=True, skip_group_check=True), sig=(b_ == 1))
        Pb, dp = Pm_ring.get()
        jb.Pm, jb.Pm_k = Pb, Pm_ring.cur
        jb.tP = P.op("scalar", lambda e: e.activation(out=Pb[:, :, c0:512], in_=Sb[:, :, c0:512], func=AF.Exp),
                     deps=[tz] + list(dp))
        Sc_ring.used_by(jb.tP)

    def st2(jb):
        c0, hd, st = jb.c0, jb.hd, jb.stream
        if jb.first:
            st.O, dO = O_ring.get()
            st.O_k = O_ring.cur
            P.op("tensor", lambda e: e.matmul(st.O[:, 0:512], lhsT=C.zeroI[:], rhs=C.zeroK[:], start=True, stop=False),
                 deps=list(dO), sig=False)
        Pb = jb.Pm
        tav = None
        for b_ in range(2):
            kb = jb.kb[b_]
            vb = hd.v[:, kb % 4, kb // 4, :]
            tav = P.op("tensor", lambda e, b_=b_, vb=vb: e.matmul(st.O[:, c0:512], lhsT=vb, rhs=Pb[:, b_, c0:512], start=False,
                                                                  stop=(jb.last and b_ == 1), skip_group_check=True),
                       deps=[jb.tP], sig=(b_ == 1))
        Pm_ring.used_by(tav, k=jb.Pm_k)
        if jb.last:
            L = jb.L
            if L == 0:
                state["ostg"], state["ostg_d"] = ostg_ring.get()
                state["ostg_k"] = ostg_ring.cur
            og = state["ostg"]
            rl, drl = rl_ring.get()
            t_r = P.op("vector", lambda e: e.reciprocal(out=rl[64:128, :], in_=st.O[64:128, 0:512]), deps=[tav] + list(drl))
            rls, drs = rls_ring.get()
            t_sh = P.dma("sync", ds_r[rls_ring.cur], rls[:, :], rl[64:128, :], deps=[t_r] + list(drs))
            rl_ring.used_by(t_sh)
            t_o = P.op("vector", lambda e: e.tensor_tensor(out=og[:, L * 512:(L + 1) * 512], in0=st.O[0:64, 0:512], in1=rls[:, :],
                                                           op=ALU.mult), deps=[t_sh] + list(state["ostg_d"]))
            rls_ring.used_by(t_o)
            O_ring.used_by(t_o, k=st.O_k)
            if L == 3:
                t_st = P.dma("sync", ds_o[state["ostg_k"]], oT_d[hd.h * 64:(hd.h + 1) * 64, :], og[:], deps=[t_o])
                ostg_ring.used_by(t_st, k=state["ostg_k"])
                fin[:] = [t_st]
                ka_ring.used_by(tav, k=hd.ka_k)
                v_ring.used_by(tav, k=hd.v_k)
                q_ring.used_by(tav, k=hd.q_k)

    def head_jobs(hd):
        out = []
        for L in range(4):
            sched = key_schedule(L)
            st = Job()
            npair = len(sched) // 2
            for i in range(npair):
                (kbA, c0, dlA), (kbB, c0b, dlB) = sched[2 * i], sched[2 * i + 1]
                jb = Job()
                jb.stream, jb.hd, jb.L, jb.c0 = st, hd, L, c0
                jb.kb, jb.dl = (kbA, kbB), (dlA, dlB)
                jb.first, jb.last = (i == 0), (i == npair - 1)
                out.append(jb)
        return out

    hds = {}
    hds[heads[0]] = gen(heads[0])
    prev_job = None
    for hi, h in enumerate(heads):
        for ji, jb in enumerate(head_jobs(hds[h])):
            if ji == 2 and hi + 1 < len(heads):
                hds[heads[hi + 1]] = gen(heads[hi + 1])
            st1(jb)
            if prev_job is not None:
                st2(prev_job)
            prev_job = jb
    st2(prev_job)
    P.release(m0)
    return fin


CONST_SPECS = [("c_ident", [128, 128], BF16), ("c_identf", [128, 128], F32), ("c_negU", [128, 128], BF16),
               ("c_mask_sb", [4, 128, 128], BF16), ("c_mask_mla", [4, 128, 128], BF16),
               ("c_ones", [8192], BF16), ("c_invf", [16], F32)]
WEIGHT_SPECS = [("norm_g", [2, 6, D], F32), ("ffn_w_in", [2, 2, D, 2 * DFF], F32), ("ffn_w_out", [2, 2, DFF, D], F32),
                ("sb_w_in", [1, D, 3 * D], F32), ("sb_w_out", [1, D, D], F32), ("mla_w_in", [1, D, 416], F32),
                ("mla_q_norm", [1, 256], F32), ("mla_w_uq", [1, 256, 1536], F32), ("mla_kv_norm", [1, 128], F32),
                ("mla_w_ukv", [1, 128, 2048], F32), ("mla_w_out", [1, D, D], F32)]
ACT_SPECS = {"qT": ([D, TOK], BF16), "kT": ([D, TOK], BF16), "v": ([TOK, D], BF16),
             "KT_g": ([4, D, TOK], BF16), "V_g": ([4, TOK, D], BF16),
             "qaT": ([16, 96, TOK], BF16), "qnT": ([16, TOK], F32), "latT": ([160, TOK], BF16), "knmax": ([128, 16], F32),
             "LatT_g": ([4, 160, TOK], BF16), "KN_g": ([4, 128, 16], F32),
             "x": ([TOK, D], F32), "pos": ([TOK], I32), "x1": ([TOK, D], F32), "x4": ([TOK, D], F32), "y": ([TOK, D], F32)}


def build_program(stage, debug=False):
    nc = bass.Bass("TRN2", target_bir_lowering=False)
    dr = {}

    def ext(name, kind):
        shp, dt = ACT_SPECS[name]
        dr[name] = nc.dram_tensor(name, shp, dt, kind=kind).ap()

    CC_NAMES = ("kT", "v", "KT_g", "V_g", "latT", "knmax", "LatT_g", "KN_g")

    def internal(name, shp, dt):
        if debug and name not in CC_NAMES:
            dr[name] = nc.dram_tensor(name, shp, dt, kind="ExternalOutput").ap()
        else:
            dr[name] = nc.dram_tensor(name, shp, dt).ap()

    for n, shp, dt in CONST_SPECS + WEIGHT_SPECS:
        dr[n] = nc.dram_tensor(n, shp, dt, kind="ExternalInput").ap()
    ins = {0: ["x", "pos"], 1: ["x"], 2: ["x1", "qT", "KT_g", "V_g", "pos"], 3: ["x4", "qaT", "qnT", "LatT_g", "KN_g"]}[stage]
    outs = {0: ["y"], 1: ["x1", "qT", "kT", "v"], 2: ["x4", "qaT", "qnT", "latT", "knmax"], 3: ["y"]}[stage]
    for n in ins:
        ext(n, "ExternalInput")
    for n in outs:
        ext(n, "ExternalOutput")
    g = dr["norm_g"]
    with ExitStack() as es:
        P = Plan(nc, es)
        C = load_consts(P, dr)
        if stage == 0:
            GR = [[0, 1, 2, 3], [4, 5, 6, 7]]
            for n in ("xa", "xb", "xc", "xd", "xe"):
                internal(n, [TOK, D], F32)
            for n in ("qT", "kT", "v", "qaT", "qnT", "latT", "knmax"):
                internal(n, *ACT_SPECS[n])
            internal("oT", [D, TOK], BF16)
            internal("oT2", [D, TOK], BF16)
            internal("KT_g", [4 * D, TOK], BF16)
            internal("V_g", [4 * TOK, D], BF16)
            internal("LatT_g", [4 * 160, TOK], BF16)
            internal("KN_g", [4 * 128, 16], F32)
            KT_g = dr["KT_g"].rearrange("(k r p) t -> k r p t", k=4, r=4)
            V_g = dr["V_g"].rearrange("(k r p) t -> k r p t", k=4, r=4)
            LatT_g = dr["LatT_g"].rearrange("(r p) t -> r p t", r=4)
            KN_g = dr["KN_g"].rearrange("(r p) t -> r p t", r=4)
            ffn_phase(P, C, dr["x"], dr["xa"], dr["ffn_w_in"][0, 0], dr["ffn_w_out"][0, 0], g[0, 0], g[0, 1], "a")
            P.barrier()
            tc_k, tc_v = sbproj_phase(P, C, dr["xa"], dr["sb_w_in"][0], g[0, 2], dr["qT"], dr["kT"], dr["v"], "a",
                                      coll=(dr["KT_g"], dr["V_g"], GR))
            P.barrier(exclude_cc=True)
            sbattn_phase(P, C, dr["qT"], KT_g, V_g, dr["oT"], "a", chunked=True, kv_toks=(tc_k, tc_v))
            P.barrier()
            oproj_phase(P, C, dr["oT"], dr["sb_w_out"][0], g[0, 3], dr["xa"], dr["xb"], "a")
            P.barrier()
            ffn_phase(P, C, dr["xb"], dr["xc"], dr["ffn_w_in"][0, 1], dr["ffn_w_out"][0, 1], g[0, 4], g[0, 5], "b")
            P.barrier()
            ffn_phase(P, C, dr["xc"], dr["xd"], dr["ffn_w_in"][1, 0], dr["ffn_w_out"][1, 0], g[1, 0], g[1, 1], "c")
            P.barrier()
            mlaproj_phase(P, C, dr["xd"], dr["pos"], dr["mla_w_in"][0], dr["mla_q_norm"][0], dr["mla_w_uq"][0],
                          dr["mla_kv_norm"][0], dr["mla_w_ukv"][0], g[1, 2], dr["c_invf"],
                          dr["qaT"], dr["qnT"], dr["latT"], dr["knmax"], "a")
            P.barrier()
            t1 = P.collective(dr["latT"], dr["LatT_g"], GR)
            t2 = P.collective(dr["knmax"], dr["KN_g"], GR)
            P.barrier([t1, t2])
            mlaattn_phase(P, C, dr["qaT"], dr["qnT"], LatT_g, KN_g, dr["mla_w_ukv"][0], dr["c_ones"], dr["oT2"], "a")
            P.barrier()
            oproj_phase(P, C, dr["oT2"], dr["mla_w_out"][0], g[1, 3], dr["xd"], dr["xe"], "b")
            P.barrier()
            ffn_phase(P, C, dr["xe"], dr["y"], dr["ffn_w_in"][1, 1], dr["ffn_w_out"][1, 1], g[1, 4], g[1, 5], "d")
        elif stage == 1:
            ffn_phase(P, C, dr["x"], dr["x1"], dr["ffn_w_in"][0, 0], dr["ffn_w_out"][0, 0], g[0, 0], g[0, 1], "a")
            P.barrier()
            sbproj_phase(P, C, dr["x1"], dr["sb_w_in"][0], g[0, 2], dr["qT"], dr["kT"], dr["v"], "a")
        elif stage == 2:
            internal("oT", [D, TOK], BF16)
            internal("x2", [TOK, D], F32)
            internal("x3", [TOK, D], F32)
            sbattn_phase(P, C, dr["qT"], dr["KT_g"], dr["V_g"], dr["oT"], "a")
            P.barrier()
            oproj_phase(P, C, dr["oT"], dr["sb_w_out"][0], g[0, 3], dr["x1"], dr["x2"], "a")
            P.barrier()
            ffn_phase(P, C, dr["x2"], dr["x3"], dr["ffn_w_in"][0, 1], dr["ffn_w_out"][0, 1], g[0, 4], g[0, 5], "b")
            P.barrier()
            ffn_phase(P, C, dr["x3"], dr["x4"], dr["ffn_w_in"][1, 0], dr["ffn_w_out"][1, 0], g[1, 0], g[1, 1], "c")
            P.barrier()
            mlaproj_phase(P, C, dr["x4"], dr["pos"], dr["mla_w_in"][0], dr["mla_q_norm"][0], dr["mla_w_uq"][0],
                          dr["mla_kv_norm"][0], dr["mla_w_ukv"][0], g[1, 2], dr["c_invf"],
                          dr["qaT"], dr["qnT"], dr["latT"], dr["knmax"], "a")
        elif stage == 3:
            internal("oT", [D, TOK], BF16)
            internal("x5", [TOK, D], F32)
            mlaattn_phase(P, C, dr["qaT"], dr["qnT"], dr["LatT_g"], dr["KN_g"], dr["mla_w_ukv"][0], dr["c_ones"],
                          dr["oT"], "a")
            P.barrier()
            oproj_phase(P, C, dr["oT"], dr["mla_w_out"][0], g[1, 3], dr["x4"], dr["x5"], "b")
            P.barrier()
            ffn_phase(P, C, dr["x5"], dr["y"], dr["ffn_w_in"][1, 1], dr["ffn_w_out"][1, 1], g[1, 4], g[1, 5], "d")
        if debug and stage == 0:
            P.barrier()
            dsd = P.dsem("dbg")
            for n in CC_NAMES:
                src = dr[n]
                o = nc.dram_tensor("dbg_" + n, list(src.shape), src.dtype, kind="ExternalOutput").ap()
                R = src.shape[0]
                step = max(1, R // 8)
                for r0 in range(0, R, step):
                    P.dma("sync", dsd, o[r0:r0 + step, :], src[r0:r0 + step, :])
        P.emit([(d["sem"], d["val"]) for d in P.dsems if d["val"] > 0])
    return nc


def host_consts(cp):
    s = np.arange(128)[:, None]
    t = np.arange(128)[None, :]
    negU = np.where(s >= t, -1.0, 0.0)
    msb = np.zeros((4, 128, 128), np.float32)
    mml = np.zeros((4, 128, 128), np.float32)
    for d in range(4):
        if d == cp:
            msb[d] = np.where(s >= t, NEG, 0.0)
            mml[d] = np.where(s > t, NEG, 0.0)
        elif d > cp:
            msb[d] = NEG
            mml[d] = NEG
    bf = ml_dtypes.bfloat16
    invf = (10000.0 ** (-np.arange(0, 32, 2, dtype=np.float32) / np.float32(32))).astype(np.float32)
    return {"c_ident": np.eye(128, dtype=bf), "c_identf": np.eye(128, dtype=np.float32), "c_negU": negU.astype(bf),
            "c_mask_sb": msb.astype(bf), "c_mask_mla": mml.astype(bf), "c_ones": np.ones(8192, dtype=bf),
            "c_invf": invf}


_PROGS = {}


def _prog(stage):
    if stage not in _PROGS:
        _PROGS[stage] = build_program(stage)
    return _PROGS[stage]


def kernel(x, positions, norm_g, ffn_w_in, ffn_w_out, sb_w_in, sb_w_out, mla_w_in, mla_q_norm, mla_w_uq,
           mla_kv_norm, mla_w_ukv, mla_w_out):
    ncore = 8
    W = {"norm_g": norm_g, "ffn_w_in": ffn_w_in, "ffn_w_out": ffn_w_out, "sb_w_in": sb_w_in, "sb_w_out": sb_w_out,
         "mla_w_in": mla_w_in, "mla_q_norm": mla_q_norm, "mla_w_uq": mla_w_uq, "mla_kv_norm": mla_kv_norm,
         "mla_w_ukv": mla_w_ukv, "mla_w_out": mla_w_out}
    W = {k: np.ascontiguousarray(np.asarray(v, dtype=np.float32)) for k, v in W.items()}
    x = np.asarray(x, dtype=np.float32)
    positions = np.asarray(positions, dtype=np.int32)
    base = []
    for c in range(ncore):
        b, cp = c // 4, c % 4
        m = dict(W)
        m.update(host_consts(cp))
        base.append(m)
    xs = x.reshape(2, 16, 4, 128, D)
    ps = positions.reshape(2, 16, 4, 128)
    ids = list(range(ncore))

    maps = [dict(base[c], x=np.ascontiguousarray(xs[c // 4, :, c % 4]).reshape(TOK, D),
                 pos=np.ascontiguousarray(ps[c // 4, :, c % 4]).reshape(TOK)) for c in range(ncore)]
    r3 = run_bass_kernel_spmd(_prog(0), maps, core_ids=ids).results
    out = np.empty((2, 16, 4, 128, D), np.float32)
    for c in range(ncore):
        out[c // 4, :, c % 4] = r3[c]["y"].reshape(16, 128, D)
    return out.reshape(2, 8192, D)
```

```python
import numpy as np
import ml_dtypes
from contextlib import ExitStack
import concourse.bass as bass
import concourse.mybir as mybir
from concourse.bass_utils import run_bass_kernel_spmd

F32 = mybir.dt.float32
BF16 = mybir.dt.bfloat16
I32 = mybir.dt.int32
AF = mybir.ActivationFunctionType
ALU = mybir.AluOpType
AX = mybir.AxisListType

D = 1024
DFF = 2816
NT = 16
TOK = 2048
EPS = 1e-6
NEG = -30000.0
U8 = mybir.dt.uint8
SB_BYTES = 204 * 1024
DT_SIZE = {F32: 4, BF16: 2, I32: 4, U8: 1}


class Plan:
    ENGS = ("tensor", "vector", "scalar", "gpsimd", "sync")

    def __init__(self, nc, es):
        self.nc = nc
        self.es = es
        self.ops = {e: [] for e in self.ENGS}
        self.sem = {e: es.enter_context(nc.semaphore("s_" + e)) for e in self.ENGS}
        self.cnt = {e: 0 for e in self.ENGS}
        self.seen = {e: {} for e in self.ENGS}
        self.dsems = []
        self.dfree = []
        self.dlive = []
        self.nsb = 0
        self.pending = {e: [] for e in self.ENGS}
        self._init_mem()

    def _init_mem(self):
        self.arena = self.es.enter_context(self.nc.sbuf_tensor("arena", [128, SB_BYTES], U8))
        self.psum = self.es.enter_context(self.nc.psum_tensor("psum", [128, 4096], F32))
        self.sb_off = 0
        self.ps_off = 0

    def sb(self, shape, dtype, name=None):
        esz = DT_SIZE[dtype]
        n = 1
        for d in shape[1:]:
            n *= d
        nb = (n * esz + 63) // 64 * 64
        assert self.sb_off + nb <= SB_BYTES, f"SBUF arena overflow {self.sb_off}+{nb}"
        v = self.arena[0:shape[0], self.sb_off:self.sb_off + n * esz]
        self.sb_off += nb
        if dtype != U8:
            v = v.bitcast(dtype)
        if len(shape) == 3:
            v = v.rearrange("p (a b) -> p a b", a=shape[1])
        elif len(shape) == 4:
            v = v.rearrange("p (a b c) -> p a b c", a=shape[1], b=shape[2])
        return v

    def ps(self, shape, dtype=F32, name=None):
        esz = DT_SIZE[dtype]
        nb = shape[1] * esz
        nbank = (nb + 2047) // 2048
        assert self.ps_off + nbank <= 8, "PSUM overflow"
        c0 = self.ps_off * 512
        self.ps_off += nbank
        v = self.psum[0:shape[0], c0:c0 + nb // 4]
        if dtype != F32:
            v = v.bitcast(dtype)
        return v

    def mark(self):
        return (self.sb_off, self.ps_off, len(self.dlive))

    def release(self, m):
        self.sb_off, self.ps_off, nd = m
        self.dfree.extend(self.dlive[nd:])
        del self.dlive[nd:]

    def barrier(self, extra=(), exclude_cc=False):
        toks = [(self.sem[e], self.cnt[e]) for e in self.ENGS if self.cnt[e] > 0]
        toks += [(d["sem"], d["val"]) for d in self.dsems if d["val"] > 0 and not (exclude_cc and d.get("cc"))]
        toks += list(extra)
        for e in self.ENGS:
            self.pending[e] = self.pending[e] + self._waits(e, toks)

    def dsem(self, name):
        if self.dfree:
            st = self.dfree.pop()
        else:
            s = self.es.enter_context(self.nc.semaphore(f"d{len(self.dsems)}_{name}"))
            st = {"sem": s, "val": 0}
            self.dsems.append(st)
        self.dlive.append(st)
        return st

    def _waits(self, eng, deps):
        w = []
        best = {}
        for d in deps:
            if d is None:
                continue
            if isinstance(d, (list, tuple)) and d and isinstance(d[0], (list, tuple)):
                for dd in d:
                    if dd is not None:
                        k = id(dd[0])
                        if k not in best or best[k][1] < dd[1]:
                            best[k] = dd
                continue
            k = id(d[0])
            if k not in best or best[k][1] < d[1]:
                best[k] = d
        for k, (s, v) in best.items():
            if self.seen[eng].get(k, 0) < v:
                self.seen[eng][k] = v
                w.append((s, v))
        return w

    def op(self, eng, fn, deps=(), sig=True):
        w = self.pending[eng] + self._waits(eng, deps)
        self.pending[eng] = []
        tok = None
        if sig:
            self.cnt[eng] += 1
            tok = (self.sem[eng], self.cnt[eng])
        self.ops[eng].append((w, fn, self.sem[eng] if sig else None, 1))
        return tok

    def dma(self, eng, ds, out, in_, deps=(), **kw):
        w = self.pending[eng] + self._waits(eng, deps)
        self.pending[eng] = []
        ds["val"] += 16
        tok = (ds["sem"], ds["val"])
        self.ops[eng].append((w, lambda e: e.dma_start(out=out, in_=in_, **kw), ds["sem"], 16))
        return tok

    def collective(self, src, dst, groups, deps=()):
        st = {"sem": self.es.enter_context(self.nc.semaphore(f"cc{len(self.dsems)}")), "val": 0, "cc": True}
        self.dsems.append(st)
        w = self.pending["gpsimd"] + self._waits("gpsimd", list(deps))
        self.pending["gpsimd"] = []
        st["val"] += 1
        tok = (st["sem"], st["val"])
        self.ops["gpsimd"].append((w, lambda e: e.collective_compute(
            "AllGather", ALU.bypass, replica_groups=groups, ins=[src.opt()], outs=[dst.opt()]), st["sem"], 1))
        return tok

    def emit(self, final_waits):
        nc = self.nc
        with nc.Block() as block:
            def mk(eng):
                def body(e):
                    for (w, fn, s, inc) in self.ops[eng]:
                        for (ws, wv) in w:
                            e.wait_ge(ws, wv)
                        ins = fn(e)
                        if s is not None:
                            ins.then_inc(s, inc)
                    if eng == "sync":
                        for (ws, wv) in final_waits:
                            e.wait_ge(ws, wv)
                return body
            block.tensor(mk("tensor"))
            block.vector(mk("vector"))
            block.scalar(mk("scalar"))
            block.gpsimd(mk("gpsimd"))
            block.sync(mk("sync"))


class Job:
    pass


class Ring:
    def __init__(self, bufs):
        self.bufs = bufs
        self.users = [[] for _ in bufs]
        self.i = 0

    def get(self):
        k = self.i % len(self.bufs)
        self.i += 1
        deps = self.users[k]
        self.users[k] = []
        self.cur = k
        return self.bufs[k], deps

    def used_by(self, tok, k=None):
        if tok is not None:
            self.users[self.cur if k is None else k].append(tok)


class Consts:
    pass


def load_consts(P, dr):
    C = Consts()
    ds = P.dsem("d_const")
    toks = []
    C.ident = P.sb([128, 128], BF16)
    toks.append(P.dma("sync", ds, C.ident[:], dr["c_ident"][:, :]))
    C.identf = P.sb([128, 128], F32)
    toks.append(P.dma("sync", ds, C.identf[:], dr["c_identf"][:, :]))
    C.negU = P.sb([128, 128], BF16)
    toks.append(P.dma("sync", ds, C.negU[:], dr["c_negU"][:, :]))
    C.mask_sb = P.sb([128, 4, 128], BF16)
    toks.append(P.dma("sync", ds, C.mask_sb[:], dr["c_mask_sb"].rearrange("d p t -> p d t")))
    C.mask_mla = P.sb([128, 4, 128], BF16)
    toks.append(P.dma("sync", ds, C.mask_mla[:], dr["c_mask_mla"].rearrange("d p t -> p d t")))
    C.negOnes = P.sb([128, 128], BF16)
    C.ones64 = P.sb([128, 64], BF16)
    C.zeroW = P.sb([128, 64], BF16)
    C.zeroK = P.sb([128, 512], BF16)
    C.zeroI = P.sb([128, 128], BF16)
    C.mhalf = P.sb([128, 1], F32)
    C.phalf = P.sb([128, 1], F32)
    toks.append(P.op("vector", lambda e: e.memset(C.negOnes[:], -1.0)))
    toks.append(P.op("vector", lambda e: e.memset(C.ones64[:], 1.0)))
    toks.append(P.op("vector", lambda e: e.memset(C.zeroW[:], 0.0)))
    toks.append(P.op("vector", lambda e: e.memset(C.zeroK[:], 0.0)))
    toks.append(P.op("vector", lambda e: e.memset(C.zeroI[:], 0.0)))
    toks.append(P.op("vector", lambda e: e.memset(C.mhalf[:], -0.5)))
    toks.append(P.op("vector", lambda e: e.memset(C.phalf[:], 0.5)))
    C.ready = toks
    return C


def bcast_vec(P, ds, vec_ap, n, dtype=F32):
    t = P.sb([128, n], dtype)
    tok = P.dma("sync", ds, t[:], vec_ap.rearrange("(o n) -> o n", o=1).to_broadcast([128, n]))
    return t, tok


def rstd_from_ss(P, C, ss, rstd, n, deps, post_scale=1.0):
    t = P.op("vector", lambda e: e.tensor_scalar(out=rstd, in0=ss, scalar1=1.0 / n, scalar2=EPS,
                                                 op0=ALU.mult, op1=ALU.add), deps=deps)
    t = P.op("gpsimd", lambda e: e.tensor_tensor(out=rstd, in0=rstd, in1=C.mhalf[:], op=ALU.pow), deps=[t])
    if post_scale != 1.0:
        t = P.op("gpsimd", lambda e: e.tensor_scalar(out=rstd, in0=rstd, scalar1=post_scale, scalar2=None,
                                                     op0=ALU.mult), deps=[t])
    return t


class NormT:
    def __init__(self, P, C, gvec_ap, tag, ntiles=NT, nps=2):
        self.P, self.C = P, C
        self.ds = P.dsem("nt" + tag)
        self.xds = [P.dsem("ntx" + tag) for _ in range(3)]
        self.g, self.tg = bcast_vec(P, self.ds, gvec_ap, D)
        self.xs_ring = Ring([P.sb([128, D], F32) for _ in range(3)])
        self.xn_ring = Ring([P.sb([128, D], BF16) for _ in range(2)])
        self.junk = P.sb([128, D], BF16)
        self.stat = P.sb([128, 2 * ntiles], F32)
        self.ps_tr = Ring([P.ps([128, 1024], BF16) for _ in range(nps)])
        self.jt = None
        self.n = 0

    def run_many(self, tiles, hT, extra_deps=()):
        P, C = self.P, self.C
        jobs = []
        for (x_ap, col0) in tiles:
            jb = Job()
            jb.x_ap, jb.col0 = x_ap, col0
            jb.i = self.n
            self.n += 1
            jobs.append(jb)
        junk, g = self.junk, self.g

        def s1(jb):
            jb.xs, d0 = self.xs_ring.get()
            jb.xs_k = self.xs_ring.cur
            xs = jb.xs
            t_ld = P.dma("sync", self.xds[jb.xs_k], xs[:], jb.x_ap, deps=list(d0))
            ss = self.stat[:, 2 * jb.i:2 * jb.i + 1]
            jb.rstd = self.stat[:, 2 * jb.i + 1:2 * jb.i + 2]
            t_ss = P.op("scalar", lambda e: e.activation(out=junk[:], in_=xs[:], func=AF.Square, accum_out=ss),
                        deps=[t_ld, self.jt] + C.ready)
            self.jt = t_ss
            jb.t_r = rstd_from_ss(P, C, ss, jb.rstd, D, [t_ss])

        def s2(jb):
            xs, rstd = jb.xs, jb.rstd
            xn, d1 = self.xn_ring.get()
            t_xn = P.op("vector", lambda e: e.scalar_tensor_tensor(
                out=xn[:], in0=xs[:], scalar=rstd, in1=g[:], op0=ALU.mult, op1=ALU.mult),
                deps=[jb.t_r, self.tg] + list(d1))
            self.xs_ring.used_by(t_xn, k=jb.xs_k)
            jb.pt, d2 = self.ps_tr.get()
            jb.pt_k = self.ps_tr.cur
            pt = jb.pt
            t_tr = None
            for k in range(8):
                t_tr = P.op("tensor", lambda e, k=k: e.transpose(
                    out=pt[:, k * 128:(k + 1) * 128], in_=xn[:, k * 128:(k + 1) * 128], identity=C.ident[:]),
                    deps=[t_xn] + list(d2), sig=(k == 7))
            self.xn_ring.used_by(t_tr)
            jb.t_tr = t_tr

        def s3(jb):
            pt, col0 = jb.pt, jb.col0
            t_cp = P.op("scalar", lambda e: e.copy(
                out=hT[:, :, col0:col0 + 128], in_=pt[:].rearrange("p (k t) -> p k t", k=8)),
                deps=[jb.t_tr] + list(extra_deps))
            self.ps_tr.used_by(t_cp, k=jb.pt_k)
            return t_cp

        toks = []
        n = len(jobs)
        lag = 1 if len(self.ps_tr.bufs) < 2 else 2
        for i in range(n + lag):
            if i < n:
                s1(jobs[i])
            if 0 <= i - 1 < n:
                s2(jobs[i - 1])
            if lag == 1:
                if 0 <= i - 1 < n:
                    toks.append(s3(jobs[i - 1]))
            elif 0 <= i - 2 < n:
                toks.append(s3(jobs[i - 2]))
        return toks


class Epilogue:
    def __init__(self, P, C, gvec_ap, x_in, x_out, post_scale, tag):
        self.P, self.C = P, C
        self.ds = P.dsem("ep" + tag)
        self.xds = [P.dsem("epx" + tag) for _ in range(2)]
        self.ods = [P.dsem("epo" + tag) for _ in range(2)]
        self.g, self.tg = bcast_vec(P, self.ds, gvec_ap, D)
        self.x_in, self.x_out, self.post_scale = x_in, x_out, post_scale
        self.junk = P.sb([128, D], BF16)
        self.stat = P.sb([128, 2 * NT], F32)
        self.xs_ring = Ring([P.sb([128, D], F32) for _ in range(2)])
        self.tmp_ring = Ring([P.sb([128, D], F32) for _ in range(2)])
        self.xo_ring = Ring([P.sb([128, D], F32) for _ in range(2)])
        self.jt = None

    def run(self, pf, tt, t_f):
        P, C = self.P, self.C
        ss = self.stat[:, 2 * tt:2 * tt + 1]
        rstd = self.stat[:, 2 * tt + 1:2 * tt + 2]
        junk, g = self.junk, self.g
        t_ss = P.op("scalar", lambda e: e.activation(out=junk[:], in_=pf, func=AF.Square, accum_out=ss),
                    deps=[t_f, self.jt] + C.ready)
        self.jt = t_ss
        t_r = rstd_from_ss(P, C, ss, rstd, D, [t_ss], post_scale=self.post_scale)
        xs, d2 = self.xs_ring.get()
        t_ld = P.dma("sync", self.xds[self.xs_ring.cur], xs[:], self.x_in[tt * 128:(tt + 1) * 128, :], deps=list(d2))
        tmp, d3 = self.tmp_ring.get()
        t_t = P.op("vector", lambda e: e.scalar_tensor_tensor(
            out=tmp[:], in0=pf, scalar=rstd, in1=g[:], op0=ALU.mult, op1=ALU.mult),
            deps=[t_r, self.tg] + list(d3))
        xo, d4 = self.xo_ring.get()
        t_a = P.op("vector", lambda e: e.tensor_tensor(out=xo[:], in0=tmp[:], in1=xs[:], op=ALU.add),
                   deps=[t_t, t_ld] + list(d4))
        self.tmp_ring.used_by(t_a)
        self.xs_ring.used_by(t_a)
        t_st = P.dma("sync", self.ods[self.xo_ring.cur], self.x_out[tt * 128:(tt + 1) * 128, :], xo[:], deps=[t_a])
        self.xo_ring.used_by(t_st)
        return t_t, t_st


def ffn_phase(P, C, x_in, x_out, w_in, w_out, g_pre, g_post, tag):
    m0 = P.mark()
    HT, KC, FC = 8, D // 128, DFF // 128
    ds_w = [P.dsem(f"dw{tag}{i}") for i in range(2)]
    ds_wo = P.dsem("dwo" + tag)
    nt = NormT(P, C, g_pre, "f" + tag)
    ep = Epilogue(P, C, g_post, x_in, x_out, 0.5, "f" + tag)
    xnT = P.sb([128, KC, HT * 128], BF16)
    g = P.sb([128, FC, HT * 128], BF16)
    wo = P.sb([128, FC, D], BF16)
    win_ring = Ring([P.sb([128, KC, 256], BF16) for _ in range(2)])
    sg_ring = Ring([P.sb([128, 512], F32) for _ in range(2)])
    pfs = [P.ps([128, 1024], F32) for _ in range(2)]
    ps_f = Ring(pfs)
    ps_g = Ring([pfs[0][:, 0:512], pfs[1][:, 0:512]])
    ps_u = Ring([pfs[0][:, 512:1024], pfs[1][:, 512:1024]])
    pf_free = []
    w_in_v = w_in.rearrange("(kc p) n -> p kc n", p=128)
    w_out_v = w_out.rearrange("(kc p) n -> p kc n", p=128)
    t_wo = []
    for k0 in range(0, FC, 6):
        k1 = min(FC, k0 + 6)
        t_wo.append(P.dma("gpsimd", ds_wo, wo[:, k0:k1, :], w_out_v[:, k0:k1, :]))
    last = None
    xnT_free, g_free = [], []
    wcount = 0
    t_s = None
    for hf in range(2):
        tA = nt.run_many([(x_in[(hf * HT + j) * 128:(hf * HT + j + 1) * 128, :], j * 128) for j in range(HT)], xnT, xnT_free)
        xnT_free = []
        tB = []
        for c in range(FC):
            wb, d0 = win_ring.get()
            dsw = ds_w[wcount % 2]
            wcount += 1
            t_w1 = P.dma("gpsimd", dsw, wb[:, :, 0:128], w_in_v[:, :, c * 128:(c + 1) * 128], deps=list(d0))
            t_w2 = P.dma("gpsimd", dsw, wb[:, :, 128:256], w_in_v[:, :, DFF + c * 128:DFF + (c + 1) * 128], deps=list(d0))
            for tg in range(2):
                pg, dg = ps_g.get()
                pu, du = ps_u.get()
                sl = slice(tg * 512, (tg + 1) * 512)
                t_g = t_u = None
                for k in range(KC):
                    t_g = P.op("tensor", lambda e, pg=pg, wb=wb, k=k, sl=sl: e.matmul(
                        pg, lhsT=wb[:, k, 0:128], rhs=xnT[:, k, sl], start=(k == 0), stop=(k == KC - 1)),
                        deps=[t_w1, t_w2] + tA + list(dg) + pf_free, sig=(k == KC - 1))
                for k in range(KC):
                    t_u = P.op("tensor", lambda e, pu=pu, wb=wb, k=k, sl=sl: e.matmul(
                        pu, lhsT=wb[:, k, 128:256], rhs=xnT[:, k, sl], start=(k == 0), stop=(k == KC - 1)),
                        deps=list(du), sig=(k == KC - 1))
                win_ring.used_by(t_u)
                xnT_free.append(t_u)
                sg, ds_ = sg_ring.get()
                t_s = P.op("scalar", lambda e, sg=sg, pg=pg: e.activation(out=sg[:], in_=pg, func=AF.Silu),
                           deps=[t_g] + list(ds_))
                ps_g.used_by(t_s)
                t_m = P.op("vector", lambda e, sg=sg, pu=pu, c=c, sl=sl: e.tensor_tensor(
                    out=g[:, c, sl], in0=sg[:], in1=pu, op=ALU.mult), deps=[t_s, t_u] + g_free)
                sg_ring.used_by(t_m)
                ps_u.used_by(t_m)
                tB.append(t_m)
        g_free, pf_free = [], []
        xnT_free = xnT_free[-2:]
        for j in range(HT):
            tt = hf * HT + j
            pf, d0 = ps_f.get()
            t_f = None
            for nh in range(2):
                for k in range(FC):
                    t_f = P.op("tensor", lambda e, pf=pf, nh=nh, k=k, j=j: e.matmul(
                        pf[:, nh * 512:(nh + 1) * 512], lhsT=g[:, k, j * 128:(j + 1) * 128],
                        rhs=wo[:, k, nh * 512:(nh + 1) * 512], start=(k == 0), stop=(k == FC - 1)),
                        deps=tB[-2:] + t_wo + list(d0) + [t_s], sig=(nh == 1 and k == FC - 1))
            g_free.append(t_f)
            t_t, last = ep.run(pf, tt, t_f)
            ps_f.used_by(t_t)
            pf_free = [t_t]
    P.release(m0)
    return [last]


def oproj_phase(P, C, oT_d, w_out, g_post, x_in, x_out, tag):
    m0 = P.mark()
    ds = P.dsem("op" + tag)
    ep = Epilogue(P, C, g_post, x_in, x_out, 1.0, "o" + tag)
    oT = P.sb([128, 8, TOK], BF16)
    wo = P.sb([128, 8, D], BF16)
    t_in = [P.dma("sync", ds, oT[:, 0:4, :], oT_d.rearrange("(kc p) t -> p kc t", p=128)[:, 0:4, :]),
            P.dma("sync", ds, oT[:, 4:8, :], oT_d.rearrange("(kc p) t -> p kc t", p=128)[:, 4:8, :]),
            P.dma("gpsimd", ds, wo[:], w_out.rearrange("(kc p) n -> p kc n", p=128))]
    ps_f = Ring([P.ps([128, 1024], F32) for _ in range(2)])
    last = None
    for tt in range(NT):
        pf, d0 = ps_f.get()
        t_f = None
        for nh in range(2):
            for k in range(8):
                t_f = P.op("tensor", lambda e, pf=pf, nh=nh, k=k, tt=tt: e.matmul(
                    pf[:, nh * 512:(nh + 1) * 512], lhsT=oT[:, k, tt * 128:(tt + 1) * 128],
                    rhs=wo[:, k, nh * 512:(nh + 1) * 512], start=(k == 0), stop=(k == 7)),
                    deps=t_in + list(d0), sig=(nh == 1 and k == 7))
        t_t, last = ep.run(pf, tt, t_f)
        ps_f.used_by(t_t)
    P.release(m0)
    return [last]


def sbproj_phase(P, C, x_in, w_in, g_pre, qT_d, kT_d, v_d, tag, coll=None):
    m0 = P.mark()
    ds_w = [P.dsem(f"sw{tag}{i}") for i in range(2)]
    ds_v = P.dsem("sv" + tag)
    ds_o = [P.dsem(f"so{tag}{i}") for i in range(4)]
    ds_o2 = [P.dsem(f"sp{tag}{i}") for i in range(4)]
    nt = NormT(P, C, g_pre, "s" + tag)
    hT = P.sb([128, 8, TOK], BF16)
    w_v = w_in.rearrange("(kc p) n -> p kc n", p=128)
    wv = P.sb([128, 8, D], BF16)
    t_wv = [P.dma("gpsimd", ds_v, wv[:, 0:4, :], w_v[:, 0:4, 2048:3072]),
            P.dma("gpsimd", ds_v, wv[:, 4:8, :], w_v[:, 4:8, 2048:3072])]
    stg_ring = Ring([P.sb([128, TOK], BF16) for _ in range(4)])
    ps_ring = Ring([P.ps([128, 512], F32) for _ in range(2)])
    order = list(range(8, 16)) + list(range(0, 8))
    wq_all = P.sb([128, 16, 8, 128], BF16)
    ds_wg = [P.dsem(f"swg{tag}{i}") for i in range(2)]
    tA = nt.run_many([(x_in[j * 128:(j + 1) * 128, :], j * 128) for j in range(NT)], hT)
    t_wq = {}
    hist = []
    for i, oc in enumerate(order):
        dep = [hist[i - 2]] if i >= 2 else []
        t_wq[oc] = P.dma("gpsimd", ds_wg[i % 2], wq_all[:, oc, :, :], w_v[:, :, oc * 128:(oc + 1) * 128], deps=dep)
        hist.append(t_wq[oc])
    tc_v, tc_k = [], []
    pv_ring = Ring([P.ps([128, 1024], F32) for _ in range(2)])
    vs_ring = Ring([P.sb([128, D], BF16) for _ in range(4)])
    vst = []
    for tt in range(NT):
        pv, d0 = pv_ring.get()
        t_m = None
        for nh in range(2):
            for k in range(8):
                t_m = P.op("tensor", lambda e, pv=pv, nh=nh, k=k, tt=tt: e.matmul(
                    pv[:, nh * 512:(nh + 1) * 512], lhsT=hT[:, k, tt * 128:(tt + 1) * 128],
                    rhs=wv[:, k, nh * 512:(nh + 1) * 512], start=(k == 0), stop=(k == 7)),
                    deps=t_wv + [tA[tt]] + list(d0), sig=(nh == 1 and k == 7))
        vs, d1 = vs_ring.get()
        t_c = P.op("vector", lambda e, pv=pv, vs=vs: e.tensor_copy(out=vs[:], in_=pv), deps=[t_m] + list(d1))
        pv_ring.used_by(t_c)
        t_st = P.dma("sync", ds_o2[vs_ring.cur], v_d[tt * 128:(tt + 1) * 128, :], vs[:], deps=[t_c])
        vs_ring.used_by(t_st)
        vst.append(t_st)
        if coll is not None and tt % 4 == 3:
            k4 = tt // 4
            tc_v.append(P.collective(v_d[k4 * 512:(k4 + 1) * 512, :], coll[1][k4 * 2048:(k4 + 1) * 2048, :], coll[2],
                                     deps=vst[-4:]))
    kst = []
    pend_coll = None
    for i, oc in enumerate(order):
        wb = wq_all[:, oc, :, :]
        t_w = t_wq[oc]
        if pend_coll is not None:
            k4, deps_ = pend_coll
            tc_k.append(P.collective(kT_d[k4 * 256:(k4 + 1) * 256, :], coll[0][k4 * 1024:(k4 + 1) * 1024, :], coll[2], deps=deps_))
            pend_coll = None
        stg, d1 = stg_ring.get()
        t_e = None
        t_m = None
        for tg in range(4):
            pp, d2 = ps_ring.get()
            for k in range(8):
                t_m = P.op("tensor", lambda e, pp=pp, wb=wb, k=k, tg=tg: e.matmul(
                    pp, lhsT=wb[:, k, :], rhs=hT[:, k, tg * 512:(tg + 1) * 512], start=(k == 0), stop=(k == 7)),
                    deps=[t_w] + tA + list(d2), sig=(k == 7))
            sc = 0.125 if oc < 8 else 1.0
            t_e = P.op("scalar", lambda e, pp=pp, stg=stg, tg=tg, sc=sc: e.mul(
                out=stg[:, tg * 512:(tg + 1) * 512], in_=pp, mul=sc), deps=[t_m] + list(d1))
            ps_ring.used_by(t_e)
        dst = (qT_d if oc < 8 else kT_d)[(oc % 8) * 128:(oc % 8 + 1) * 128, :]
        t_st = P.dma("sync", ds_o[stg_ring.cur], dst, stg[:], deps=[t_e])
        stg_ring.used_by(t_st)
        if oc >= 8:
            kst.append(t_st)
            if coll is not None and (oc - 8) % 2 == 1:
                pend_coll = ((oc - 8) // 2, kst[-2:])
    P.release(m0)
    return tc_k, tc_v


def key_schedule(L):
    out = []
    for kb in range(16 * L + 15, -1, -1):
        if kb >= 16 * L:
            j = (kb - 16 * L) // 4
            out.append((kb, 128 * j, (kb - 16 * L) % 4))
        else:
            out.append((kb, 0, None))
    return out


class Job:
    pass


def sbattn_phase(P, C, qT_d, KT_g, V_g, oT_d, tag, heads=range(16), chunked=False, kv_toks=None):
    m0 = P.mark()
    ds_k = [P.dsem(f"ak{tag}{i}") for i in range(2)]
    ds_o = [P.dsem(f"ao{tag}{i}") for i in range(2)]
    kT_bufs = [(P.sb([128, 4, TOK], BF16), P.sb([128, 4, TOK], BF16)) for _ in range(2)]
    t_kz = []
    for (ka_, kb_) in kT_bufs:
        t_kz.append(P.op("vector", lambda e, ka_=ka_: e.memset(ka_[64:128, :, :], 0.0)))
        t_kz.append(P.op("vector", lambda e, kb_=kb_: e.memset(kb_[0:64, :, :], 0.0)))
    kT_ring = Ring(kT_bufs)
    v_ring = Ring([P.sb([128, 4, 16, 128], BF16) for _ in range(2)])
    q_ring = Ring([P.sb([128, TOK], BF16) for _ in range(2)])
    E_ring = Ring([P.sb([128, 2, 512], BF16) for _ in range(2)])
    SP_ring = Ring([P.sb([128, 2, 512], BF16) for _ in range(3)])
    A_ring = Ring([P.sb([128, 2, 512], BF16) for _ in range(3)])
    Smid_ring = Ring([P.sb([128, 512], BF16) for _ in range(3)])
    Snx_ring = Ring([P.sb([128, 512], BF16) for _ in range(3)])
    S32 = P.sb([128, 512], F32)
    S32m = P.sb([128, 512], F32)
    ostg_ring = Ring([P.sb([128, TOK], BF16) for _ in range(2)])
    Z_ring = Ring([P.ps([128, 1024], F32).rearrange("p (b n) -> p b n", b=2) for _ in range(3)])
    O_ring = Ring([P.ps([128, 512], F32) for _ in range(2)])

    jobs = []
    pairs = sorted(set(h // 2 for h in heads))
    holders = []

    def load_pair(pi):
        if pi >= len(pairs):
            return
        hd = holders[pi]
        hp = pairs[pi]
        hd.kT, d0 = kT_ring.get()
        hd.v, d1 = v_ring.get()
        hd.q, d2 = q_ring.get()
        dsk = ds_k[pi % 2]
        if kv_toks is not None:
            d0 = list(d0) + [kv_toks[0][hp // 2]]
            d1 = list(d1) + list(kv_toks[1])
        if chunked:
            ksrc = KT_g[hp // 2, :, (hp % 2) * 128:(hp % 2) * 128 + 128, :].rearrange("r p t -> p r t")
        else:
            ksrc = KT_g[:, hp * 128:(hp + 1) * 128, :].rearrange("r p t -> p r t")
        lt = [P.dma("sync", dsk, hd.kT[0][0:64, :, :], ksrc[0:64], deps=list(d0) + t_kz),
              P.dma("sync", dsk, hd.kT[1][64:128, :, :], ksrc[64:128], deps=list(d0) + t_kz)]
        for r in range(4):
            if chunked:
                for k in range(4):
                    lt.append(P.dma("sync", dsk, hd.v[:, r, 4 * k:4 * k + 4, :],
                                    V_g[k, r].rearrange("(m p) c -> p m c", p=128)[:, :, hp * 128:(hp + 1) * 128],
                                    deps=list(d1)))
            else:
                lt.append(P.dma("sync", dsk, hd.v[:, r, :, :],
                                V_g[r].rearrange("(m p) c -> p m c", p=128)[:, :, hp * 128:(hp + 1) * 128], deps=list(d1)))
        lt.append(P.dma("sync", dsk, hd.q[:], qT_d[hp * 128:(hp + 1) * 128, :], deps=list(d2)))
        hd.loads = lt
        hd.pair_slots = (kT_ring.cur, v_ring.cur, q_ring.cur)

    for pi, hp in enumerate(pairs):
        hd = Job()
        holders.append(hd)
        hs = [h for h in heads if h // 2 == hp]
        cnt = 0
        for h in hs:
            for L in range(4):
                sched = key_schedule(L)
                st = Job()
                npair = len(sched) // 2
                for i in range(npair):
                    (kbA, c0, dlA), (kbB, c0b, dlB) = sched[2 * i], sched[2 * i + 1]
                    assert c0 == c0b
                    jb = Job()
                    jb.stream, jb.hd = st, hd
                    jb.h, jb.hh, jb.L, jb.c0 = h, h % 2, L, c0
                    jb.kb, jb.dl = (kbA, kbB), (dlA, dlB)
                    jb.first, jb.last = (i == 0), (i == npair - 1)
                    jb.next_c0 = sched[2 * i + 2][1] if not jb.last else None
                    jb.pair_last = jb.last and L == 3 and h == hs[-1]
                    jb.pair_first_head = (h == hs[0])
                    jb.prefetch = pi + 1 if cnt == 3 else None
                    cnt += 1
                    jobs.append(jb)
    load_pair(0)
    fin = []
    state = {"s32": None, "s32m": None, "ostg": None}

    def kq(jb, b):
        kb = jb.kb[b]
        ks = jb.hd.kT[jb.hh][:, kb % 4, (kb // 4) * 128:(kb // 4) * 128 + 128]
        qs = jb.hd.q[:, jb.L * 512 + jb.c0:(jb.L + 1) * 512]
        return ks, qs

    def chain(ops, sig_last, fresh=True, close=True):
        n_ = len(ops)
        tok = None
        for i_, (o, l, r, d) in enumerate(ops):
            tok = P.op("tensor", lambda e, o=o, l=l, r=r, i_=i_: e.matmul(
                o, lhsT=l, rhs=r, start=(fresh and i_ == 0), stop=(close and i_ == n_ - 1), skip_group_check=True),
                deps=d, sig=(sig_last and i_ == n_ - 1))
        return tok

    def zops(jb, dst, b, deps):
        c0 = jb.c0
        ks, qs = kq(jb, b)
        ops = [(dst[:, b, c0:512], ks, qs, deps)]
        if jb.dl[b] is not None:
            ops.append((dst[:, b, c0:c0 + 128], C.ident[:], C.mask_sb[:, jb.dl[b], :], []))
        return ops

    def st1_pe(jb):
        Zb, dz = Z_ring.get()
        jb.Z = Zb
        chain(zops(jb, Zb, 0, jb.hd.loads + list(dz) + C.ready), False, close=False)
        jb.tz = chain(zops(jb, Zb, 1, []), True, close=False)

    def st1_act_e(jb):
        c0 = jb.c0
        Eb, de = E_ring.get()
        jb.E = Eb
        Zb = jb.Z
        jb.te = P.op("scalar", lambda e: e.activation(out=Eb[:, :, c0:512], in_=Zb[:, :, c0:512], func=AF.Exp),
                     deps=[jb.tz] + list(de))
        jb.Z_k = Z_ring.cur

    def st1_act_sp(jb):
        c0 = jb.c0
        Eb = jb.E
        SPb, dsp = SP_ring.get()
        jb.SP, jb.SP_k = SPb, SP_ring.cur
        jb.tsp = P.op("scalar", lambda e: e.activation(out=SPb[:, :, c0:512], in_=Eb[:, :, c0:512], func=AF.Ln, bias=1.0),
                      deps=[jb.te] + list(dsp))
        E_ring.used_by(jb.tsp)
        st = jb.stream
        Sm, dsm = Smid_ring.get()
        jb.Smid, jb.Smid_k = Sm, Smid_ring.cur
        if jb.first:
            t_a = P.op("vector", lambda e: e.tensor_copy(out=S32m[:, c0:512], in_=SPb[:, 0, c0:512]),
                       deps=[jb.tsp, state["s32m"]])
            jb.tSmid = P.op("vector", lambda e: e.tensor_copy(out=Sm[:, c0:512], in_=SPb[:, 0, c0:512]),
                            deps=[jb.tsp] + list(dsm))
        else:
            t_a = P.op("vector", lambda e: e.tensor_tensor(out=S32m[:, c0:512], in0=S32[:, c0:512], in1=SPb[:, 0, c0:512],
                                                           op=ALU.add), deps=[jb.tsp, state["s32m"], st.t_b])
            jb.tSmid = P.op("vector", lambda e: e.tensor_tensor(out=Sm[:, c0:512], in0=S32[:, c0:512], in1=SPb[:, 0, c0:512],
                                                                op=ALU.add), deps=[jb.tsp, st.t_b] + list(dsm))
        SP_ring.used_by(jb.tSmid, k=jb.SP_k)
        if not jb.last:
            n0 = jb.next_c0
            t_z = None
            if jb.first and c0 > 0:
                t_z = P.op("vector", lambda e: e.memset(S32m[:, 0:c0], 0.0), deps=[state["s32m"]])
            t_b = P.op("vector", lambda e: e.tensor_tensor(out=S32[:, c0:512], in0=S32m[:, c0:512], in1=SPb[:, 1, c0:512],
                                                           op=ALU.add), deps=[t_a, state["s32"]])
            Sn, dsn = Snx_ring.get()
            t_cv = P.op("vector", lambda e: e.tensor_tensor(out=Sn[:, c0:512], in0=S32m[:, c0:512], in1=SPb[:, 1, c0:512],
                                                            op=ALU.add), deps=[t_a] + list(dsn))
            t_x = None
            if n0 < c0:
                P.op("vector", lambda e: e.memset(S32[:, n0:c0], 0.0), deps=[state["s32"]], sig=False)
                t_cv = P.op("vector", lambda e: e.memset(Sn[:, n0:c0], 0.0), deps=[t_cv])
            st.t_b = t_cv
            state["s32"] = t_cv
            state["s32m"] = t_cv
            SP_ring.used_by(t_cv, k=jb.SP_k)
            st.next_S = (Sn, Snx_ring.cur, t_cv)
        else:
            state["s32m"] = jb.tSmid

    def st2_pe(jb):
        c0 = jb.c0
        Tb = jb.Z
        jb.T = Tb
        SPb = jb.SP
        ops0 = [(Tb[:, 0, c0:512], C.negU[:], SPb[:, 0, c0:512], [jb.tsp, jb.te])]
        if not jb.first:
            Sb, Sk, tS = jb.Sprev
            ops0.append((Tb[:, 0, c0:512], C.negOnes[:], Sb[:, c0:512], [tS]))
        chain(ops0, False, fresh=False)
        ops1 = [(Tb[:, 1, c0:512], C.negU[:], SPb[:, 1, c0:512], []),
                (Tb[:, 1, c0:512], C.negOnes[:], jb.Smid[:, c0:512], [jb.tSmid])]
        jb.tT = chain(ops1, True, fresh=False)
        if not jb.first:
            Snx_ring.used_by(jb.tT, k=Sk)
        Smid_ring.used_by(jb.tT, k=jb.Smid_k)
        SP_ring.used_by(jb.tT, k=jb.SP_k)

    def st2_act(jb):
        c0 = jb.c0
        Tb = jb.T
        Ab, da = A_ring.get()
        jb.A, jb.A_k = Ab, A_ring.cur
        jb.tA = P.op("scalar", lambda e: e.activation(out=Ab[:, :, c0:512], in_=Tb[:, :, c0:512], func=AF.Exp),
                     deps=[jb.tT] + list(da))
        Z_ring.used_by(jb.tA, k=jb.Z_k)

    def st3(jb):
        c0 = jb.c0
        st = jb.stream
        if jb.first:
            st.O, dO = O_ring.get()
            st.O_k = O_ring.cur
            P.op("tensor", lambda e: e.matmul(st.O[:, 0:512], lhsT=C.zeroI[:], rhs=C.zeroK[:], start=True, stop=False),
                 deps=list(dO), sig=False)
        Ab = jb.A
        tav = None
        for b in range(2):
            kb = jb.kb[b]
            vb = jb.hd.v[:, kb % 4, kb // 4, :]
            tav = P.op("tensor", lambda e, b=b, vb=vb: e.matmul(st.O[:, c0:512], lhsT=vb, rhs=Ab[:, b, c0:512], start=False,
                                                                stop=(jb.last and b == 1), skip_group_check=True),
                       deps=[jb.tA], sig=(b == 1))
        A_ring.used_by(tav, k=jb.A_k)
        if jb.last:
            if jb.L == 0 and jb.pair_first_head:
                state["ostg"], state["ostg_d"] = ostg_ring.get()
                state["ostg_k"] = ostg_ring.cur
            og = state["ostg"]
            L = jb.L
            pr = slice(jb.hh * 64, jb.hh * 64 + 64)
            tcp = P.op("vector", lambda e: e.tensor_copy(out=og[pr, L * 512:(L + 1) * 512], in_=st.O[pr, 0:512]),
                       deps=[tav] + list(state["ostg_d"]))
            O_ring.used_by(tcp, k=st.O_k)
            if jb.L == 3:
                hp_ = jb.h // 2
                t_st = P.dma("sync", ds_o[state["ostg_k"]], oT_d[jb.h * 64:(jb.h + 1) * 64, :], og[pr, :], deps=[tcp])
                ostg_ring.used_by(t_st, k=state["ostg_k"])
                fin[:] = [t_st]
            if jb.pair_last:
                kk, vk, qk = jb.hd.pair_slots
                kT_ring.used_by(tav, k=kk)
                v_ring.used_by(tav, k=vk)
                q_ring.used_by(tav, k=qk)

    n = len(jobs)
    for i in range(-1, n + 2):
        nx = jobs[i + 1] if 0 <= i + 1 < n else None
        cur = jobs[i] if 0 <= i < n else None
        pv = jobs[i - 1] if 0 <= i - 1 < n else None
        pv2 = jobs[i - 2] if 0 <= i - 2 < n else None
        if nx is not None:
            if not nx.first:
                nx.Sprev = nx.stream.next_S
            if nx.prefetch is not None:
                load_pair(nx.prefetch)
            st1_pe(nx)
        if cur is not None:
            st2_pe(cur)
        if pv2 is not None:
            st3(pv2)
        if nx is not None:
            st1_act_e(nx)
        if pv is not None:
            st2_act(pv)
        if nx is not None:
            st1_act_sp(nx)
    P.release(m0)
    return fin


MLA_SCALE = 1.0 / np.sqrt(96.0)
TWO_PI = 2.0 * np.pi
CW1 = 6.28125
CW2 = float(TWO_PI - 6.28125)


def mlaproj_phase(P, C, x_in, pos_d, w_in, qnorm_g, w_uq, kvnorm_g, w_ukv, g_pre, invf_d,
                  qaT_d, qnT_d, latT_d, knmax_d, tag):
    m0 = P.mark()
    ds = P.dsem("mp" + tag)
    ds_o = P.dsem("mo" + tag)
    nt = NormT(P, C, g_pre, "m" + tag, nps=1)
    hT = P.sb([128, 8, TOK], BF16)
    tA = nt.run_many([(x_in[j * 128:(j + 1) * 128, :], j * 128) for j in range(NT)], hT)
    win = P.sb([128, 8, 416], BF16)
    wuq = P.sb([128, 2, 1536], BF16)
    wuk = P.sb([128, 16, 64], BF16)
    t_w = [P.dma("gpsimd", ds, win[:], w_in.rearrange("(kc p) n -> p kc n", p=128)),
           P.dma("gpsimd", ds, wuq[:], w_uq.rearrange("(kc p) n -> p kc n", p=128)),
           P.dma("gpsimd", ds, wuk[:], w_ukv.rearrange("p (h c) -> p h c", c=128)[:, :, 0:64])]
    gq, t1 = bcast_vec(P, ds, qnorm_g, 256)
    gkv, t2 = bcast_vec(P, ds, kvnorm_g, 128)
    invf, t3 = bcast_vec(P, ds, invf_d, 16)
    posi = P.sb([128, NT], I32)
    t4 = P.dma("sync", ds, posi[:], pos_d.rearrange("(t p) -> p t", p=128), allow_slow_non_contiguous=True)
    t_c = [t1, t2, t3, t4] + t_w
    posf = P.sb([128, NT], F32)
    ang = P.sb([128, NT, 16], F32)
    ang2 = P.sb([128, NT, 16], F32)
    ni = P.sb([128, NT, 16], I32)
    nf = P.sb([128, NT, 16], F32)
    rr = P.sb([128, NT, 16], F32)
    gt = P.sb([128, NT, 16], F32)
    sint = P.sb([128, NT, 16], F32)
    cost = P.sb([128, NT, 16], F32)
    t = P.op("vector", lambda e: e.tensor_copy(out=posf[:], in_=posi[:]), deps=t_c + C.ready)
    t = P.op("vector", lambda e: e.tensor_tensor(out=ang[:], in0=posf[:].unsqueeze(2).to_broadcast([128, NT, 16]),
                                                 in1=invf[:].unsqueeze(1).to_broadcast([128, NT, 16]), op=ALU.mult), deps=[t])
    t = P.op("vector", lambda e: e.tensor_scalar(out=ang2[:], in0=ang[:], scalar1=float(np.pi / 2), scalar2=None,
                                                 op0=ALU.add), deps=[t])

    def sincos(a, out, t):
        t = P.op("vector", lambda e: e.tensor_scalar(out=ni[:], in0=a[:], scalar1=float(1.0 / TWO_PI), scalar2=None,
                                                     op0=ALU.mult), deps=[t])
        t = P.op("vector", lambda e: e.tensor_copy(out=nf[:], in_=ni[:]), deps=[t])
        t = P.op("vector", lambda e: e.scalar_tensor_tensor(out=rr[:], in0=nf[:], scalar=-CW1, in1=a[:],
                                                            op0=ALU.mult, op1=ALU.add), deps=[t])
        t = P.op("vector", lambda e: e.scalar_tensor_tensor(out=rr[:], in0=nf[:], scalar=-CW2, in1=rr[:],
                                                            op0=ALU.mult, op1=ALU.add), deps=[t])
        t = P.op("vector", lambda e: e.tensor_scalar(out=gt[:], in0=rr[:], scalar1=float(np.pi), scalar2=float(-TWO_PI),
                                                     op0=ALU.is_gt, op1=ALU.mult), deps=[t])
        t = P.op("vector", lambda e: e.tensor_tensor(out=rr[:], in0=rr[:], in1=gt[:], op=ALU.add), deps=[t])
        t = P.op("vector", lambda e: e.tensor_scalar(out=gt[:], in0=rr[:], scalar1=float(-np.pi), scalar2=float(TWO_PI),
                                                     op0=ALU.is_lt, op1=ALU.mult), deps=[t])
        t = P.op("vector", lambda e: e.tensor_tensor(out=rr[:], in0=rr[:], in1=gt[:], op=ALU.add), deps=[t])
        t = P.op("vector", lambda e: e.tensor_scalar(out=rr[:], in0=rr[:], scalar1=3.14159, scalar2=-3.14159,
                                                     op0=ALU.min, op1=ALU.max), deps=[t])
        t = P.op("scalar", lambda e: e.activation(out=out[:], in_=rr[:], func=AF.Sin), deps=[t])
        return t
    t = sincos(ang, sint, t)
    t_tab = sincos(ang2, cost, t)

    junk = P.sb([128, 1024], BF16)
    stat = P.sb([128, 8 * NT], F32)
    cqn = P.sb([128, 256], BF16)
    lat = P.sb([128, 160], BF16)
    kr = P.sb([128, 32], F32)
    kro = P.sb([128, 32], F32)
    tr = P.sb([128, 4, 16], F32)
    cqnT = [P.sb([128, 2, 128], BF16) for _ in range(2)]
    ckvT = [P.sb([128, 128], BF16) for _ in range(2)]
    latT_s = P.sb([128, TOK], BF16)
    krT_s = P.sb([32, TOK], BF16)
    qs = P.sb([128, 16, 96], F32)
    qt = P.sb([128, 4, 16, 16], F32)
    qsq = P.sb([128, 16, 96], F32)
    qn = P.sb([128, 16], F32)
    qa = P.sb([128, 16, 96], BF16)
    ksq = P.sb([128, 16, 64], F32)
    kn2 = P.sb([128, 16], F32)
    knmax = P.sb([128, 16], F32)
    qaT_s = P.sb([96, 16, 128], BF16)
    qnT_s = P.sb([16, TOK], F32)
    pj = P.ps([128, 512], F32)
    pT = P.ps([128, 1024], BF16)
    pq3 = P.ps([128, 1536], F32)
    pkn = P.ps([128, 1024], F32)
    pq = pq3[:, 0:1024].bitcast(BF16)
    pqn = pq3[:, 1024:1536]
    last = []
    def tileF(tt, prevF, prevB):
        tsl = slice(tt * 128, (tt + 1) * 128)
        sc = stat[:, 8 * tt:8 * tt + 8]
        t_p = None
        for k in range(8):
            t_p = P.op("tensor", lambda e, k=k: e.matmul(pj[:, 0:416], lhsT=hT[:, k, tsl], rhs=win[:, k, :],
                                                          start=(k == 0), stop=(k == 7)),
                       deps=tA + t_c + [prevF], sig=(k == 7))
        t_s1 = P.op("scalar", lambda e: e.activation(out=junk[:, 0:256], in_=pj[:, 0:256], func=AF.Square,
                                                     accum_out=sc[:, 0:1]), deps=[t_p, prevF])
        t_s2 = P.op("scalar", lambda e: e.activation(out=junk[:, 256:384], in_=pj[:, 256:384], func=AF.Square,
                                                     accum_out=sc[:, 2:3]), deps=[t_p, prevF])
        t_r1 = rstd_from_ss(P, C, sc[:, 0:1], sc[:, 1:2], 256, [t_s1])
        t_r2 = rstd_from_ss(P, C, sc[:, 2:3], sc[:, 3:4], 128, [t_s2])
        t_cq = P.op("vector", lambda e: e.scalar_tensor_tensor(out=cqn[:], in0=pj[:, 0:256], scalar=sc[:, 1:2],
                                                               in1=gq[:], op0=ALU.mult, op1=ALU.mult), deps=[t_r1, prevF])
        t_ck = P.op("vector", lambda e: e.scalar_tensor_tensor(out=lat[:, 0:128], in0=pj[:, 256:384], scalar=sc[:, 3:4],
                                                               in1=gkv[:], op0=ALU.mult, op1=ALU.mult), deps=[t_r2, prevF])
        t_kr = P.op("vector", lambda e: e.tensor_copy(out=kr[:], in_=pj[:, 384:416]), deps=[t_p, prevF])
        cs, sn = cost[:, tt, :], sint[:, tt, :]
        t_a = P.op("vector", lambda e: e.tensor_tensor(out=tr[:, 0, :], in0=kr[:, 0:16], in1=cs, op=ALU.mult), deps=[t_kr, t_tab])
        t_b = P.op("vector", lambda e: e.tensor_tensor(out=tr[:, 1, :], in0=kr[:, 16:32], in1=sn, op=ALU.mult), deps=[t_kr])
        t_c2 = P.op("vector", lambda e: e.tensor_tensor(out=tr[:, 2, :], in0=kr[:, 16:32], in1=cs, op=ALU.mult), deps=[t_kr])
        t_d = P.op("vector", lambda e: e.tensor_tensor(out=tr[:, 3, :], in0=kr[:, 0:16], in1=sn, op=ALU.mult), deps=[t_kr])
        t_o1 = P.op("vector", lambda e: e.tensor_tensor(out=kro[:, 0:16], in0=tr[:, 0, :], in1=tr[:, 1, :], op=ALU.subtract),
                    deps=[t_a, t_b])
        t_o2 = P.op("vector", lambda e: e.tensor_tensor(out=kro[:, 16:32], in0=tr[:, 2, :], in1=tr[:, 3, :], op=ALU.add),
                    deps=[t_c2, t_d])
        t_kl = P.op("vector", lambda e: e.tensor_copy(out=lat[:, 128:160], in_=kro[:]), deps=[t_o1, t_o2])
        t_r2k = P.op("scalar", lambda e: e.activation(out=junk[:, 384:416], in_=kro[:], func=AF.Square,
                                                      accum_out=sc[:, 4:5]), deps=[t_o1, t_o2])
        t_t = None
        for k in range(2):
            P.op("tensor", lambda e, k=k: e.transpose(out=pT[:, k * 128:(k + 1) * 128], in_=cqn[:, k * 128:(k + 1) * 128],
                                                      identity=C.ident[:]), deps=[t_cq, prevF], sig=False)
        P.op("tensor", lambda e: e.transpose(out=pT[:, 256:384], in_=lat[:, 0:128], identity=C.ident[:]),
             deps=[t_ck], sig=False)
        t_t = P.op("tensor", lambda e: e.transpose(out=pT[0:32, 384:512], in_=lat[:, 128:160], identity=C.ident[:]),
                   deps=[t_kl])
        t_x1 = P.op("scalar", lambda e: e.copy(out=cqnT[tt % 2][:], in_=pT[:, 0:256].rearrange("p (k t) -> p k t", k=2)), deps=[t_t, prevF, prevB])
        t_x2 = P.op("scalar", lambda e: e.copy(out=ckvT[tt % 2][:], in_=pT[:, 256:384]), deps=[t_t, prevF, prevB])
        t_x3 = P.op("scalar", lambda e: e.copy(out=latT_s[:, tsl], in_=pT[:, 256:384]), deps=[t_t])
        t_x4 = P.op("scalar", lambda e: e.copy(out=krT_s[:, tsl], in_=pT[0:32, 384:512]), deps=[t_t])
        return [t_x1, t_x2, t_x3, t_x4, t_r2k, t_kl, t_ck, t_cq]

    def tileB(tt, fF, prevB):
        tsl = slice(tt * 128, (tt + 1) * 128)
        sc = stat[:, 8 * tt:8 * tt + 8]
        cs, sn = cost[:, tt, :], sint[:, tt, :]
        prev = prevB
        t_q = None
        for n3 in range(3):
            for k in range(2):
                t_q = P.op("tensor", lambda e, n3=n3, k=k: e.matmul(pq3[:, n3 * 512:(n3 + 1) * 512], lhsT=cqnT[tt % 2][:, k, :],
                                                                    rhs=wuq[:, k, n3 * 512:(n3 + 1) * 512],
                                                                    start=(k == 0), stop=(k == 1)),
                           deps=[fF, prev], sig=(n3 == 2 and k == 1))
        t_kn = None
        for n2 in range(2):
            t_kn = P.op("tensor", lambda e, n2=n2: e.matmul(
                pkn[:, n2 * 512:(n2 + 1) * 512], lhsT=ckvT[tt % 2][:],
                rhs=wuk[:, n2 * 8:(n2 + 1) * 8, :], start=True, stop=True),
                deps=[fF, prev], sig=(n2 == 1))
        t_k1 = P.op("scalar", lambda e: e.activation(out=ksq[:], in_=pkn[:].rearrange("p (h c) -> p h c", c=64),
                                                     func=AF.Square), deps=[t_kn, prev])
        t_k2 = P.op("vector", lambda e: e.tensor_reduce(out=kn2[:], in_=ksq[:], axis=AX.X, op=ALU.add), deps=[t_k1, prev])
        if tt == 0:
            t_k3 = P.op("vector", lambda e: e.tensor_scalar(out=knmax[:], in0=kn2[:], scalar1=sc[:, 4:5], scalar2=None,
                                                            op0=ALU.add), deps=[t_k2, fF])
        else:
            t_k3 = P.op("vector", lambda e: e.scalar_tensor_tensor(out=knmax[:], in0=kn2[:], scalar=sc[:, 4:5], in1=knmax[:],
                                                                   op0=ALU.add, op1=ALU.max), deps=[t_k2, fF, prev])
        t_qs = P.op("scalar", lambda e: e.mul(out=qs[:], in_=pq3[:].rearrange("p (h c) -> p h c", c=96), mul=float(MLA_SCALE)),
                    deps=[t_q, prev])
        csb = cs.unsqueeze(1).to_broadcast([128, 16, 16])
        snb = sn.unsqueeze(1).to_broadcast([128, 16, 16])
        t_a = P.op("vector", lambda e: e.tensor_tensor(out=qt[:, 0, :, :], in0=qs[:, :, 64:80], in1=csb, op=ALU.mult), deps=[t_qs, t_tab, prev])
        t_b = P.op("vector", lambda e: e.tensor_tensor(out=qt[:, 1, :, :], in0=qs[:, :, 80:96], in1=snb, op=ALU.mult), deps=[t_qs])
        t_c2 = P.op("vector", lambda e: e.tensor_tensor(out=qt[:, 2, :, :], in0=qs[:, :, 80:96], in1=csb, op=ALU.mult), deps=[t_qs])
        t_d = P.op("vector", lambda e: e.tensor_tensor(out=qt[:, 3, :, :], in0=qs[:, :, 64:80], in1=snb, op=ALU.mult), deps=[t_qs])
        t_o1 = P.op("vector", lambda e: e.tensor_tensor(out=qs[:, :, 64:80], in0=qt[:, 0, :, :], in1=qt[:, 1, :, :], op=ALU.subtract),
                    deps=[t_a, t_b, t_c2, t_d])
        t_o2 = P.op("vector", lambda e: e.tensor_tensor(out=qs[:, :, 80:96], in0=qt[:, 2, :, :], in1=qt[:, 3, :, :], op=ALU.add),
                    deps=[t_o1])
        t_sq = P.op("vector", lambda e: e.tensor_tensor(out=qsq[:], in0=qs[:], in1=qs[:], op=ALU.mult), deps=[t_o1, t_o2, prev])
        t_n2 = P.op("vector", lambda e: e.tensor_reduce(out=qn[:], in_=qsq[:], axis=AX.X, op=ALU.add), deps=[t_sq])
        t_n = P.op("gpsimd", lambda e: e.tensor_tensor(out=qn[:], in0=qn[:], in1=C.phalf[:].to_broadcast([128, 16]), op=ALU.pow),
                   deps=[t_n2])
        t_qa = P.op("scalar", lambda e: e.copy(out=qa[:], in_=qs[:]), deps=[t_o1, t_o2, prev])
        t_tq = None
        for h in range(16):
            t_tq = P.op("tensor", lambda e, h=h: e.transpose(out=pq[0:96, h * 128:(h + 1) * 128], in_=qa[:, h, :],
                                                             identity=C.ident[:]), deps=[t_qa, t_qs], sig=(h == 15))
        t_tn = P.op("tensor", lambda e: e.transpose(out=pqn[0:16, 0:128], in_=qn[:], identity=C.identf[:]), deps=[t_n, t_qs])
        t_y1 = P.op("vector", lambda e: e.tensor_copy(out=qaT_s[:], in_=pq[0:96, :].rearrange("p (h t) -> p h t", h=16)),
                    deps=[t_tq, prev])
        t_y2 = P.op("vector", lambda e: e.tensor_copy(out=qnT_s[:, tsl], in_=pqn[0:16, 0:128]), deps=[t_tn])
        t_st = P.dma("sync", ds_o, qaT_d[:, :, tsl].rearrange("h d t -> d h t"), qaT_s[:], deps=[t_y1])
        return [t_st, t_y2, t_y1, t_k3, t_n, t_k1, t_tq, t_tn, t_sq]


    fF = {}
    prevF, prevB = [t_tab], [t_tab]
    for i in range(NT + 1):
        if i < NT:
            fF[i] = tileF(i, prevF, prevB)
            prevF = fF[i]
        if i >= 1:
            prevB = tileB(i - 1, fF[i - 1], prevB)
    prev = prevB
    last = [P.dma("sync", ds_o, latT_d[0:128, :], latT_s[:], deps=[prev]),
            P.dma("sync", ds_o, latT_d[128:160, :], krT_s[:], deps=[prev]),
            P.dma("sync", ds_o, qnT_d[:, :], qnT_s[:], deps=[prev]),
            P.dma("sync", ds_o, knmax_d[:, :], knmax[:], deps=[prev])]
    P.release(m0)
    return last[-1:]


def mlaattn_phase(P, C, qaT_d, qnT_d, LatT_g, KN_g, w_ukv, ones_d, oT_d, tag, heads=range(16)):
    m0 = P.mark()
    ds = P.dsem("la" + tag)
    ds_q = [P.dsem(f"lq{tag}{i}") for i in range(2)]
    ds_o = [P.dsem(f"lo{tag}{i}") for i in range(2)]
    ckvnT = P.sb([128, 4, TOK], BF16)
    wkv = P.sb([128, 2048], BF16)
    kn = P.sb([128, 4, 16], F32)
    qn = P.sb([16, TOK], F32)
    mrow = P.sb([16, TOK], BF16)
    kmax = P.sb([16, 2], F32)
    ka_bufs = [P.sb([97, 4, TOK], BF16) for _ in range(2)]
    t0 = [P.dma("sync", ds, ckvnT[:], LatT_g[:, 0:128, :].rearrange("r p t -> p r t")),
          P.dma("gpsimd", ds, wkv[:], w_ukv[:, :]),
          P.dma("sync", ds, kn[:], KN_g.rearrange("r p h -> p r h")),
          P.dma("sync", ds, qn[:], qnT_d[:, :])]
    for kb_ in ka_bufs:
        t0.append(P.dma("sync", ds, kb_[64:96, :, :], LatT_g[:, 128:160, :].rearrange("r p t -> p r t")))
        t0.append(P.dma("sync", ds, kb_[96:97, :, :], ones_d.rearrange("(o r t) -> o r t", o=1, r=4)))
    ka_ring = Ring(ka_bufs)
    v_bufs = [P.sb([128, 4, 16, 128], BF16) for _ in range(2)]
    for vb_ in v_bufs:
        t0.append(P.op("vector", lambda e, vb_=vb_: e.memset(vb_[:, :, :, 64:128], 1.0)))
    v_ring = Ring(v_bufs)
    q_ring = Ring([P.sb([97, TOK], BF16) for _ in range(2)])
    Pm_ring = Ring([P.sb([128, 2, 512], BF16) for _ in range(3)])
    rl_ring = Ring([P.sb([128, 512], F32) for _ in range(2)])
    rls_ring = Ring([P.sb([64, 512], F32) for _ in range(2)])
    ds_r = [P.dsem(f"lr{tag}{i}") for i in range(2)]
    ostg_ring = Ring([P.sb([64, TOK], BF16) for _ in range(2)])
    Sc_ring = Ring([P.ps([128, 1024], F32).rearrange("p (b n) -> p b n", b=2) for _ in range(2)])
    O_ring = Ring([P.ps([128, 512], F32) for _ in range(2)])
    G_ring = Ring([P.ps([128, 512], F32) for _ in range(2)])

    pk, _ = G_ring.get()
    t_k = None
    for r in range(4):
        t_k = P.op("tensor", lambda e, r=r: e.transpose(out=pk[0:16, r * 128:(r + 1) * 128], in_=kn[:, r, :],
                                                        identity=C.identf[:]), deps=t0 + C.ready, sig=(r == 3))
    t_k = P.op("vector", lambda e: e.tensor_reduce(out=kmax[:, 0:1], in_=pk[0:16, :], axis=AX.X, op=ALU.max), deps=[t_k])
    G_ring.used_by(t_k)
    t_k = P.op("gpsimd", lambda e: e.tensor_tensor(out=kmax[:, 1:2], in0=kmax[:, 0:1], in1=C.phalf[0:16, :], op=ALU.pow),
               deps=[t_k] + C.ready)
    t_m = P.op("vector", lambda e: e.tensor_scalar(out=mrow[:], in0=qn[:], scalar1=kmax[:, 1:2], scalar2=-1.0,
                                                   op0=ALU.mult, op1=ALU.mult), deps=[t_k] + t0)

    heads = list(heads)

    def gen(h):
        hd = Job()
        hd.h = h
        qa, dq = q_ring.get()
        hd.q_k = q_ring.cur
        dsq = ds_q[heads.index(h) % 2]
        hd.loads = [P.dma("sync", dsq, qa[0:96, :], qaT_d[h], deps=list(dq)),
                    P.dma("sync", dsq, qa[96:97, :], mrow[h:h + 1, :], deps=list(dq) + [t_m])]
        hd.qa = qa
        ka, dk = ka_ring.get()
        hd.ka_k = ka_ring.cur
        hd.ka = ka
        tg_ = []
        for r in range(4):
            for tg in range(4):
                pg, dg = G_ring.get()
                tm = P.op("tensor", lambda e, pg=pg, r=r, tg=tg: e.matmul(
                    pg[:, :], lhsT=wkv[:, h * 128:h * 128 + 128], rhs=ckvnT[:, r, tg * 512:(tg + 1) * 512],
                    start=True, stop=True), deps=t0 + list(dg))
                tcp = P.op("vector", lambda e, pg=pg, r=r, tg=tg: e.tensor_copy(
                    out=ka[0:64, r, tg * 512:(tg + 1) * 512], in_=pg[0:64, :]), deps=[tm] + list(dk))
                G_ring.used_by(tcp)
                tg_.append(tcp)
        v, dv = v_ring.get()
        hd.v_k = v_ring.cur
        hd.v = v
        for r in range(4):
            for mg in range(2):
                pg, dg = G_ring.get()
                tm = None
                for i in range(8):
                    blk = mg * 8 + i
                    tm = P.op("tensor", lambda e, pg=pg, r=r, i=i, blk=blk: e.matmul(
                        pg[:, i * 64:(i + 1) * 64], lhsT=ckvnT[:, r, blk * 128:(blk + 1) * 128],
                        rhs=wkv[:, h * 128 + 64:h * 128 + 128], start=True, stop=True),
                        deps=t0 + list(dg), sig=(i == 7))
                tcp = P.op("vector", lambda e, pg=pg, r=r, mg=mg: e.tensor_copy(
                    out=v[:, r, mg * 8:(mg + 1) * 8, 0:64], in_=pg[:].rearrange("p (i c) -> p i c", c=64)),
                    deps=[tm] + list(dv))
                G_ring.used_by(tcp)
                tg_.append(tcp)
        hd.ready = hd.loads + tg_[-1:] + t0
        return hd

    fin = []
    state = {}

    def st1(jb):
        c0, hd = jb.c0, jb.hd
        Sb, dsb = Sc_ring.get()
        qs = hd.qa[:, jb.L * 512 + c0:(jb.L + 1) * 512]
        tz = None
        for b_ in range(2):
            kb = jb.kb[b_]
            ks = hd.ka[:, kb % 4, (kb // 4) * 128:(kb // 4) * 128 + 128]
            dl = jb.dl[b_]
            tz = P.op("tensor", lambda e, b_=b_, ks=ks: e.matmul(Sb[:, b_, c0:512], lhsT=ks, rhs=qs, start=True, stop=(dl is None)),
                      deps=(hd.ready + list(dsb)) if b_ == 0 else [], sig=(b_ == 1 and dl is None))
            if dl is not None:
                tz = P.op("tensor", lambda e, b_=b_, dl=dl: e.matmul(Sb[:, b_, c0:c0 + 128], lhsT=C.ident[:], rhs=C.mask_mla[:, dl, :],
                                                                  start=False, stop=True, skip_group_check=True), sig=(b_ == 1))
        Pb, dp = Pm_ring.get()
        jb.Pm, jb.Pm_k = Pb, Pm_ring.cur
        jb.tP = P.op("scalar", lambda e: e.activation(out=Pb[:, :, c0:512], in_=Sb[:, :, c0:512], func=AF.Exp),
                     deps=[tz] + list(dp))
        Sc_ring.used_by(jb.tP)

    def st2(jb):
        c0, hd, st = jb.c0, jb.hd, jb.stream
        if jb.first:
            st.O, dO = O_ring.get()
            st.O_k = O_ring.cur
            P.op("tensor", lambda e: e.matmul(st.O[:, 0:512], lhsT=C.zeroI[:], rhs=C.zeroK[:], start=True, stop=False),
                 deps=list(dO), sig=False)
        Pb = jb.Pm
        tav = None
        for b_ in range(2):
            kb = jb.kb[b_]
            vb = hd.v[:, kb % 4, kb // 4, :]
            tav = P.op("tensor", lambda e, b_=b_, vb=vb: e.matmul(st.O[:, c0:512], lhsT=vb, rhs=Pb[:, b_, c0:512], start=False,
                                                                  stop=(jb.last and b_ == 1), skip_group_check=True),
                       deps=[jb.tP], sig=(b_ == 1))
        Pm_ring.used_by(tav, k=jb.Pm_k)
        if jb.last:
            L = jb.L
            if L == 0:
                state["ostg"], state["ostg_d"] = ostg_ring.get()
                state["ostg_k"] = ostg_ring.cur
            og = state["ostg"]
            rl, drl = rl_ring.get()
            t_r = P.op("vector", lambda e: e.reciprocal(out=rl[64:128, :], in_=st.O[64:128, 0:512]), deps=[tav] + list(drl))
            rls, drs = rls_ring.get()
            t_sh = P.dma("sync", ds_r[rls_ring.cur], rls[:, :], rl[64:128, :], deps=[t_r] + list(drs))
            rl_ring.used_by(t_sh)
            t_o = P.op("vector", lambda e: e.tensor_tensor(out=og[:, L * 512:(L + 1) * 512], in0=st.O[0:64, 0:512], in1=rls[:, :],
                                                           op=ALU.mult), deps=[t_sh] + list(state["ostg_d"]))
            rls_ring.used_by(t_o)
            O_ring.used_by(t_o, k=st.O_k)
            if L == 3:
                t_st = P.dma("sync", ds_o[state["ostg_k"]], oT_d[hd.h * 64:(hd.h + 1) * 64, :], og[:], deps=[t_o])
                ostg_ring.used_by(t_st, k=state["ostg_k"])
                fin[:] = [t_st]
                ka_ring.used_by(tav, k=hd.ka_k)
                v_ring.used_by(tav, k=hd.v_k)
                q_ring.used_by(tav, k=hd.q_k)

    def head_jobs(hd):
        out = []
        for L in range(4):
            sched = key_schedule(L)
            st = Job()
            npair = len(sched) // 2
            for i in range(npair):
                (kbA, c0, dlA), (kbB, c0b, dlB) = sched[2 * i], sched[2 * i + 1]
                jb = Job()
                jb.stream, jb.hd, jb.L, jb.c0 = st, hd, L, c0
                jb.kb, jb.dl = (kbA, kbB), (dlA, dlB)
                jb.first, jb.last = (i == 0), (i == npair - 1)
                out.append(jb)
        return out

    hds = {}
    hds[heads[0]] = gen(heads[0])
    prev_job = None
    for hi, h in enumerate(heads):
        for ji, jb in enumerate(head_jobs(hds[h])):
            if ji == 2 and hi + 1 < len(heads):
                hds[heads[hi + 1]] = gen(heads[hi + 1])
            st1(jb)
            if prev_job is not None:
                st2(prev_job)
            prev_job = jb
    st2(prev_job)
    P.release(m0)
    return fin


CONST_SPECS = [("c_ident", [128, 128], BF16), ("c_identf", [128, 128], F32), ("c_negU", [128, 128], BF16),
               ("c_mask_sb", [4, 128, 128], BF16), ("c_mask_mla", [4, 128, 128], BF16),
               ("c_ones", [8192], BF16), ("c_invf", [16], F32)]
WEIGHT_SPECS = [("norm_g", [2, 6, D], F32), ("ffn_w_in", [2, 2, D, 2 * DFF], F32), ("ffn_w_out", [2, 2, DFF, D], F32),
                ("sb_w_in", [1, D, 3 * D], F32), ("sb_w_out", [1, D, D], F32), ("mla_w_in", [1, D, 416], F32),
                ("mla_q_norm", [1, 256], F32), ("mla_w_uq", [1, 256, 1536], F32), ("mla_kv_norm", [1, 128], F32),
                ("mla_w_ukv", [1, 128, 2048], F32), ("mla_w_out", [1, D, D], F32)]
ACT_SPECS = {"qT": ([D, TOK], BF16), "kT": ([D, TOK], BF16), "v": ([TOK, D], BF16),
             "KT_g": ([4, D, TOK], BF16), "V_g": ([4, TOK, D], BF16),
             "qaT": ([16, 96, TOK], BF16), "qnT": ([16, TOK], F32), "latT": ([160, TOK], BF16), "knmax": ([128, 16], F32),
             "LatT_g": ([4, 160, TOK], BF16), "KN_g": ([4, 128, 16], F32),
             "x": ([TOK, D], F32), "pos": ([TOK], I32), "x1": ([TOK, D], F32), "x4": ([TOK, D], F32), "y": ([TOK, D], F32)}


def build_program(stage, debug=False):
    nc = bass.Bass("TRN2", target_bir_lowering=False)
    dr = {}

    def ext(name, kind):
        shp, dt = ACT_SPECS[name]
        dr[name] = nc.dram_tensor(name, shp, dt, kind=kind).ap()

    CC_NAMES = ("kT", "v", "KT_g", "V_g", "latT", "knmax", "LatT_g", "KN_g")

    def internal(name, shp, dt):
        if debug and name not in CC_NAMES:
            dr[name] = nc.dram_tensor(name, shp, dt, kind="ExternalOutput").ap()
        else:
            dr[name] = nc.dram_tensor(name, shp, dt).ap()

    for n, shp, dt in CONST_SPECS + WEIGHT_SPECS:
        dr[n] = nc.dram_tensor(n, shp, dt, kind="ExternalInput").ap()
    ins = {0: ["x", "pos"], 1: ["x"], 2: ["x1", "qT", "KT_g", "V_g", "pos"], 3: ["x4", "qaT", "qnT", "LatT_g", "KN_g"]}[stage]
    outs = {0: ["y"], 1: ["x1", "qT", "kT", "v"], 2: ["x4", "qaT", "qnT", "latT", "knmax"], 3: ["y"]}[stage]
    for n in ins:
        ext(n, "ExternalInput")
    for n in outs:
        ext(n, "ExternalOutput")
    g = dr["norm_g"]
    with ExitStack() as es:
        P = Plan(nc, es)
        C = load_consts(P, dr)
        if stage == 0:
            GR = [[0, 1, 2, 3], [4, 5, 6, 7]]
            for n in ("xa", "xb", "xc", "xd", "xe"):
                internal(n, [TOK, D], F32)
            for n in ("qT", "kT", "v", "qaT", "qnT", "latT", "knmax"):
                internal(n, *ACT_SPECS[n])
            internal("oT", [D, TOK], BF16)
            internal("oT2", [D, TOK], BF16)
            internal("KT_g", [4 * D, TOK], BF16)
            internal("V_g", [4 * TOK, D], BF16)
            internal("LatT_g", [4 * 160, TOK], BF16)
            internal("KN_g", [4 * 128, 16], F32)
            KT_g = dr["KT_g"].rearrange("(k r p) t -> k r p t", k=4, r=4)
            V_g = dr["V_g"].rearrange("(k r p) t -> k r p t", k=4, r=4)
            LatT_g = dr["LatT_g"].rearrange("(r p) t -> r p t", r=4)
            KN_g = dr["KN_g"].rearrange("(r p) t -> r p t", r=4)
            ffn_phase(P, C, dr["x"], dr["xa"], dr["ffn_w_in"][0, 0], dr["ffn_w_out"][0, 0], g[0, 0], g[0, 1], "a")
            P.barrier()
            tc_k, tc_v = sbproj_phase(P, C, dr["xa"], dr["sb_w_in"][0], g[0, 2], dr["qT"], dr["kT"], dr["v"], "a",
                                      coll=(dr["KT_g"], dr["V_g"], GR))
            P.barrier(exclude_cc=True)
            sbattn_phase(P, C, dr["qT"], KT_g, V_g, dr["oT"], "a", chunked=True, kv_toks=(tc_k, tc_v))
            P.barrier()
            oproj_phase(P, C, dr["oT"], dr["sb_w_out"][0], g[0, 3], dr["xa"], dr["xb"], "a")
            P.barrier()
            ffn_phase(P, C, dr["xb"], dr["xc"], dr["ffn_w_in"][0, 1], dr["ffn_w_out"][0, 1], g[0, 4], g[0, 5], "b")
            P.barrier()
            ffn_phase(P, C, dr["xc"], dr["xd"], dr["ffn_w_in"][1, 0], dr["ffn_w_out"][1, 0], g[1, 0], g[1, 1], "c")
            P.barrier()
            mlaproj_phase(P, C, dr["xd"], dr["pos"], dr["mla_w_in"][0], dr["mla_q_norm"][0], dr["mla_w_uq"][0],
                          dr["mla_kv_norm"][0], dr["mla_w_ukv"][0], g[1, 2], dr["c_invf"],
                          dr["qaT"], dr["qnT"], dr["latT"], dr["knmax"], "a")
            P.barrier()
            t1 = P.collective(dr["latT"], dr["LatT_g"], GR)
            t2 = P.collective(dr["knmax"], dr["KN_g"], GR)
            P.barrier([t1, t2])
            mlaattn_phase(P, C, dr["qaT"], dr["qnT"], LatT_g, KN_g, dr["mla_w_ukv"][0], dr["c_ones"], dr["oT2"], "a")
            P.barrier()
            oproj_phase(P, C, dr["oT2"], dr["mla_w_out"][0], g[1, 3], dr["xd"], dr["xe"], "b")
            P.barrier()
            ffn_phase(P, C, dr["xe"], dr["y"], dr["ffn_w_in"][1, 1], dr["ffn_w_out"][1, 1], g[1, 4], g[1, 5], "d")
        elif stage == 1:
            ffn_phase(P, C, dr["x"], dr["x1"], dr["ffn_w_in"][0, 0], dr["ffn_w_out"][0, 0], g[0, 0], g[0, 1], "a")
            P.barrier()
            sbproj_phase(P, C, dr["x1"], dr["sb_w_in"][0], g[0, 2], dr["qT"], dr["kT"], dr["v"], "a")
        elif stage == 2:
            internal("oT", [D, TOK], BF16)
            internal("x2", [TOK, D], F32)
            internal("x3", [TOK, D], F32)
            sbattn_phase(P, C, dr["qT"], dr["KT_g"], dr["V_g"], dr["oT"], "a")
            P.barrier()
            oproj_phase(P, C, dr["oT"], dr["sb_w_out"][0], g[0, 3], dr["x1"], dr["x2"], "a")
            P.barrier()
            ffn_phase(P, C, dr["x2"], dr["x3"], dr["ffn_w_in"][0, 1], dr["ffn_w_out"][0, 1], g[0, 4], g[0, 5], "b")
            P.barrier()
            ffn_phase(P, C, dr["x3"], dr["x4"], dr["ffn_w_in"][1, 0], dr["ffn_w_out"][1, 0], g[1, 0], g[1, 1], "c")
            P.barrier()
            mlaproj_phase(P, C, dr["x4"], dr["pos"], dr["mla_w_in"][0], dr["mla_q_norm"][0], dr["mla_w_uq"][0],
                          dr["mla_kv_norm"][0], dr["mla_w_ukv"][0], g[1, 2], dr["c_invf"],
                          dr["qaT"], dr["qnT"], dr["latT"], dr["knmax"], "a")
        elif stage == 3:
            internal("oT", [D, TOK], BF16)
            internal("x5", [TOK, D], F32)
            mlaattn_phase(P, C, dr["qaT"], dr["qnT"], dr["LatT_g"], dr["KN_g"], dr["mla_w_ukv"][0], dr["c_ones"],
                          dr["oT"], "a")
            P.barrier()
            oproj_phase(P, C, dr["oT"], dr["mla_w_out"][0], g[1, 3], dr["x4"], dr["x5"], "b")
            P.barrier()
            ffn_phase(P, C, dr["x5"], dr["y"], dr["ffn_w_in"][1, 1], dr["ffn_w_out"][1, 1], g[1, 4], g[1, 5], "d")
        if debug and stage == 0:
            P.barrier()
            dsd = P.dsem("dbg")
            for n in CC_NAMES:
                src = dr[n]
                o = nc.dram_tensor("dbg_" + n, list(src.shape), src.dtype, kind="ExternalOutput").ap()
                R = src.shape[0]
                step = max(1, R // 8)
                for r0 in range(0, R, step):
                    P.dma("sync", dsd, o[r0:r0 + step, :], src[r0:r0 + step, :])
        P.emit([(d["sem"], d["val"]) for d in P.dsems if d["val"] > 0])
    return nc


def host_consts(cp):
    s = np.arange(128)[:, None]
    t = np.arange(128)[None, :]
    negU = np.where(s >= t, -1.0, 0.0)
    msb = np.zeros((4, 128, 128), np.float32)
    mml = np.zeros((4, 128, 128), np.float32)
    for d in range(4):
        if d == cp:
            msb[d] = np.where(s >= t, NEG, 0.0)
            mml[d] = np.where(s > t, NEG, 0.0)
        elif d > cp:
            msb[d] = NEG
            mml[d] = NEG
    bf = ml_dtypes.bfloat16
    invf = (10000.0 ** (-np.arange(0, 32, 2, dtype=np.float32) / np.float32(32))).astype(np.float32)
    return {"c_ident": np.eye(128, dtype=bf), "c_identf": np.eye(128, dtype=np.float32), "c_negU": negU.astype(bf),
            "c_mask_sb": msb.astype(bf), "c_mask_mla": mml.astype(bf), "c_ones": np.ones(8192, dtype=bf),
            "c_invf": invf}


_PROGS = {}


def _prog(stage):
    if stage not in _PROGS:
        _PROGS[stage] = build_program(stage)
    return _PROGS[stage]


def kernel(x, positions, norm_g, ffn_w_in, ffn_w_out, sb_w_in, sb_w_out, mla_w_in, mla_q_norm, mla_w_uq,
           mla_kv_norm, mla_w_ukv, mla_w_out):
    ncore = 8
    W = {"norm_g": norm_g, "ffn_w_in": ffn_w_in, "ffn_w_out": ffn_w_out, "sb_w_in": sb_w_in, "sb_w_out": sb_w_out,
         "mla_w_in": mla_w_in, "mla_q_norm": mla_q_norm, "mla_w_uq": mla_w_uq, "mla_kv_norm": mla_kv_norm,
         "mla_w_ukv": mla_w_ukv, "mla_w_out": mla_w_out}
    W = {k: np.ascontiguousarray(np.asarray(v, dtype=np.float32)) for k, v in W.items()}
    x = np.asarray(x, dtype=np.float32)
    positions = np.asarray(positions, dtype=np.int32)
    base = []
    for c in range(ncore):
        b, cp = c // 4, c % 4
        m = dict(W)
        m.update(host_consts(cp))
        base.append(m)
    xs = x.reshape(2, 16, 4, 128, D)
    ps = positions.reshape(2, 16, 4, 128)
    ids = list(range(ncore))

    maps = [dict(base[c], x=np.ascontiguousarray(xs[c // 4, :, c % 4]).reshape(TOK, D),
                 pos=np.ascontiguousarray(ps[c // 4, :, c % 4]).reshape(TOK)) for c in range(ncore)]
    r3 = run_bass_kernel_spmd(_prog(0), maps, core_ids=ids).results
    out = np.empty((2, 16, 4, 128, D), np.float32)
    for c in range(ncore):
        out[c // 4, :, c % 4] = r3[c]["y"].reshape(16, 128, D)
    return out.reshape(2, 8192, D)
```
